# Optimizing a Trainium2 kernel written in Bass

```python
import jax, jax.numpy as jnp
from jax import lax
import numpy as np

D_MODEL = 1024
BATCH = 16
SEQ = 2048
DEPTH = 1

N_MOD = 6
NORM_EPS = 1e-6
RWKV_HEAD_DIM = 64
RWKV_HEADS = D_MODEL // RWKV_HEAD_DIM
RWKV_WIDTH = RWKV_HEADS * RWKV_HEAD_DIM
DECAY_LORA = 64
A_LORA = 64
GATE_LORA = 128
RWKV_GN_EPS = 64e-5
RWKV_COLS = 3 * RWKV_WIDTH + DECAY_LORA + A_LORA + GATE_LORA
ATTN_GROUPS = ((128, 1), (512, 4), (2048, 16))
N_GROUPS = len(ATTN_GROUPS)
HEADS_PER_GROUP = 4
ATTN_HEAD_DIM = 64
ATTN_WIDTH = N_GROUPS * HEADS_PER_GROUP * ATTN_HEAD_DIM
ATTN_OUT_WIDTH = HEADS_PER_GROUP * ATTN_HEAD_DIM
ROPE_DIM = ATTN_HEAD_DIM // 4
ROPE_THETA = 500000.0
IN_SPLITS = (RWKV_COLS, ATTN_WIDTH, ATTN_WIDTH, ATTN_WIDTH, D_MODEL, D_MODEL)
IN_COLS = sum(IN_SPLITS)
PEER_HEADS = 8
PEER_N_KEYS = 128
PEER_N_EXPERTS = PEER_N_KEYS * PEER_N_KEYS
PEER_QUERY_DIM = 256
PEER_HALF = PEER_QUERY_DIM // 2
PEER_TOPK = 16
PEER_CHUNK = 128

kernel_name = 'hybrid_rwkv7_dilated_attn_peer_block'


def _split(z, sizes):
    offs = np.cumsum(sizes)[:-1].tolist()
    return jnp.split(z, offs, axis=-1)


def rmsnorm(x, g):
    xf = x.astype(jnp.float32)
    y = xf * lax.rsqrt(jnp.mean(xf * xf, axis=-1, keepdims=True) + NORM_EPS)
    return (y * g.astype(jnp.float32)).astype(x.dtype)


def modulate(n, shift, scale):
    return n * (1 + scale[:, None, :]) + shift[:, None, :]


def partial_rope(x, pos):
    half = ROPE_DIM // 2
    inv = ROPE_THETA ** (-jnp.arange(half, dtype=jnp.float32) / half)
    ang = pos.astype(jnp.float32)[:, None] * inv[None, :]
    bshape = (pos.shape[0],) + (1,) * (x.ndim - 3) + (half,)
    cos, sin = jnp.cos(ang).reshape(bshape), jnp.sin(ang).reshape(bshape)
    xf = x.astype(jnp.float32)
    x1, x2, xp = xf[..., :half], xf[..., half:ROPE_DIM], xf[..., ROPE_DIM:]
    out = jnp.concatenate([x1 * cos - x2 * sin, x2 * cos + x1 * sin, xp], axis=-1)
    return out.astype(x.dtype)


def wkv7_scan(r, w, k, v, a, b):
    B, S, H, N = r.shape
    xs = tuple(t.astype(jnp.float32).transpose(1, 0, 2, 3) for t in (r, w, k, v, a, b))

    def step(state, inp):
        r_t, w_t, k_t, v_t, a_t, b_t = inp
        sa = jnp.einsum('bhij,bhj->bhi', state, a_t)
        state = (state * w_t[:, :, None, :] + sa[..., None] * b_t[:, :, None, :]
                 + v_t[..., None] * k_t[:, :, None, :])
        return state, jnp.einsum('bhij,bhj->bhi', state, r_t)

    init = jnp.zeros((B, H, N, N), jnp.float32)
    _, ys = lax.scan(step, init, xs)
    return ys.transpose(1, 0, 2, 3)


def rwkv7_time_mix(z, rwkv_mu, w0, w2, a0, a2, g2, k_k, k_a, r_k, lnx_g, lnx_b):
    B, S, _ = z.shape
    H, N = RWKV_HEADS, RWKV_HEAD_DIM
    z_prev = jnp.pad(z, ((0, 0), (1, 0), (0, 0)))[:, :-1]
    z = z + (z_prev - z) * rwkv_mu
    r, k, v, wl, al, gl = _split(z, (RWKV_WIDTH, RWKV_WIDTH, RWKV_WIDTH, DECAY_LORA, A_LORA, GATE_LORA))
    w = -jax.nn.softplus(-(w0 + jnp.tanh(wl) @ w2)) - 0.5
    decay = jnp.exp(-jnp.exp(w.astype(jnp.float32)))
    a = jax.nn.sigmoid(a0 + al @ a2)
    g = jax.nn.sigmoid(gl) @ g2
    hv = lambda t: t.reshape(B, S, H, N)
    kk = hv(k * k_k).astype(jnp.float32)
    kk = kk / jnp.maximum(jnp.linalg.norm(kk, axis=-1, keepdims=True), 1e-12)
    k = k * (1 + (a - 1) * k_a)
    r_h, k_h, v_h, a_h = hv(r), hv(k), hv(v), hv(a)
    y = wkv7_scan(r_h, hv(decay), k_h, v_h, -kk, kk * a_h)
    mu = jnp.mean(y, axis=-1, keepdims=True)
    var = jnp.mean(jnp.square(y - mu), axis=-1, keepdims=True)
    y = (y - mu) * lax.rsqrt(var + RWKV_GN_EPS) * lnx_g.reshape(H, N) + lnx_b.reshape(H, N)
    y = y + jnp.sum(r_h * k_h * r_k, axis=-1, keepdims=True) * v_h
    return y.reshape(B, S, RWKV_WIDTH).astype(z.dtype) * g


def dilated_group_attention(q, k, v, window, dilation):
    B, S, H, hd = q.shape
    L = S // dilation
    blk = window // dilation
    nb = -(-L // blk)
    Lp = nb * blk

    def to_sub(t):
        t = t.reshape(B, L, dilation, H, hd).transpose(0, 2, 3, 1, 4)
        t = jnp.pad(t, ((0, 0), (0, 0), (0, 0), (0, Lp - L), (0, 0)))
        return t.reshape(B, dilation, H, nb, blk, hd).astype(jnp.float32)

    qb, kb, vb = to_sub(q), to_sub(k), to_sub(v)

    def band(t):
        prev = jnp.pad(t, ((0, 0), (0, 0), (0, 0), (1, 0), (0, 0), (0, 0)))[:, :, :, :-1]
        return jnp.concatenate([prev, t], axis=-2)

    kk, vv = band(kb), band(vb)
    s = jnp.einsum('bdhnqe,bdhnke->bdhnqk', qb, kk) * (hd ** -0.5)
    qi = jnp.arange(blk)[:, None]
    kj = jnp.arange(2 * blk)[None, :]
    dist = blk + qi - kj
    band_mask = (dist >= 0) & (dist <= blk)
    mask = band_mask[None] & ((jnp.arange(nb)[:, None, None] > 0) | (kj[None] >= blk))
    s = jnp.where(mask, s, -jnp.inf)
    lse = jax.nn.logsumexp(s, axis=-1)
    p = jnp.exp(s - lse[..., None])
    o = jnp.einsum('bdhnqk,bdhnke->bdhnqe', p, vv)
    o = o.reshape(B, dilation, H, Lp, hd)[:, :, :, :L].transpose(0, 3, 1, 2, 4).reshape(B, S, H, hd)
    lse = lse.reshape(B, dilation, H, Lp)[..., :L].transpose(0, 3, 1, 2).reshape(B, S, H)
    return o, lse


def dilated_attention(zq, zk, zv, q_norm_g, k_norm_g, pos):
    B, S, _ = zq.shape
    shp = (B, S, N_GROUPS, HEADS_PER_GROUP, ATTN_HEAD_DIM)
    q = partial_rope(rmsnorm(zq.reshape(shp), q_norm_g[:, None, :]), pos)
    k = partial_rope(rmsnorm(zk.reshape(shp), k_norm_g[:, None, :]), pos)
    v = zv.reshape(shp)
    outs, lses = [], []
    for gi, (win, dil) in enumerate(ATTN_GROUPS):
        o, l = dilated_group_attention(q[:, :, gi], k[:, :, gi], v[:, :, gi], win, dil)
        outs.append(o)
        lses.append(l)
    wts = jax.nn.softmax(jnp.stack(lses), axis=0)
    y = jnp.sum(wts[..., None] * jnp.stack(outs), axis=0)
    return y.reshape(B, S, ATTN_OUT_WIDTH).astype(zq.dtype)


def peer(u, peer_wq, peer_k1, peer_k2, peer_u, peer_v):
    B, S, D = u.shape
    q = (u @ peer_wq).reshape(B, S, PEER_HEADS, PEER_QUERY_DIM).astype(jnp.float32)
    s1 = jnp.einsum('bshe,ne->bshn', q[..., :PEER_HALF], peer_k1.astype(jnp.float32))
    s2 = jnp.einsum('bshe,ne->bshn', q[..., PEER_HALF:], peer_k2.astype(jnp.float32))
    v1, i1 = lax.top_k(s1, PEER_TOPK)
    v2, i2 = lax.top_k(s2, PEER_TOPK)
    cshape = (B, S, PEER_HEADS, PEER_TOPK * PEER_TOPK)
    cand = (v1[..., :, None] + v2[..., None, :]).reshape(cshape)
    cidx = (i1[..., :, None] * PEER_N_KEYS + i2[..., None, :]).reshape(cshape)
    top, sel = lax.top_k(cand, PEER_TOPK)
    idx = jnp.take_along_axis(cidx, sel, axis=-1)
    gate = jax.nn.softmax(top, axis=-1)
    n_chunks = (B * S) // PEER_CHUNK
    hk = PEER_HEADS * PEER_TOPK
    u_c = u.reshape(n_chunks, PEER_CHUNK, D)
    i_c = idx.reshape(n_chunks, PEER_CHUNK, hk)
    g_c = gate.reshape(n_chunks, PEER_CHUNK, hk)

    def chunk(args):
        uc, ic, gc = args
        ue = jnp.take(peer_u, ic, axis=0)
        act = jax.nn.gelu(jnp.einsum('cd,ced->ce', uc, ue).astype(jnp.float32), approximate=False)
        ve = jnp.take(peer_v, ic, axis=0)
        return jnp.einsum('ce,ced->cd', (gc * act).astype(ve.dtype), ve)

    out = lax.map(chunk, (u_c, i_c, g_c))
    return out.reshape(B, S, D).astype(u.dtype)


def _layer(x, c, pos, w_ada, b_ada, norm1_g, w_in, rwkv_mu, w0, w2, a0, a2, g2, k_k, k_a, r_k,
           lnx_g, lnx_b, q_norm_g, k_norm_g, w_br_rwkv, w_br_attn, w_out, norm2_g,
           peer_wq, peer_k1, peer_k2, peer_u, peer_v):
    mod = jax.nn.silu(c) @ w_ada + b_ada
    sh1, sc1, gt1, sh2, sc2, gt2 = jnp.split(mod, N_MOD, axis=-1)
    n1 = modulate(rmsnorm(x, norm1_g), sh1, sc1)
    z = n1 @ w_in
    z_rwkv, zq, zk, zv, z_gr, z_ga = _split(z, IN_SPLITS)
    y_r = rwkv7_time_mix(z_rwkv, rwkv_mu, w0, w2, a0, a2, g2, k_k, k_a, r_k, lnx_g, lnx_b)
    y_a = dilated_attention(zq, zk, zv, q_norm_g, k_norm_g, pos)
    merged = jax.nn.sigmoid(z_gr) * (y_r @ w_br_rwkv) + jax.nn.sigmoid(z_ga) * (y_a @ w_br_attn)
    h = x + gt1[:, None, :] * (merged @ w_out)
    n2 = modulate(rmsnorm(h, norm2_g), sh2, sc2)
    return h + gt2[:, None, :] * peer(n2, peer_wq, peer_k1, peer_k2, peer_u, peer_v)


def setup_inputs(seed: int = 0) -> dict:
    key = jax.random.key(seed)
    ks = iter(jax.random.split(key, 32))
    f32 = jnp.float32
    D, L = D_MODEL, DEPTH

    def nrm(shape, scale):
        return jax.random.normal(next(ks), shape, f32) * scale

    return {
        'x': nrm((BATCH, SEQ, D), 1.0),
        'c': nrm((BATCH, D), 1.0),
        'w_ada': nrm((L, D, N_MOD * D), 0.2 * D ** -0.5),
        'b_ada': nrm((L, N_MOD * D), 0.02),
        'norm1_g': 1.0 + nrm((L, D), 0.05),
        'w_in': nrm((L, D, IN_COLS), D ** -0.5),
        'rwkv_mu': jax.random.uniform(next(ks), (L, RWKV_COLS), f32, 0.05, 0.95),
        'w0': nrm((L, RWKV_WIDTH), 0.5) - 1.0,
        'w2': nrm((L, DECAY_LORA, RWKV_WIDTH), 0.5 * DECAY_LORA ** -0.5),
        'a0': nrm((L, RWKV_WIDTH), 0.3),
        'a2': nrm((L, A_LORA, RWKV_WIDTH), A_LORA ** -0.5),
        'g2': nrm((L, GATE_LORA, RWKV_WIDTH), GATE_LORA ** -0.5),
        'k_k': 0.85 + nrm((L, RWKV_WIDTH), 0.05),
        'k_a': 1.0 + nrm((L, RWKV_WIDTH), 0.05),
        'r_k': nrm((L, RWKV_HEADS, RWKV_HEAD_DIM), 0.1),
        'lnx_g': 1.0 + nrm((L, RWKV_WIDTH), 0.05),
        'lnx_b': nrm((L, RWKV_WIDTH), 0.02),
        'q_norm_g': 1.0 + nrm((L, N_GROUPS, ATTN_HEAD_DIM), 0.05),
        'k_norm_g': 1.0 + nrm((L, N_GROUPS, ATTN_HEAD_DIM), 0.05),
        'w_br_rwkv': nrm((L, RWKV_WIDTH, D), RWKV_WIDTH ** -0.5),
        'w_br_attn': nrm((L, ATTN_OUT_WIDTH, D), ATTN_OUT_WIDTH ** -0.5),
        'w_out': nrm((L, D, D), D ** -0.5),
        'norm2_g': 1.0 + nrm((L, D), 0.05),
        'peer_wq': nrm((L, D, PEER_HEADS * PEER_QUERY_DIM), D ** -0.5),
        'peer_k1': nrm((L, PEER_N_KEYS, PEER_HALF), PEER_HALF ** -0.5),
        'peer_k2': nrm((L, PEER_N_KEYS, PEER_HALF), PEER_HALF ** -0.5),
        'peer_u': nrm((L, PEER_N_EXPERTS, D), D ** -0.5),
        'peer_v': nrm((L, PEER_N_EXPERTS, D), PEER_HEADS ** -0.5),
    }


def reference(x, c, w_ada, b_ada, norm1_g, w_in, rwkv_mu, w0, w2, a0, a2, g2, k_k, k_a, r_k,
              lnx_g, lnx_b, q_norm_g, k_norm_g, w_br_rwkv, w_br_attn, w_out, norm2_g,
              peer_wq, peer_k1, peer_k2, peer_u, peer_v):
    pos = jnp.arange(x.shape[1], dtype=jnp.int32)
    h = x
    for l in range(DEPTH):
        h = _layer(h, c, pos, w_ada[l], b_ada[l], norm1_g[l], w_in[l], rwkv_mu[l], w0[l], w2[l],
                   a0[l], a2[l], g2[l], k_k[l], k_a[l], r_k[l], lnx_g[l], lnx_b[l],
                   q_norm_g[l], k_norm_g[l], w_br_rwkv[l], w_br_attn[l], w_out[l], norm2_g[l],
                   peer_wq[l], peer_k1[l], peer_k2[l], peer_u[l], peer_v[l])
    return h.astype(x.dtype)
```

```python
import contextlib
import numpy as np
import ml_dtypes
import concourse.bass as bass
import concourse.mybir as mybir
from concourse.bass_utils import run_bass_kernel_spmd

F32 = mybir.dt.float32
BF16 = mybir.dt.bfloat16
U32 = mybir.dt.uint32
I32 = mybir.dt.int32
AF = mybir.ActivationFunctionType
ALU = mybir.AluOpType
AX = mybir.AxisListType

ENGS = ("pe", "act", "dve", "pool", "sp")
NDMA = {"sp": 8, "pool": 16, "act": 4}

D = 1024
SEQ = 2048
NB = 2
NT = NB * SEQ
NTILE = NT // 128
ZC = 7680
EPS = 1e-6
GN_EPS = 64e-5


class Sched:
    def __init__(self, nc, sems):
        self.nc = nc
        self.sems = sems
        self.ops = {e: [] for e in ENGS}
        self.cnt = {e: 0 for e in ENGS}
        self.dma_slot_cnt = {e: [0] * n for e, n in NDMA.items()}
        self.dma_rr = {e: 0 for e in NDMA}
        self.state = {}
        self.waited = {e: {} for e in ENGS}

    def _deps(self, eng, reads, writes):
        need = {}

        def add(tok):
            if tok is None:
                return
            s, v, e = tok
            if e == "pe" and eng == "pe" and s == "c_pe":
                return
            if need.get(s, 0) < v:
                need[s] = v
        for k in reads:
            st = self.state.get(k)
            if st:
                add(st[0])
        for k in writes:
            st = self.state.get(k)
            if st:
                add(st[0])
                for t in st[1].values():
                    add(t)
        out = []
        w = self.waited[eng]
        for s, v in need.items():
            if w.get(s, 0) < v:
                w[s] = v
                out.append((s, v))
        return out

    def _commit(self, tok, reads, writes):
        for k in reads:
            st = self.state.setdefault(k, [None, {}])
            st[1][tok[0]] = tok
        for k in writes:
            self.state[k] = [tok, {}]

    def op(self, eng, fn, reads=(), writes=()):
        waits = self._deps(eng, reads, writes)
        self.cnt[eng] += 1
        tok = ("c_" + eng, self.cnt[eng], eng)
        self.ops[eng].append(("op", fn, waits, tok))
        self._commit(tok, reads, writes)
        return tok

    def dma(self, eng, fn, reads=(), writes=()):
        slot = self.dma_rr[eng]
        self.dma_rr[eng] = (slot + 1) % NDMA[eng]
        sname = "d_%s_%d" % (eng, slot)
        waits = self._deps(eng, reads, writes)
        prev = self.dma_slot_cnt[eng][slot]
        w = self.waited[eng]
        if prev > 0 and w.get(sname, 0) < prev:
            w[sname] = prev
            waits.append((sname, prev))
        self.dma_slot_cnt[eng][slot] = prev + 16
        tok = (sname, prev + 16, eng)
        self.ops[eng].append(("dma", fn, waits, tok))
        self._commit(tok, reads, writes)
        return tok

    def barrier(self):
        toks = []
        for e in ENGS:
            if e != "sp" and self.cnt[e] > 0:
                toks.append(("c_" + e, self.cnt[e]))
        for e, n in NDMA.items():
            for i in range(n):
                if self.dma_slot_cnt[e][i] > 0:
                    toks.append(("d_%s_%d" % (e, i), self.dma_slot_cnt[e][i]))
        for e in ENGS:
            w = self.waited[e]
            ws = []
            for s, v in toks:
                if w.get(s, 0) < v:
                    w[s] = v
                    ws.append((s, v))
            if ws:
                self.ops[e].append(("wait", None, ws, None))
        self.state = {}

    def emit(self):
        nc = self.nc
        sems = self.sems
        ops = self.ops
        self.ops = {e: [] for e in ENGS}
        with nc.Block() as block:
            def run(engname):
                def body(engine):
                    for kind, fn, waits, tok in ops[engname]:
                        for s, v in waits:
                            engine.wait_ge(sems[s], v)
                        if kind == "wait":
                            continue
                        ins = fn(engine)
                        ins.then_inc(sems[tok[0]], 16 if kind == "dma" else 1)
                return body
            block.tensor(run("pe"))
            block.scalar(run("act"))
            block.vector(run("dve"))
            block.gpsimd(run("pool"))
            block.sync(run("sp"))


def sem_names():
    names = ["c_" + e for e in ENGS if e != "sp"]
    for e, n in NDMA.items():
        names += ["d_%s_%d" % (e, i) for i in range(n)]
    return names


class H:
    def __init__(self, S):
        self.S = S

    def dma(self, out, in_, r, w, eng="sp"):
        return self.S.dma(eng, lambda e: e.dma_start(out=out, in_=in_), reads=r, writes=w)

    def tt(self, eng, out, in0, in1, op, r, w):
        return self.S.op(eng, lambda e: e.tensor_tensor(out=out, in0=in0, in1=in1, op=op), reads=r, writes=w)

    def ts(self, eng, out, in0, s1, s2, op0, op1, r, w, accum=None):
        if op1 is None:
            return self.S.op(eng, lambda e: e.tensor_scalar(out=out, in0=in0, scalar1=s1, scalar2=None, op0=op0), reads=r, writes=w)
        if accum is None:
            return self.S.op(eng, lambda e: e.tensor_scalar(out=out, in0=in0, scalar1=s1, scalar2=s2, op0=op0, op1=op1), reads=r, writes=w)
        return self.S.op(eng, lambda e: e.tensor_scalar(out=out, in0=in0, scalar1=s1, scalar2=s2, op0=op0, op1=op1, accum_out=accum), reads=r, writes=w)

    def stt(self, out, in0, scalar, in1, op0, op1, r, w, accum=None):
        if accum is None:
            return self.S.op("dve", lambda e: e.scalar_tensor_tensor(out=out, in0=in0, scalar=scalar, in1=in1, op0=op0, op1=op1), reads=r, writes=w)
        return self.S.op("dve", lambda e: e.scalar_tensor_tensor(out=out, in0=in0, scalar=scalar, in1=in1, op0=op0, op1=op1, accum_out=accum), reads=r, writes=w)

    def copy(self, eng, out, in_, r, w):
        if eng == "act":
            return self.S.op("act", lambda e: e.copy(out=out, in_=in_), reads=r, writes=w)
        return self.S.op(eng, lambda e: e.tensor_copy(out=out, in_=in_), reads=r, writes=w)

    def act(self, out, in_, func, r, w, scale=1.0, bias=None, accum=None):
        def fn(e):
            kw = dict(out=out, in_=in_, func=func, scale=scale)
            if bias is not None:
                kw["bias"] = bias
            if accum is not None:
                kw["accum_out"] = accum
            return e.activation(**kw)
        return self.S.op("act", fn, reads=r, writes=w)

    def red(self, out, in_, op, r, w):
        return self.S.op("dve", lambda e: e.tensor_reduce(out=out, in_=in_, axis=AX.X, op=op), reads=r, writes=w)

    def recip(self, out, in_, r, w):
        return self.S.op("dve", lambda e: e.reciprocal(out=out, in_=in_), reads=r, writes=w)

    def memset(self, eng, ap, val, w):
        return self.S.op(eng, lambda e: e.memset(ap, val), reads=(), writes=w)

    def mm(self, out, lhsT, rhs, start, stop, r, w):
        return self.S.op("pe", lambda e: e.matmul(out, lhsT, rhs, start=start, stop=stop), reads=r, writes=w)

    def tr(self, out, in_, ident, r, w):
        return self.S.op("pe", lambda e: e.transpose(out=out, in_=in_, identity=ident), reads=r, writes=w)


def bc3(ap, n_mid):
    p, f = ap.shape
    return ap.unsqueeze(1).to_broadcast([p, n_mid, f])


def bcl(ap, n_last):
    p, f = ap.shape
    return ap.unsqueeze(2).to_broadcast([p, f, n_last])


def build_core(stop_after=None, dbg=()):
    nc = bass.Bass("TRN2", target_bir_lowering=False)
    din = lambda name, shape, dt=F32: nc.dram_tensor(name, shape, dt, kind="ExternalInput").ap()
    x = din("x", [NT, D])
    cT = din("cT", [128, 8, NB])
    w_ada = din("w_ada", [D, 6 * D]); b_ada = din("b_ada", [1, 6 * D])
    norm1_g = din("norm1_g", [1, D]); w_in = din("w_in", [D, ZC])
    rwkv_mu = din("rwkv_mu", [1, 3328]); w0 = din("w0", [1, D]); w2 = din("w2", [64, D])
    a0 = din("a0", [1, D]); a2 = din("a2", [64, D]); g2 = din("g2", [128, D])
    k_k = din("k_k", [1, D]); k_a = din("k_a", [1, D]); r_k = din("r_k", [1, D])
    lnx_g = din("lnx_g", [1, D]); lnx_b = din("lnx_b", [1, D])
    q_norm_g = din("q_norm_g", [3, 64]); k_norm_g = din("k_norm_g", [3, 64])
    w_br_rwkv = din("w_br_rwkv", [D, D]); w_br_attn = din("w_br_attn", [256, D]); w_out = din("w_out", [D, D])
    norm2_g = din("norm2_g", [1, D]); peer_wq = din("peer_wq", [D, 2048])
    peer_k1 = din("peer_k1", [128, 128]); peer_k2 = din("peer_k2", [128, 128])
    peer_u = din("peer_u", [16384, D]); peer_v = din("peer_v", [16384, D])
    ident_d = din("ident", [128, 128]); cs_d = din("cs", [128, 16, 16])
    mcur_d = din("mcur", [128, 128]); mprev_d = din("mprev", [128, 128])
    ltri_d = din("ltri", [128, 128]); lblk_d = din("lblk", [128, 128]); m256_d = din("m256", [128, 256]); mL_d = din("mL", [128, 128])
    y_out = nc.dram_tensor("y", [NT, D], F32, kind="ExternalOutput").ap()

    dscr = lambda name, shape, dt=F32: nc.dram_tensor(name, shape, dt).ap()
    modd = dscr("modd", [NB, 6 * D])
    zs = dscr("zs", [NT, ZC])
    sc = {k: dscr("sc_" + k, [NT, D]) for k in ("r", "w", "k", "v", "a", "b", "g", "y")}
    yrs = dscr("yrs", [NT, D], BF16)
    hs = dscr("hs", [NT, D])
    n2s = dscr("n2s", [NT, D], BF16)
    uvb = dscr("uvb", [16384, 2 * D], BF16)
    dbg_out = {}
    for name, shape in dbg:
        dbg_out[name] = nc.dram_tensor("dbg_" + name, shape, F32, kind="ExternalOutput").ap()

    with contextlib.ExitStack() as top:
        sems = {n: top.enter_context(nc.semaphore(n)) for n in sem_names()}
        S = Sched(nc, sems)
        h = H(S)
        uid = [0]

        def sbt(es, name, shape, dt=F32):
            uid[0] += 1
            return es.enter_context(nc.sbuf_tensor("s%d_%s" % (uid[0], name), shape, dt))

        def pst(es, name, shape, dt=F32):
            uid[0] += 1
            return es.enter_context(nc.psum_tensor("p%d_%s" % (uid[0], name), shape, dt))

        idf = sbt(top, "idf", [128, 128]); idb = sbt(top, "idb", [128, 128], BF16)
        yaT = sbt(top, "yaT", [64, 4, NT], BF16)
        h.dma(idf[:], ident_d, [], ["idf"])
        h.copy("dve", idb[:], idf[:], ["idf"], ["idb"])
        last_tok = [None]

        def end_phase():
            S.barrier()
            S.emit()

        with contextlib.ExitStack() as es:
            ct = sbt(es, "ct", [128, 8, NB]); sil = sbt(es, "sil", [128, 8, NB])
            wad = [sbt(es, "wad%d" % i, [128, 8, 512]) for i in range(2)]
            bad = sbt(es, "bad", [1, 6 * D]); modrow = sbt(es, "modrow", [1, NB, 6 * D])
            psm = [pst(es, "psm%d" % i, [128, 512]) for i in range(NB)]
            h.dma(ct[:], cT, [], ["ct"])
            h.dma(bad[:], b_ada, [], ["bad"])
            h.act(sil[:], ct[:], AF.Silu, ["ct"], ["sil"])
            for nb in range(12):
                wb = wad[nb % 2]
                h.dma(wb[:], w_ada[:, nb * 512:(nb + 1) * 512].rearrange("(kc p) n -> p kc n", p=128), [], ["wad%d" % (nb % 2)])
                for b in range(NB):
                    for kc in range(8):
                        h.mm(psm[b][0:1, :], sil[:, kc, b:b + 1], wb[:, kc, :], kc == 0, kc == 7,
                             ["sil", "wad%d" % (nb % 2)], ["psm%d" % b])
                    h.tt("dve", modrow[0:1, b, nb * 512:(nb + 1) * 512], psm[b][0:1, :], bad[0:1, nb * 512:(nb + 1) * 512],
                         ALU.add, ["psm%d" % b, "bad"], ["modrow"])
            for b in range(NB):
                for o in (1024, 4096):
                    h.ts("dve", modrow[0:1, b, o:o + 1024], modrow[0:1, b, o:o + 1024], 1.0, None, ALU.add, None, ["modrow"], ["modrow"])
                h.dma(modd[b:b + 1, :], modrow[0:1, b, :], ["modrow"], ["modd"])
            end_phase()
        if stop_after == "0":
            return nc, dbg_out, S

        def bcast_load(dst, src_row, w, r=()):
            return h.dma(dst, src_row.partition_broadcast(128), list(r), w)

        with contextlib.ExitStack() as es:
            n1T = sbt(es, "n1T", [128, 8, NT], BF16)
            with contextlib.ExitStack() as es1:
                G1 = [sbt(es1, "G1_%d" % b, [128, D]) for b in range(NB)]
                SH1 = [sbt(es1, "SH1_%d" % b, [128, D]) for b in range(NB)]
                tmpg = sbt(es1, "tmpg", [128, D])
                xt = [sbt(es1, "xt%d" % i, [128, D]) for i in range(2)]
                junk = sbt(es1, "junk", [128, D]); n1f = sbt(es1, "n1f", [128, D])
                n1b = [sbt(es1, "n1b%d" % i, [128, D], BF16) for i in range(2)]
                ss = sbt(es1, "ss", [128, NTILE]); rs = sbt(es1, "rs", [128, NTILE])
                ptr = [pst(es1, "ptr%d" % i, [128, 8, 128], BF16) for i in range(2)]
                bcast_load(tmpg[:], norm1_g[0:1, :], ["tmpg"])
                for b in range(NB):
                    bcast_load(G1[b][:], modd[b:b + 1, 1024:2048], ["G1_%d" % b], ["modd"])
                    bcast_load(SH1[b][:], modd[b:b + 1, 0:1024], ["SH1_%d" % b], ["modd"])
                    h.tt("dve", G1[b][:], G1[b][:], tmpg[:], ALU.mult, ["G1_%d" % b, "tmpg"], ["G1_%d" % b])
                for i in range(NTILE):
                    b = i // 16; j = i % 2
                    h.dma(xt[j][:], x[i * 128:(i + 1) * 128, :], [], ["xt%d" % j])
                    h.act(junk[:], xt[j][:], AF.Square, ["xt%d" % j], ["junk", "ss%d" % i], accum=ss[:, i:i + 1])
                    h.ts("dve", rs[:, i:i + 1], ss[:, i:i + 1], 1.0 / D, EPS, ALU.mult, ALU.add, ["ss%d" % i], ["rs%d" % i])
                    h.act(rs[:, i:i + 1], rs[:, i:i + 1], AF.Sqrt, ["rs%d" % i], ["rs%d" % i])
                    h.recip(rs[:, i:i + 1], rs[:, i:i + 1], ["rs%d" % i], ["rs%d" % i])
                    h.stt(n1f[:], xt[j][:], rs[:, i:i + 1], G1[b][:], ALU.mult, ALU.mult, ["xt%d" % j, "rs%d" % i, "G1_%d" % b], ["n1f"])
                    h.tt("pool", n1b[j][:], n1f[:], SH1[b][:], ALU.add, ["n1f", "SH1_%d" % b], ["n1b%d" % j])
                    for kc in range(8):
                        h.tr(ptr[j][:, kc, :], n1b[j][:, kc * 128:(kc + 1) * 128], idb[:], ["n1b%d" % j, "idb"], ["ptr%d" % j])
                    h.copy("act", n1T[:, :, i * 128:(i + 1) * 128], ptr[j][:], ["ptr%d" % j], ["n1T%d" % i])
                end_phase()
            with contextlib.ExitStack() as es2:
                wf = [sbt(es2, "wf%d" % i, [128, 8, 512]) for i in range(2)]
                wbf = [sbt(es2, "wbf%d" % i, [128, 8, 512], BF16) for i in range(2)]
                zo = [sbt(es2, "zo%d" % i, [128, 512]) for i in range(4)]
                pz = [pst(es2, "pz%d" % i, [128, 512]) for i in range(4)]
                cnt = 0
                for nb in range(15):
                    j = nb % 2
                    h.dma(wf[j][:], w_in[:, nb * 512:(nb + 1) * 512].rearrange("(kc p) n -> p kc n", p=128), [], ["wf%d" % j])
                    h.copy("pool", wbf[j][:], wf[j][:], ["wf%d" % j], ["wbf%d" % j])
                    for i in range(NTILE):
                        q = cnt % 4; cnt += 1
                        for kc in range(8):
                            h.mm(pz[q][:], n1T[:, kc, i * 128:(i + 1) * 128], wbf[j][:, kc, :], kc == 0, kc == 7,
                                 ["wbf%d" % j], ["pz%d" % q])
                        h.copy("act" if q % 2 == 0 else "dve", zo[q][:], pz[q][:], ["pz%d" % q], ["zo%d" % q])
                        h.dma(zs[i * 128:(i + 1) * 128, nb * 512:(nb + 1) * 512], zo[q][:], ["zo%d" % q], ["zs"])
                end_phase()
        if stop_after == "A":
            return nc, dbg_out, S

        for b in range(NB):
            tb = b * SEQ
            with contextlib.ExitStack() as es:
                qT = sbt(es, "qT", [128, 6, SEQ], BF16); kT = sbt(es, "kT", [128, 6, SEQ], BF16)
                VA = [sbt(es, "VA%d" % g, [128, 16, 4, 65], BF16) for g in range(3)]
                ACC = sbt(es, "ACC", [65, 4, SEQ]); RD = sbt(es, "RD", [65, SEQ])
                onesf = sbt(es, "onesf", [65, 64])
                mcur = sbt(es, "mcur", [128, 128], BF16); mprev = sbt(es, "mprev", [128, 128], BF16)
                mtmp = sbt(es, "mtmp", [128, 128])
                QG = sbt(es, "QG", [128, 12, 64]); KG = sbt(es, "KG", [128, 12, 64])
                cs = sbt(es, "cs", [128, 16, 16])
                h.copy("dve", onesf[:, :], idf[0:65, 64:65].to_broadcast([65, 64]), ["idf"], ["onesf"])
                h.dma(mtmp[:], mcur_d, [], ["mtmp"]); h.copy("dve", mcur[:], mtmp[:], ["mtmp"], ["mcur"])
                h.dma(mtmp[:], mprev_d, ["mtmp"], ["mtmp"]); h.copy("dve", mprev[:], mtmp[:], ["mtmp"], ["mprev"])
                h.dma(cs[:], cs_d, [], ["cs"])
                for gh in range(12):
                    bcast_load(QG[:, gh, :], q_norm_g[gh // 4:gh // 4 + 1, :], ["QG"])
                    bcast_load(KG[:, gh, :], k_norm_g[gh // 4:gh // 4 + 1, :], ["KG"])
                for g in range(3):
                    h.memset("pool", VA[g][:, :, :, 64:65], 1.0, ["VA%d" % g])
                with contextlib.ExitStack() as es1:
                    zq = [sbt(es1, "zq%d" % i, [128, 768]) for i in range(2)]
                    zk = [sbt(es1, "zk%d" % i, [128, 768]) for i in range(2)]
                    sq = sbt(es1, "sq", [128, 768]); st = sbt(es1, "st", [128, 12]); qn = sbt(es1, "qn", [128, 768])
                    r1 = sbt(es1, "r1", [128, 12, 8]); r2 = sbt(es1, "r2", [128, 12, 8])
                    qb = [sbt(es1, "qb%d" % i, [128, 768], BF16) for i in range(2)]
                    ptq = [pst(es1, "ptq%d" % i, [128, 6, 128], BF16) for i in range(2)]
                    vtmp = [sbt(es1, "vtmp%d" % i, [128, 256]) for i in range(2)]
                    cntp = 0
                    for i in range(16):
                        cosb = bc3(cs[:, i, 0:8], 12); sinb = bc3(cs[:, i, 8:16], 12)
                        for which, zt, dstT, GG, c0, scl in (("q", zq, qT, QG, 3328, 0.125), ("k", zk, kT, KG, 3328 + 768, 1.0)):
                            j = cntp % 2; cntp += 1
                            zn = "z%s%d" % (which, i % 2)
                            zz = zt[i % 2]
                            h.dma(zz[:], zs[tb + i * 128:tb + (i + 1) * 128, c0:c0 + 768], ["zs"], [zn])
                            h.tt("pool", sq[:], zz[:], zz[:], ALU.mult, [zn], ["sq"])
                            h.red(st[:], sq[:].rearrange("p (a b) -> p a b", a=12), ALU.add, ["sq"], ["st"])
                            h.ts("dve", st[:], st[:], 1.0 / 64, EPS, ALU.mult, ALU.add, ["st"], ["st"])
                            h.act(st[:], st[:], AF.Sqrt, ["st"], ["st"])
                            h.recip(st[:], st[:], ["st"], ["st"])
                            if scl != 1.0:
                                h.ts("dve", st[:], st[:], scl, None, ALU.mult, None, ["st"], ["st"])
                            z3 = zz[:].rearrange("p (a b) -> p a b", a=12)
                            q3 = qn[:].rearrange("p (a b) -> p a b", a=12)
                            h.tt("dve", q3, z3, bcl(st[:], 64), ALU.mult, [zn, "st"], ["qn"])
                            h.tt("pool", q3, q3, GG[:], ALU.mult, ["qn", "QG", "KG"], ["qn"])
                            qb3 = qb[j][:].rearrange("p (a b) -> p a b", a=12)
                            h.tt("dve", r1[:], q3[:, :, 0:8], cosb, ALU.mult, ["qn", "cs"], ["r1"])
                            h.tt("dve", r2[:], q3[:, :, 8:16], sinb, ALU.mult, ["qn", "cs"], ["r2"])
                            h.tt("dve", qb3[:, :, 0:8], r1[:], r2[:], ALU.subtract, ["r1", "r2"], ["qb%d" % j])
                            h.tt("dve", r1[:], q3[:, :, 8:16], cosb, ALU.mult, ["qn", "cs", "qb%d" % j], ["r1"])
                            h.tt("dve", r2[:], q3[:, :, 0:8], sinb, ALU.mult, ["qn", "cs", "qb%d" % j], ["r2"])
                            h.tt("dve", qb3[:, :, 8:16], r1[:], r2[:], ALU.add, ["r1", "r2"], ["qb%d" % j])
                            h.copy("pool", qb3[:, :, 16:64], q3[:, :, 16:64], ["qn"], ["qb%d" % j])
                            for c6 in range(6):
                                h.tr(ptq[j][:, c6, :], qb[j][:, c6 * 128:(c6 + 1) * 128], idb[:], ["qb%d" % j, "idb"], ["ptq%d" % j])
                            h.copy("act", dstT[:, :, i * 128:(i + 1) * 128], ptq[j][:], ["ptq%d" % j], [which + "T"])
                    cv = 0
                    for g, dil in enumerate((1, 4, 16)):
                        nmt = SEQ // dil // 128
                        for r_ in range(dil):
                            for mt in range(nmt):
                                kt = r_ * nmt + mt
                                j = cv % 2; cv += 1
                                st0 = tb + r_ + dil * 128 * mt
                                src = zs[st0:st0 + dil * 127 + 1:dil, 3328 + 1536 + g * 256:3328 + 1536 + (g + 1) * 256]
                                h.dma(vtmp[j][:], src, ["zs"], ["vtmp%d" % j])
                                h.copy("pool", VA[g][:, kt, :, 0:64], vtmp[j][:].rearrange("p (a b) -> p a b", a=4),
                                       ["vtmp%d" % j], ["VA%d" % g])
                    S.barrier()
                import os
                EP = os.environ.get("EPART", "")
                if EP == "prep":
                    end_phase()
                    continue
                with contextlib.ExitStack() as es1:
                    pss = [pst(es1, "pss%d" % i, [128, 512]) for i in range(2)]
                    pso = [pst(es1, "pso%d" % i, [128, 512]) for i in range(2)]
                    psb = pst(es1, "psb", [128, 512])
                    ee = [sbt(es1, "ee%d" % i, [128, 128], BF16) for i in range(3)]
                    pp = [sbt(es1, "pp%d" % i, [128, 128], BF16) for i in range(3)]
                    ce = 0; co = 0
                    for g, dil in enumerate((1, 4, 16)):
                        nmt = SEQ // dil // 128
                        if EP == "g0" and g > 0:
                            continue
                        for hh in range(4):
                            gh = g * 4 + hh; ch = gh // 2; base = (gh % 2) * 64
                            for r_ in range(dil):
                                for mt in range(nmt):
                                    cq = slice(r_ + dil * 128 * mt, r_ + dil * 128 * mt + dil * 127 + 1, dil)
                                    jo = co % 2; co += 1
                                    kts = ([(mt - 1, mprev)] if mt > 0 else []) + [(mt, mcur)]
                                    for n_, (kmt, msk) in enumerate(kts):
                                        ck = slice(r_ + dil * 128 * kmt, r_ + dil * 128 * kmt + dil * 127 + 1, dil)
                                        js = ce % 2; je = ce % 3; ce += 1
                                        h.mm(pss[js][:, 0:128], kT[base:base + 64, ch, ck], qT[base:base + 64, ch, cq], True, True,
                                             ["qT", "kT"], ["pss%d" % js])
                                        h.act(ee[je][:], pss[js][:, 0:128], AF.Exp, ["pss%d" % js], ["ee%d" % je])
                                        h.tt("dve" if ce % 2 else "pool", pp[je][:], ee[je][:], msk[:], ALU.mult, ["ee%d" % je, "mcur", "mprev"], ["pp%d" % je])
                                        h.mm(pso[jo][0:65, 0:128], VA[g][:, r_ * nmt + kmt, hh, :], pp[je][:], n_ == 0, n_ == len(kts) - 1,
                                             ["pp%d" % je, "VA%d" % g], ["pso%d" % jo])
                                    if g == 0:
                                        h.copy("dve", ACC[:, hh, cq], pso[jo][0:65, 0:128], ["pso%d" % jo], ["ACC%d" % hh])
                                    else:
                                        h.tt("dve", ACC[:, hh, cq], ACC[:, hh, cq], pso[jo][0:65, 0:128], ALU.add, ["pso%d" % jo, "ACC%d" % hh], ["ACC%d" % hh])
                    for hh in range(4):
                        if EP in ("g0", "nofinal"):
                            continue
                        h.ts("dve", RD[:, :], ACC[:, hh, :], 1e-30, None, ALU.max, None, ["ACC%d" % hh], ["RD"])
                        h.recip(RD[:, :], RD[:, :], ["RD"], ["RD"])
                        for cb in range(4):
                            cs_ = slice(cb * 512, (cb + 1) * 512)
                            h.mm(psb[0:64, :], onesf[:, :], RD[:, cs_], True, True, ["RD", "onesf"], ["psb"])
                            h.tt("dve", yaT[:, hh, tb + cb * 512:tb + (cb + 1) * 512], ACC[0:64, hh, cs_], psb[0:64, :], ALU.mult,
                                 ["psb", "ACC%d" % hh], ["yaT"])
                    end_phase()
        if "yaT" in dbg_out:
            with contextlib.ExitStack() as es:
                tmpd = [sbt(es, "tmpd%d" % i, [64, 2048]) for i in range(2)]
                ci = 0
                for hh in range(4):
                    for cb in range(NT // 2048):
                        j = ci % 2; ci += 1
                        h.copy("dve", tmpd[j][:], yaT[:, hh, cb * 2048:(cb + 1) * 2048], ["yaT"], ["tmpd%d" % j])
                        h.dma(dbg_out["yaT"][:, hh, cb * 2048:(cb + 1) * 2048], tmpd[j][:], ["tmpd%d" % j], ["dbg"])
                end_phase()
        if stop_after == "E":
            return nc, dbg_out, S

        with contextlib.ExitStack() as es:
            MU = sbt(es, "MU", [128, 3328])
            W0b = sbt(es, "W0b", [128, D]); A0b = sbt(es, "A0b", [128, D]); KKb = sbt(es, "KKb", [128, D]); KAb = sbt(es, "KAb", [128, D])
            W2A2 = sbt(es, "W2A2", [128, D]); G2t = sbt(es, "G2t", [128, D])
            zc = sbt(es, "zc", [128, 3328]); zp = sbt(es, "zp", [128, 3328])
            L = sbt(es, "L", [128, 256]); LT = sbt(es, "LT", [128, 2, 128])
            wt = sbt(es, "wt", [128, D]); at = sbt(es, "at", [128, D]); kk = sbt(es, "kk", [128, D]); sq = sbt(es, "sqb", [128, D])
            av = sbt(es, "av", [128, D]); bv = sbt(es, "bv", [128, D]); k2 = sbt(es, "k2", [128, D]); gt = sbt(es, "gt", [128, D])
            s16 = sbt(es, "s16", [128, 16])
            pl = pst(es, "pl", [128, 2, 128]); plw = pst(es, "plw", [128, D]); pla = pst(es, "pla", [128, D]); plg = pst(es, "plg", [128, D])
            bcast_load(MU[:], rwkv_mu[0:1, :], ["MU"])
            for t_, src in ((W0b, w0), (A0b, a0), (KKb, k_k), (KAb, k_a)):
                bcast_load(t_[:], src[0:1, :], ["cB"])
            h.dma(W2A2[0:64, :], w2, [], ["cB"]); h.dma(W2A2[64:128, :], a2, [], ["cB"]); h.dma(G2t[:], g2, [], ["cB"])
            for i in range(NTILE):
                T0 = i * 128
                h.dma(zc[:], zs[T0:T0 + 128, 0:3328], [], ["zc"])
                if i % 16 == 0:
                    h.memset("pool", zp[0:1, :], 0.0, ["zp"])
                    h.dma(zp[1:128, :], zs[T0:T0 + 127, 0:3328], ["zp"], ["zp"])
                else:
                    h.dma(zp[:], zs[T0 - 1:T0 + 127, 0:3328], [], ["zp"])
                h.tt("dve", zp[:], zp[:], zc[:], ALU.subtract, ["zp", "zc"], ["zp"])
                h.tt("pool", zp[:], zp[:], MU[:], ALU.mult, ["zp", "MU"], ["zp"])
                h.tt("dve", zp[:], zp[:], zc[:], ALU.add, ["zp", "zc"], ["zp"])
                h.act(L[:, 0:64], zp[:, 3072:3136], AF.Tanh, ["zp"], ["L"])
                h.copy("pool", L[:, 64:128], zp[:, 3136:3200], ["zp"], ["L"])
                h.act(L[:, 128:256], zp[:, 3200:3328], AF.Sigmoid, ["zp"], ["L"])
                h.tr(pl[:, 0, :], L[:, 0:128], idf[:], ["L", "idf"], ["pl"])
                h.tr(pl[:, 1, :], L[:, 128:256], idf[:], ["L", "idf"], ["pl"])
                h.copy("act", LT[:], pl[:], ["pl"], ["LT"])
                for hb in range(2):
                    cs_ = slice(hb * 512, (hb + 1) * 512)
                    h.mm(plw[:, cs_], LT[0:64, 0, :], W2A2[0:64, cs_], True, True, ["LT", "cB"], ["plw"])
                    h.mm(pla[:, cs_], LT[64:128, 0, :], W2A2[64:128, cs_], True, True, ["LT", "cB"], ["pla"])
                    h.mm(plg[:, cs_], LT[:, 1, :], G2t[:, cs_], True, True, ["LT", "cB"], ["plg"])
                h.tt("dve", wt[:], plw[:], W0b[:], ALU.add, ["plw", "cB"], ["wt"])
                h.act(wt[:], wt[:], AF.Sigmoid, ["wt"], ["wt"])
                h.ts("pool", wt[:], wt[:], -float(np.exp(-0.5)), None, ALU.mult, None, ["wt"], ["wt"])
                h.tt("dve", at[:], pla[:], A0b[:], ALU.add, ["pla", "cB"], ["at"])
                h.act(at[:], at[:], AF.Sigmoid, ["at"], ["at"])
                h.copy("act", gt[:], plg[:], ["plg"], ["gt"])
                h.tt("pool", kk[:], zp[:, 1024:2048], KKb[:], ALU.mult, ["zp", "cB"], ["kk"])
                h.tt("pool", sq[:], kk[:], kk[:], ALU.mult, ["kk"], ["sqb"])
                h.red(s16[:], sq[:].rearrange("p (a b) -> p a b", a=16), ALU.add, ["sqb"], ["s16"])
                h.act(s16[:], s16[:], AF.Sqrt, ["s16"], ["s16"])
                h.ts("dve", s16[:], s16[:], 1e-12, None, ALU.max, None, ["s16"], ["s16"])
                h.recip(s16[:], s16[:], ["s16"], ["s16"])
                kk3 = kk[:].rearrange("p (a b) -> p a b", a=16)
                h.tt("dve", kk3, kk3, bcl(s16[:], 64), ALU.mult, ["kk", "s16"], ["kk"])
                h.ts("pool", av[:], kk[:], -1.0, None, ALU.mult, None, ["kk"], ["av"])
                h.tt("pool", bv[:], kk[:], at[:], ALU.mult, ["kk", "at"], ["bv"])
                h.stt(k2[:], at[:], -1.0, KAb[:], ALU.add, ALU.mult, ["at", "cB"], ["k2"])
                h.stt(k2[:], k2[:], 1.0, zp[:, 1024:2048], ALU.add, ALU.mult, ["k2", "zp"], ["k2"])
                rows = slice(T0, T0 + 128)
                h.dma(sc["r"][rows, :], zp[:, 0:1024], ["zp"], ["sc_r"])
                h.dma(sc["v"][rows, :], zp[:, 2048:3072], ["zp"], ["sc_v"])
                h.dma(sc["w"][rows, :], wt[:], ["wt"], ["sc_w"])
                h.dma(sc["k"][rows, :], k2[:], ["k2"], ["sc_k"])
                h.dma(sc["a"][rows, :], av[:], ["av"], ["sc_a"])
                h.dma(sc["b"][rows, :], bv[:], ["bv"], ["sc_b"])
                h.dma(sc["g"][rows, :], gt[:], ["gt"], ["sc_g"])
            end_phase()

        with contextlib.ExitStack() as es:
            ltri = sbt(es, "ltri", [128, 128]); lblk = sbt(es, "lblk", [128, 128])
            m256 = sbt(es, "m256", [128, 256], BF16); mL = sbt(es, "mL", [128, 128], BF16); mtmp2 = sbt(es, "mtmp2", [128, 256])
            h.dma(ltri[:], ltri_d, [], ["cC"]); h.dma(lblk[:], lblk_d, [], ["cC"])
            h.dma(mtmp2[:], m256_d, [], ["mtmp2"]); h.copy("dve", m256[:], mtmp2[:], ["mtmp2"], ["cC"])
            h.dma(mtmp2[:, 0:128], mL_d, ["mtmp2"], ["mtmp2"]); h.copy("dve", mL[:], mtmp2[:, 0:128], ["mtmp2"], ["cC"])
            ld = {n: [sbt(es, "l%s%d" % (n, i), [128, D]) for i in range(2)] for n in ("r", "w", "k", "v", "a", "b")}
            ep = sbt(es, "ep", [128, D]); em = sbt(es, "em", [128, D]); eA = sbt(es, "eA", [128, D]); eE = sbt(es, "eE", [128, D])
            cumS = sbt(es, "cumS", [128, D]); gE = sbt(es, "gE", [128, D])
            Rb, Ab, Kb, Bb, Kh, Bh, Vb = (sbt(es, n, [128, D], BF16) for n in ("Rb", "Ab", "Kb", "Bb", "Kh", "Bh", "Vb"))
            ART = sbt(es, "ART", [128, 8, 2, 128], BF16); KT = sbt(es, "KTc", [128, 8, 128], BF16); BT = sbt(es, "BTc", [128, 8, 128], BF16)
            Gf = sbt(es, "Gf", [128, 8, 2])
            STf = sbt(es, "STf", [128, 8, 64]); STb = sbt(es, "STb", [128, 8, 64], BF16)
            ytile = [sbt(es, "ytile%d" % i, [128, D]) for i in range(2)]
            NG = 4
            hb_ = lambda n, shape: [sbt(es, "%s_%d" % (n, q), shape, BF16) for q in range(NG)]
            X1 = hb_("X1", [128, 256]); X2 = hb_("X2", [128, 256]); Q0 = hb_("Q0", [128, 128]); Q1 = hb_("Q1", [128, 128])
            P0 = hb_("P0", [128, 128]); P1 = hb_("P1", [128, 128]); QI = hb_("QI", [128, 128]); TT = hb_("TT", [128, 128])
            AVs = hb_("AVs", [128, 64]); Wt = hb_("Wt", [128, 64]); U0b = hb_("U0b", [128, 64]); R2T = hb_("R2T", [128, 128])
            MTc = hb_("MTc", [128, 2, 64])
            Y0s = [sbt(es, "Y0s_%d" % q, [128, 64]) for q in range(NG)]; Z0c = [sbt(es, "Z0c_%d" % q, [128, 2, 64]) for q in range(NG)]
            PB = [pst(es, "PB%d" % q, [128, 512]) for q in range(7)]
            PT = pst(es, "PTc", [128, 8, 128], BF16)

            def head_gen(hd, hq, yt):
                hb = hd // 2; base = (hd % 2) * 64; hs = slice(base, base + 64); cols = slice(hd * 64, (hd + 1) * 64)
                B = PB[hq]; bk = "PB%d" % hq; q_ = "_%d" % hq
                cp = "act" if hq % 2 == 0 else "dve"
                art2 = ART[hs, hb, :, :].rearrange("p a b -> p (a b)")
                h.mm(B[:, 0:256], BT[hs, hb, :], art2, True, True, ["ART", "BTc"], [bk])
                h.tt("dve", X1[hq][:], B[:, 0:256], m256[:], ALU.mult, ["cC"], [bk, "X1" + q_])
                yield
                h.mm(B[:, 0:256], KT[hs, hb, :], art2, True, True, ["ART", "KTc"], [bk])
                h.tt("dve", X2[hq][:], B[:, 0:256], m256[:], ALU.mult, ["cC"], [bk, "X2" + q_])
                yield
                h.mm(B[:, 0:128], ART[hs, hb, 0, :], BT[hs, hb, :], True, True, ["ART", "BTc"], [bk])
                h.tt("dve", Q0[hq][:], B[:, 0:128], mL[:], ALU.mult, ["cC"], [bk, "Q0" + q_])
                h.tt("pool", TT[hq][:], X1[hq][:, 0:128], idb[:], ALU.add, ["X1" + q_, "idb"], ["TT" + q_])
                yield
                P_, Pk = X1[hq][:, 0:128], "X1" + q_
                Q_, Qk = Q0[hq][:], "Q0" + q_
                Qbuf = [Q0[hq], Q1[hq]]; Pbuf = [P0[hq], P1[hq]]
                for k in range(1, 6):
                    Qn, Qnk = Qbuf[k % 2][:], "Q%d%s" % (k % 2, q_)
                    h.mm(B[:, 0:128], P_, Q_, True, True, [Pk, Qk], [bk])
                    h.copy(cp, Qn, B[:, 0:128], [], [bk, Qnk])
                    if k < 5:
                        Pn, Pnk = Pbuf[k % 2][:], "P%d%s" % (k % 2, q_)
                        h.mm(B[:, 128:256], Q_, P_, True, True, [Pk, Qk], [bk])
                        h.copy(cp, Pn, B[:, 128:256], [], [bk, Pnk])
                    yield
                    h.tt("pool", QI[hq][:], Qn, idb[:], ALU.add, [Qnk, "idb"], ["QI" + q_])
                    h.mm(B[:, 256:384], QI[hq][:], TT[hq][:], True, True, ["QI" + q_, "TT" + q_], [bk])
                    h.copy(cp, TT[hq][:], B[:, 256:384], [], [bk, "TT" + q_])
                    if k < 5:
                        P_, Pk = Pn, Pnk
                    Q_, Qk = Qn, Qnk
                    yield
                h.mm(B[:, 0:64], X2[hq][:, 0:128], Vb[:, cols], True, True, ["X2" + q_, "Vb"], [bk])
                h.copy(cp, AVs[hq][:], B[:, 0:64], [], [bk, "AVs" + q_])
                h.mm(B[:, 64:128], TT[hq][:], Ab[:, cols], True, True, ["TT" + q_, "Ab"], [bk])
                h.copy(cp, Wt[hq][:], B[:, 64:128], [], [bk, "Wt" + q_])
                yield
                h.mm(B[:, 0:64], TT[hq][:], AVs[hq][:], True, True, ["TT" + q_, "AVs" + q_], [bk])
                h.copy(cp, U0b[hq][:], B[:, 0:64], [], [bk, "U0b" + q_])
                h.mm(B[hs, 128:256], Wt[hq][:], X1[hq][:, 128:256], True, True, ["Wt" + q_, "X1" + q_], [bk])
                h.tt("dve", R2T[hq][hs, :], B[hs, 128:256], ART[hs, hb, 1, :], ALU.add, ["ART"], [bk, "R2T" + q_])
                yield
                h.mm(B[:, 0:64], X1[hq][:, 128:256], U0b[hq][:], True, False, ["X1" + q_, "U0b" + q_], [bk])
                h.mm(B[:, 0:64], X2[hq][:, 128:256], Vb[:, cols], False, True, ["X2" + q_, "Vb"], [bk])
                h.copy(cp, Y0s[hq][:], B[:, 0:64], [], [bk, "Y0s" + q_])
                for cidx in range(2):
                    c0_ = cidx * 64; cs = slice(c0_, c0_ + 64)
                    h.mm(B[hs, 64 + cidx * 64:128 + cidx * 64], Wt[hq][cs, :], Bh[cs, cols], True, True, ["Wt" + q_, "Bh"], [bk])
                    h.copy(cp, MTc[hq][hs, cidx, :], B[hs, 64 + cidx * 64:128 + cidx * 64], [], [bk, "MTc" + q_])
                yield
                for cidx in range(2):
                    c0_ = cidx * 64; cs = slice(c0_, c0_ + 64)
                    h.mm(B[hs, 256 + cidx * 64:320 + cidx * 64], Bh[cs, cols], U0b[hq][cs, :], True, False, ["Bh", "U0b" + q_], [bk])
                    h.mm(B[hs, 256 + cidx * 64:320 + cidx * 64], Kh[cs, cols], Vb[cs, cols], False, True, ["Kh", "Vb"], [bk])
                    h.copy(cp, Z0c[hq][hs, cidx, :], B[hs, 256 + cidx * 64:320 + cidx * 64], [], [bk, "Z0c" + q_])
                yield
                sk = "ST%d" % hd
                for cidx in range(2):
                    c0_ = cidx * 64; cs = slice(c0_, c0_ + 64)
                    h.mm(B[cs, 0:64], R2T[hq][hs, cs], STb[hs, hb, :], True, True, ["R2T" + q_, sk + "b"], [bk])
                    h.tt("dve", yt[cs, cols], B[cs, 0:64], Y0s[hq][cs, :], ALU.add, ["Y0s" + q_], [bk, "ytile"])
                    h.mm(B[hs, 64:128], MTc[hq][hs, cidx, :], STb[hs, hb, :], True, True, ["MTc" + q_, sk + "b"], [bk])
                    h.stt(STf[hs, hb, :], STf[hs, hb, :], Gf[hs, hb, cidx:cidx + 1], B[hs, 64:128], ALU.mult, ALU.add, [sk + "f", "Gf"], [bk, sk + "f"])
                    h.tt("pool", STf[hs, hb, :], STf[hs, hb, :], Z0c[hq][hs, cidx, :], ALU.add, [sk + "f", "Z0c" + q_], [sk + "f"])
                    h.copy("pool", STb[hs, hb, :], STf[hs, hb, :], [sk + "f"], [sk + "b"])
                    yield

            ntile_c = NTILE if stop_after != "Cshort" else 2
            for i in range(ntile_c):
                j = i % 2; rows = slice(i * 128, (i + 1) * 128)
                if i % 16 == 0:
                    h.memset("dve", STf[:], 0.0, ["ST%df" % q for q in range(16)])
                    h.memset("pool", STb[:], 0.0, ["ST%db" % q for q in range(16)])
                for n in ("r", "w", "k", "v", "a", "b"):
                    h.dma(ld[n][j][:], sc[n][rows, :], [], ["l%s%d" % (n, j)])
                lr, lw_, lk, lv, la, lb = (ld[n][j] for n in ("r", "w", "k", "v", "a", "b"))
                for hf in range(2):
                    cs_ = slice(hf * 512, (hf + 1) * 512)
                    h.mm(PB[hf][:, :], ltri[:], lw_[:, cs_], True, True, ["cC", "lw%d" % j], ["PB%d" % hf])
                    h.mm(PB[2 + hf][:, :], lblk[:], lw_[:, cs_], True, True, ["cC", "lw%d" % j], ["PB%d" % (2 + hf)])
                for hf in range(2):
                    cs_ = slice(hf * 512, (hf + 1) * 512); bk = "PB%d" % hf; bk2 = "PB%d" % (2 + hf)
                    h.act(ep[:, cs_], PB[hf][:, :], AF.Exp, [], [bk, "ep"])
                    h.act(em[:, cs_], PB[hf][:, :], AF.Exp, [], [bk, "em"], scale=-1.0)
                    h.copy("act", cumS[:, cs_], PB[hf][:, :], [], [bk, "cumS"])
                    h.act(gE[:, cs_], PB[2 + hf][:, :], AF.Exp, [], [bk2, "gE"])
                    h.tt("dve", eE[:, cs_], PB[2 + hf][:, :], cumS[:, cs_], ALU.subtract, ["cumS"], [bk2, "eE"])
                h.tt("dve", eA[:], cumS[:], lw_[:], ALU.subtract, ["cumS", "lw%d" % j], ["eA"])
                h.act(eA[:], eA[:], AF.Exp, ["eA"], ["eA"])
                h.act(eE[:], eE[:], AF.Exp, ["eE"], ["eE"])
                h.tt("dve", Rb[:], lr[:], ep[:], ALU.mult, ["lr%d" % j, "ep"], ["Rb"])
                h.tt("pool", Ab[:], la[:], eA[:], ALU.mult, ["la%d" % j, "eA"], ["Ab"])
                h.tt("dve", Kb[:], lk[:], em[:], ALU.mult, ["lk%d" % j, "em"], ["Kb"])
                h.tt("pool", Bb[:], lb[:], em[:], ALU.mult, ["lb%d" % j, "em"], ["Bb"])
                h.tt("dve", Kh[:], lk[:], eE[:], ALU.mult, ["lk%d" % j, "eE"], ["Kh"])
                h.tt("pool", Bh[:], lb[:], eE[:], ALU.mult, ["lb%d" % j, "eE"], ["Bh"])
                h.copy("pool", Vb[:], lv[:], ["lv%d" % j], ["Vb"])
                for hb in range(8):
                    h.mm(PB[4][:, hb * 2:hb * 2 + 2], gE[:, hb * 128:(hb + 1) * 128], idf[:, 0:128:64], True, True, ["gE", "idf"], ["PB4"])
                h.copy("dve", Gf[:].rearrange("p a b -> p (a b)"), PB[4][:, 0:16], [], ["PB4", "Gf"])
                for src, sk_, dst, dk in ((Ab, "Ab", ART[:, :, 0, :], "ART"), (Rb, "Rb", ART[:, :, 1, :], "ART"), (Kb, "Kb", KT[:], "KTc"), (Bb, "Bb", BT[:], "BTc")):
                    for kc in range(8):
                        h.tr(PT[:, kc, :], src[:, kc * 128:(kc + 1) * 128], idb[:], [sk_, "idb"], ["PTc"])
                    h.copy("act", dst, PT[:], [], ["PTc", dk])
                for g0 in range(0, 16, NG):
                    gens = [head_gen(g0 + q, q, ytile[j]) for q in range(NG)]
                    alive = list(gens)
                    while alive:
                        nxt = []
                        for g_ in alive:
                            try:
                                next(g_)
                                nxt.append(g_)
                            except StopIteration:
                                pass
                        alive = nxt
                h.dma(sc["y"][rows, :], ytile[j][:], ["ytile"], ["sc_y"])
            end_phase()

        with contextlib.ExitStack() as es:
            LNG = sbt(es, "LNG", [128, D]); LNB = sbt(es, "LNB", [128, D]); RKb = sbt(es, "RKb", [128, D])
            ld = {n: [sbt(es, "d%s%d" % (n, i), [128, D]) for i in range(2)] for n in ("y", "r", "k", "v", "g")}
            yc = sbt(es, "yc", [128, D]); sq2 = sbt(es, "sq2", [128, D]); prod = sbt(es, "prod", [128, D])
            m16 = sbt(es, "m16", [128, 16]); v16 = sbt(es, "v16", [128, 16]); b16 = sbt(es, "b16", [128, 16])
            yb = [sbt(es, "yb%d" % i, [128, D], BF16) for i in range(2)]
            yf = sbt(es, "yf", [128, D])
            bcast_load(LNG[:], lnx_g[0:1, :], ["cD"]); bcast_load(LNB[:], lnx_b[0:1, :], ["cD"]); bcast_load(RKb[:], r_k[0:1, :], ["cD"])
            v3 = lambda t_: t_[:].rearrange("p (a b) -> p a b", a=16)
            for i in range(NTILE):
                j = i % 2; rows = slice(i * 128, (i + 1) * 128)
                for n in ("y", "r", "k", "v", "g"):
                    h.dma(ld[n][j][:], sc[n][rows, :], [], ["d%s%d" % (n, j)])
                yt, rt, kt, vt, gt_ = (ld[n][j] for n in ("y", "r", "k", "v", "g"))
                h.red(m16[:], v3(yt), ALU.add, ["dy%d" % j], ["m16"])
                h.ts("dve", m16[:], m16[:], 1.0 / 64, None, ALU.mult, None, ["m16"], ["m16"])
                h.tt("dve", v3(yc), v3(yt), bcl(m16[:], 64), ALU.subtract, ["dy%d" % j, "m16"], ["yc"])
                h.tt("pool", sq2[:], yc[:], yc[:], ALU.mult, ["yc"], ["sq2"])
                h.red(v16[:], v3(sq2), ALU.add, ["sq2"], ["v16"])
                h.ts("dve", v16[:], v16[:], 1.0 / 64, GN_EPS, ALU.mult, ALU.add, ["v16"], ["v16"])
                h.act(v16[:], v16[:], AF.Sqrt, ["v16"], ["v16"])
                h.recip(v16[:], v16[:], ["v16"], ["v16"])
                h.tt("dve", v3(yc), v3(yc), bcl(v16[:], 64), ALU.mult, ["yc", "v16"], ["yc"])
                h.tt("pool", yc[:], yc[:], LNG[:], ALU.mult, ["yc", "cD"], ["yc"])
                h.tt("pool", yc[:], yc[:], LNB[:], ALU.add, ["yc", "cD"], ["yc"])
                h.tt("pool", prod[:], rt[:], kt[:], ALU.mult, ["dr%d" % j, "dk%d" % j], ["prod"])
                h.tt("pool", prod[:], prod[:], RKb[:], ALU.mult, ["prod", "cD"], ["prod"])
                h.red(b16[:], v3(prod), ALU.add, ["prod"], ["b16"])
                h.tt("dve", v3(prod), v3(vt), bcl(b16[:], 64), ALU.mult, ["dv%d" % j, "b16", "prod"], ["prod"])
                h.tt("dve", yc[:], yc[:], prod[:], ALU.add, ["yc", "prod"], ["yc"])
                h.tt("dve", yb[j][:], yc[:], gt_[:], ALU.mult, ["yc", "dg%d" % j], ["yb%d" % j])
                h.dma(yrs[rows, :], yb[j][:], ["yb%d" % j], ["yrs"])
                if "yr" in dbg_out:
                    h.tt("pool", yf[:], yc[:], gt_[:], ALU.mult, ["yc", "dg%d" % j], ["yf"])
                    h.dma(dbg_out["yr"][rows, :], yf[:], ["yf"], ["dbg"])
            end_phase()
        if stop_after in ("D", "Cshort"):
            return nc, dbg_out, S

        def load_w_bf16(es, dst, src, kcs, ncols, prows=128):
            stg = [sbt(es, "stg%d" % i, [128, 8, 512]) for i in range(2)]
            ci = 0
            for nb in range(ncols // 512):
                j = ci % 2; ci += 1
                h.dma(stg[j][0:prows, 0:kcs, :], src[:, nb * 512:(nb + 1) * 512].rearrange("(kc p) n -> p kc n", p=prows), [], ["stg%d" % j])
                h.copy("pool", dst[:, :, nb * 512:(nb + 1) * 512], stg[j][0:prows, 0:kcs, :], ["stg%d" % j], ["wconst"])

        with contextlib.ExitStack() as es:
            WBR = sbt(es, "WBR", [128, 8, D], BF16); WBA = sbt(es, "WBA", [64, 4, D], BF16); WO = sbt(es, "WO", [128, 8, D], BF16)
            with contextlib.ExitStack() as es1:
                load_w_bf16(es1, WBR, w_br_rwkv, 8, D)
                load_w_bf16(es1, WO, w_out, 8, D)
                load_w_bf16(es1, WBA, w_br_attn, 4, D, prows=64)
                S.barrier()
            GT1 = [sbt(es, "GT1_%d" % b, [128, D]) for b in range(NB)]
            G2n = [sbt(es, "G2n_%d" % b, [128, D]) for b in range(NB)]
            SH2 = [sbt(es, "SH2_%d" % b, [128, D]) for b in range(NB)]
            tmpg = sbt(es, "tmpg2", [128, D])
            bcast_load(tmpg[:], norm2_g[0:1, :], ["tmpg"])
            for b in range(NB):
                bcast_load(GT1[b][:], modd[b:b + 1, 2048:3072], ["cF"])
                bcast_load(SH2[b][:], modd[b:b + 1, 3072:4096], ["cF"])
                bcast_load(G2n[b][:], modd[b:b + 1, 4096:5120], ["G2n%d" % b])
                h.tt("dve", G2n[b][:], G2n[b][:], tmpg[:], ALU.mult, ["G2n%d" % b, "tmpg"], ["G2n%d" % b])
            ybl = [sbt(es, "ybl%d" % i, [128, D], BF16) for i in range(2)]
            yrT = sbt(es, "yrT", [128, 8, 128], BF16); mgT = sbt(es, "mgT", [128, 8, 128], BF16)
            zg = [sbt(es, "zg%d" % i, [128, 2048]) for i in range(2)]
            t1 = sbt(es, "t1", [128, D]); t2 = sbt(es, "t2", [128, D]); mgb = sbt(es, "mgb", [128, D], BF16)
            xt = [sbt(es, "xtF%d" % i, [128, D]) for i in range(2)]
            ht = [sbt(es, "htF%d" % i, [128, D]) for i in range(2)]
            junk = sbt(es, "junkF", [128, D]); n2b = [sbt(es, "n2bF%d" % i, [128, D], BF16) for i in range(2)]
            ss = sbt(es, "ssF", [128, NTILE]); rs = sbt(es, "rsF", [128, NTILE])
            ptr = pst(es, "ptrF", [128, 8, 128], BF16)
            pm1 = pst(es, "pm1", [128, D]); pm2 = pst(es, "pm2", [128, D]); po = pst(es, "poF", [128, D])
            for i in range(NTILE):
                b = i // 16; j = i % 2; rows = slice(i * 128, (i + 1) * 128)
                h.dma(ybl[j][:], yrs[rows, :], [], ["ybl%d" % j])
                h.dma(zg[j][:], zs[rows, 5632:7680], [], ["zg%d" % j])
                h.dma(xt[j][:], x[rows, :], [], ["xtF%d" % j])
                for kc in range(8):
                    h.tr(ptr[:, kc, :], ybl[j][:, kc * 128:(kc + 1) * 128], idb[:], ["ybl%d" % j, "idb"], ["ptrF"])
                h.copy("act", yrT[:], ptr[:], ["ptrF"], ["yrT"])
                for nb in range(2):
                    cs_ = slice(nb * 512, (nb + 1) * 512)
                    for kc in range(8):
                        h.mm(pm1[:, cs_], yrT[:, kc, :], WBR[:, kc, cs_], kc == 0, kc == 7, ["yrT"], ["pm1"])
                    for hh in range(4):
                        h.mm(pm2[:, cs_], yaT[:, hh, rows], WBA[:, hh, cs_], hh == 0, hh == 3, [], ["pm2"])
                h.act(zg[j][:], zg[j][:], AF.Sigmoid, ["zg%d" % j], ["zg%d" % j])
                h.tt("dve", t1[:], pm1[:], zg[j][:, 0:1024], ALU.mult, ["pm1", "zg%d" % j], ["t1"])
                h.tt("dve", t2[:], pm2[:], zg[j][:, 1024:2048], ALU.mult, ["pm2", "zg%d" % j], ["t2"])
                h.tt("pool", mgb[:], t1[:], t2[:], ALU.add, ["t1", "t2"], ["mgb"])
                for kc in range(8):
                    h.tr(ptr[:, kc, :], mgb[:, kc * 128:(kc + 1) * 128], idb[:], ["mgb", "idb"], ["ptrF"])
                h.copy("act", mgT[:], ptr[:], ["ptrF"], ["mgT"])
                for nb in range(2):
                    cs_ = slice(nb * 512, (nb + 1) * 512)
                    for kc in range(8):
                        h.mm(po[:, cs_], mgT[:, kc, :], WO[:, kc, cs_], kc == 0, kc == 7, ["mgT"], ["poF"])
                h.tt("dve", t1[:], po[:], GT1[b][:], ALU.mult, ["poF", "cF"], ["t1"])
                h.tt("pool", ht[j][:], t1[:], xt[j][:], ALU.add, ["t1", "xtF%d" % j], ["htF%d" % j])
                h.dma(hs[rows, :], ht[j][:], ["htF%d" % j], ["hs"])
                if "h" in dbg_out:
                    h.dma(dbg_out["h"][rows, :], ht[j][:], ["htF%d" % j], ["dbg"])
                h.act(junk[:], ht[j][:], AF.Square, ["htF%d" % j], ["junkF", "ssF%d" % i], accum=ss[:, i:i + 1])
                h.ts("dve", rs[:, i:i + 1], ss[:, i:i + 1], 1.0 / D, EPS, ALU.mult, ALU.add, ["ssF%d" % i], ["rsF%d" % i])
                h.act(rs[:, i:i + 1], rs[:, i:i + 1], AF.Sqrt, ["rsF%d" % i], ["rsF%d" % i])
                h.recip(rs[:, i:i + 1], rs[:, i:i + 1], ["rsF%d" % i], ["rsF%d" % i])
                h.stt(t2[:], ht[j][:], rs[:, i:i + 1], G2n[b][:], ALU.mult, ALU.mult, ["htF%d" % j, "rsF%d" % i, "G2n%d" % b], ["t2"])
                h.tt("pool", n2b[j][:], t2[:], SH2[b][:], ALU.add, ["t2", "cF"], ["n2bF%d" % j])
                h.dma(n2s[rows, :], n2b[j][:], ["n2bF%d" % j], ["n2s"])
            end_phase()
        if stop_after == "F":
            return nc, dbg_out, S

        with contextlib.ExitStack() as es:
            WQ = sbt(es, "WQ", [128, 8, 2048], BF16)
            with contextlib.ExitStack() as es1:
                load_w_bf16(es1, WQ, peer_wq, 8, 2048)
                S.barrier()
            with contextlib.ExitStack() as es1:
                cf = [sbt(es1, "cf%d" % i, [128, 4, D]) for i in range(3)]
                cb = [sbt(es1, "cb%d" % i, [128, 4, D], BF16) for i in range(3)]
                ci = 0
                for src, off_ in ((peer_u, 0), (peer_v, D)):
                    s3 = src.rearrange("(p r) d -> p r d", p=128); d3 = uvb.rearrange("(p r) d -> p r d", p=128)[:, :, off_:off_ + D]
                    for ch in range(32):
                        j = ci % 3; ci += 1
                        h.dma(cf[j][:], s3[:, ch * 4:(ch + 1) * 4, :], [], ["cf%d" % j])
                        h.copy(("act", "pool", "dve")[j], cb[j][:], cf[j][:], ["cf%d" % j], ["cb%d" % j])
                        h.dma(d3[:, ch * 4:(ch + 1) * 4, :], cb[j][:], ["cb%d" % j], ["ubvb"])
                S.barrier()
            K1T = sbt(es, "K1T", [128, 128]); K2T = sbt(es, "K2T", [128, 128]); ktmp = sbt(es, "ktmp", [128, 128])
            GT2 = [sbt(es, "GT2_%d" % b, [128, D]) for b in range(NB)]
            n2b = sbt(es, "n2bG", [128, D], BF16); n2Tt = sbt(es, "n2Tt", [128, 8, 128], BF16)
            qTt = sbt(es, "qTt", [128, 16, 128])
            S1 = sbt(es, "S1", [128, 8, 128]); S2 = sbt(es, "S2", [128, 8, 128]); wk = sbt(es, "wk", [128, 128])
            v1 = sbt(es, "v1", [128, 8, 16]); v2 = sbt(es, "v2", [128, 8, 16])
            i1u = sbt(es, "i1u", [128, 8, 16], U32); i2u = sbt(es, "i2u", [128, 8, 16], U32)
            i1f = sbt(es, "i1f", [128, 8, 16]); i2f = sbt(es, "i2f", [128, 8, 16])
            cand = sbt(es, "cand", [128, 16, 16]); cidx = sbt(es, "cidx", [128, 16, 16]); wk2 = sbt(es, "wk2", [128, 256]); junk2 = sbt(es, "junk2", [128, 256])
            t16 = sbt(es, "t16", [128, 8, 16]); idxf = sbt(es, "idxf", [128, 128]); e16 = sbt(es, "e16", [128, 8, 16]); gate = sbt(es, "gate", [128, 128])
            nmx = sbt(es, "nmx", [128, 8]); Zs = sbt(es, "Zs", [128, 8])
            IDXT = sbt(es, "IDXT", [128, 128], I32); GTt = sbt(es, "GTt", [128, 128]); dots = sbt(es, "dots", [128, 128]); coef = sbt(es, "coef", [128, 128])
            NUB = 10
            UV = [sbt(es, "UV%d" % i, [128, 2 * D], BF16) for i in range(NUB)]
            glu = sbt(es, "glu", [128, 128])
            junkU = sbt(es, "junkU", [128, D], BF16)
            IX1 = [sbt(es, "IX1_%d" % i, [128, 1], I32) for i in range(NUB)]
            coefb = sbt(es, "coefb", [128, 128], BF16)
            WB = [sbt(es, "WB%d" % i, [128, 256], BF16) for i in range(4)]
            htG = sbt(es, "htG", [128, D]); t1 = sbt(es, "t1G", [128, D]); yo = sbt(es, "yo", [128, D])
            ptr = pst(es, "ptrG", [128, 8, 128], BF16)
            pB = pst(es, "pB", [128, 512])
            pbc = [pst(es, "pbc%d" % i, [128, D]) for i in range(2)]
            po = pst(es, "poG", [128, D])
            for b in range(NB):
                bcast_load(GT2[b][:], modd[b:b + 1, 5120:6144], ["cG"])
            for kt_, src in ((K1T, peer_k1), (K2T, peer_k2)):
                h.dma(ktmp[:], src, [], ["ktmp"])
                h.tr(pB[:, 0:128], ktmp[:], idf[:], ["ktmp", "idf"], ["pB"])
                h.copy("dve", kt_[:], pB[:, 0:128], ["pB"], ["cG"])
            for w_ in WB:
                h.memset("pool", w_[:], 0.0, ["cG"])
            ntile_g = NTILE if stop_after != "Gshort" else 1
            for i in range(ntile_g):
                b = i // 16; rows = slice(i * 128, (i + 1) * 128)
                h.dma(n2b[:], n2s[rows, :], [], ["n2bG"])
                h.dma(htG[:], hs[rows, :], [], ["htG"])
                for kc in range(8):
                    h.tr(ptr[:, kc, :], n2b[:, kc * 128:(kc + 1) * 128], idb[:], ["n2bG", "idb"], ["ptrG"])
                h.copy("act", n2Tt[:], ptr[:], ["ptrG"], ["n2Tt"])
                for c4 in range(4):
                    for cq in range(4):
                        cc = c4 * 4 + cq
                        for kc in range(8):
                            h.mm(pB[:, cq * 128:(cq + 1) * 128], WQ[:, kc, cc * 128:(cc + 1) * 128], n2Tt[:, kc, :], kc == 0, kc == 7, ["n2Tt"], ["pB"])
                    h.copy("act" if c4 % 2 == 0 else "dve", qTt[:, c4 * 4:(c4 + 1) * 4, :], pB[:].rearrange("p (a b) -> p a b", a=4), ["pB"], ["qTt"])
                for which, Sx, KT in ((0, S1, K1T), (1, S2, K2T)):
                    for h4 in range(2):
                        for hq in range(4):
                            hd = h4 * 4 + hq
                            h.mm(pB[:, hq * 128:(hq + 1) * 128], qTt[:, 2 * hd + which, :], KT[:], True, True, ["qTt", "cG"], ["pB"])
                        h.copy("act" if h4 == 0 else "dve", Sx[:, h4 * 4:(h4 + 1) * 4, :], pB[:].rearrange("p (a b) -> p a b", a=4), ["pB"], ["S%d" % which])
                import os
                GCUT = float(os.environ.get("GCUT", "9"))
                if GCUT <= 1:
                    continue
                for which, Sx, vx, ix in ((0, S1, v1, i1u), (1, S2, v2, i2u)):
                    sk = "S%d" % which; vk_ = "v%d" % which
                    for hd in range(8):
                        S.op("dve", (lambda o, i_: (lambda e: e.max(out=o, in_=i_)))(vx[:, hd, 0:8], Sx[:, hd, :]), reads=[sk], writes=[vk_])
                        S.op("dve", (lambda o, r_, i_: (lambda e: e.match_replace(out=o, in_to_replace=r_, in_values=i_, imm_value=-1e30)))(wk[:], vx[:, hd, 0:8], Sx[:, hd, :]),
                             reads=[sk, vk_], writes=["wk"])
                        S.op("dve", (lambda o, i_: (lambda e: e.max(out=o, in_=i_)))(vx[:, hd, 8:16], wk[:]), reads=["wk"], writes=[vk_])
                        S.op("dve", (lambda o, m_, i_: (lambda e: e.max_index(out=o, in_max=m_, in_values=i_)))(ix[:, hd, 0:8], vx[:, hd, 0:8], Sx[:, hd, :]),
                             reads=[sk, vk_], writes=["ix%d" % which])
                        S.op("dve", (lambda o, m_, i_: (lambda e: e.max_index(out=o, in_max=m_, in_values=i_)))(ix[:, hd, 8:16], vx[:, hd, 8:16], Sx[:, hd, :]),
                             reads=[sk, vk_], writes=["ix%d" % which])
                if GCUT <= 2:
                    continue
                h.copy("dve", i1f[:], i1u[:], ["ix0"], ["i1f"])
                h.copy("dve", i2f[:], i2u[:], ["ix1"], ["i2f"])
                h.ts("dve", i1f[:], i1f[:], 128.0, None, ALU.mult, None, ["i1f"], ["i1f"])
                cflat = cand[:].rearrange("p a b -> p (a b)"); xflat = cidx[:].rearrange("p a b -> p (a b)")
                for hd in range(8):
                    h.tt("pool", cand[:], bcl(v1[:, hd, :], 16), bc3(v2[:, hd, :], 16), ALU.add, ["v0", "v1"], ["cand"])
                    h.tt("pool", cidx[:], bcl(i1f[:, hd, :], 16), bc3(i2f[:, hd, :], 16), ALU.add, ["i1f", "i2f"], ["cidx"])
                    S.op("dve", (lambda o, i_: (lambda e: e.max(out=o, in_=i_)))(t16[:, hd, 0:8], cflat), reads=["cand"], writes=["t16"])
                    S.op("dve", (lambda o, r_, i_: (lambda e: e.match_replace(out=o, in_to_replace=r_, in_values=i_, imm_value=-1e30)))(wk2[:], t16[:, hd, 0:8], cflat),
                         reads=["cand", "t16"], writes=["wk2"])
                    S.op("dve", (lambda o, i_: (lambda e: e.max(out=o, in_=i_)))(t16[:, hd, 8:16], wk2[:]), reads=["wk2"], writes=["t16"])
                    for k in range(16):
                        if GCUT <= 2.2:
                            continue
                        h.stt(junk2[:], cflat, t16[:, hd, k:k + 1], xflat, ALU.is_equal, ALU.mult, ["cand", "cidx", "t16"], ["junk2", "idxf"],
                              accum=idxf[:, hd * 16 + k:hd * 16 + k + 1])
                h.ts("dve", idxf[:], idxf[:], 16383.0, 0.0, ALU.min, ALU.max, ["idxf"], ["idxf"])
                if GCUT <= 2.4:
                    continue
                h.ts("dve", nmx[:], t16[:, :, 0], -1.0, None, ALU.mult, None, ["t16"], ["nmx"])
                for hd in range(8):
                    h.act(e16[:, hd, :], t16[:, hd, :], AF.Exp, ["t16", "nmx"], ["e16", "Zs"], bias=nmx[:, hd:hd + 1], accum=Zs[:, hd:hd + 1])
                h.recip(Zs[:], Zs[:], ["Zs"], ["Zs"])
                h.tt("dve", gate[:].rearrange("p (a b) -> p a b", a=8), e16[:], bcl(Zs[:], 16), ALU.mult, ["e16", "Zs"], ["gate"])
                if GCUT <= 2.6:
                    continue
                h.tr(pB[:, 0:128], idxf[:], idf[:], ["idxf", "idf"], ["pB"])
                h.tr(pB[:, 128:256], gate[:], idf[:], ["gate", "idf"], ["pB"])
                if GCUT <= 2.7:
                    continue
                h.ts("dve", dots[:], pB[:, 0:128], 8388608.0, None, ALU.add, None, ["pB"], ["dots"])
                S.op("dve", (lambda o, i_: (lambda e: e.tensor_scalar(out=o, in0=i_, scalar1=0x7FFFFF, scalar2=None, op0=ALU.bitwise_and)))(IDXT[:], dots[:].bitcast(I32)), reads=["dots"], writes=["IDXT"])
                if GCUT <= 2.8:
                    continue
                h.copy("dve", GTt[:], pB[:, 128:256], ["pB"], ["GTt"])
                import os
                GCUT = float(os.environ.get("GCUT", "9"))
                if "idxf" in dbg_out and i == 0:
                    h.dma(dbg_out["idxf"], idxf[:], ["idxf"], ["dbg"])
                    h.dma(dbg_out["gate"], gate[:], ["gate"], ["dbg"])
                    h.dma(dbg_out["GTt"], GTt[:], ["GTt"], ["dbg"])
                    h.copy("dve", dots[:], IDXT[:], ["IDXT"], ["dots"])
                    h.dma(dbg_out["IDXTf"], dots[:], ["dots"], ["dbg"])
                if GCUT <= 3:
                    continue
                LA = NUB - 2

                def issue_gather(c2):
                    j2 = c2 % NUB
                    h.copy("act", IX1[j2][:], IDXT[:, c2:c2 + 1], ["IDXT"], ["IX%d" % j2])
                    S.dma("pool", (lambda o, ix_: (lambda e: e.indirect_dma_start(out=o, out_offset=None, in_=uvb,
                                                                                  in_offset=bass.IndirectOffsetOnAxis(ap=ix_, axis=0))))(UV[j2][:], IX1[j2][:, 0:1]),
                          reads=["IX%d" % j2], writes=["UV%d" % j2])
                def emit_out(c3):
                    j3 = c3 % NUB; jw3 = c3 % 4
                    h.tt("pool", WB[jw3][:, 127:128], glu[:, c3:c3 + 1], GTt[:, c3:c3 + 1], ALU.mult, ["glu%d" % (c3 % 8), "GTt"], ["WB%d" % jw3])
                    for hb3 in range(2):
                        h.mm(po[:, hb3 * 512:(hb3 + 1) * 512], WB[jw3][:, 127 - c3:255 - c3], UV[j3][:, D + hb3 * 512:D + (hb3 + 1) * 512], c3 == 0, c3 == 127,
                             ["WB%d" % jw3, "UV%d" % j3], ["poG"])
                def emit_bcast(c4):
                    jb4 = c4 % 2
                    for hb4 in range(2):
                        cs4 = slice(hb4 * 512, (hb4 + 1) * 512)
                        h.mm(pbc[jb4][:, cs4], idb[:, c4:c4 + 1].to_broadcast([128, 128]), n2b[:, cs4], True, True, ["n2bG", "idb"], ["pbc%d" % jb4])
                for c2 in range(LA):
                    issue_gather(c2)
                emit_bcast(0)
                for c in range(128):
                    ju = c % NUB; jb = c % 2; jw = c % 4
                    if c + LA < 128:
                        issue_gather(c + LA)
                    if c + 1 < 128:
                        emit_bcast(c + 1)
                    h.stt(junkU[:], UV[ju][:, 0:D], 1.0, pbc[jb][:], ALU.mult, ALU.mult, ["UV%d" % ju, "pbc%d" % jb], ["junkU", "dots%d" % (c % 8)], accum=dots[:, c:c + 1])
                    h.act(glu[:, c:c + 1], dots[:, c:c + 1], AF.Gelu, ["dots%d" % (c % 8)], ["glu%d" % (c % 8)])
                    if c >= 1:
                        emit_out(c - 1)
                emit_out(127)
                h.tt("dve", t1[:], po[:], GT2[b][:], ALU.mult, ["poG", "cG"], ["t1G"])
                h.tt("pool", yo[:], t1[:], htG[:], ALU.add, ["t1G", "htG"], ["yo"])
                last_tok[0] = h.dma(y_out[rows, :], yo[:], ["yo"], ["y"])
            end_phase()
        return nc, dbg_out, S


def host_consts():
    ident = np.eye(128, dtype=np.float32)
    half = 8
    inv = (500000.0 ** (-np.arange(half, dtype=np.float32) / half)).astype(np.float32)
    pos = np.arange(SEQ, dtype=np.float32)
    ang = (pos[:, None] * inv[None, :]).astype(np.float32)
    cs = np.concatenate([np.cos(ang), np.sin(ang)], axis=1).astype(np.float32)
    cs = cs.reshape(16, 128, 16).transpose(1, 0, 2).copy()
    k = np.arange(128)[:, None]; q = np.arange(128)[None, :]
    mcur = (k <= q).astype(np.float32); mprev = (k >= q).astype(np.float32)
    same = (k // 64) == (q // 64)
    ltri = (same & (k <= q)).astype(np.float32)
    lblk = same.astype(np.float32)
    m256 = np.concatenate([(same & (k < q)), (same & (k <= q))], axis=1).astype(np.float32)
    mL = (same & (k > q)).astype(np.float32)
    return dict(ident=ident, cs=cs, mcur=mcur, mprev=mprev, ltri=ltri, lblk=lblk, m256=m256, mL=mL)


def make_in_maps(inputs, n_cores=8):
    consts = host_consts()
    shared = {}
    for k in ("w_ada", "w_in", "w2", "a2", "g2", "w_br_rwkv", "w_br_attn", "w_out", "peer_wq", "peer_k1", "peer_k2",
              "peer_u", "peer_v", "q_norm_g", "k_norm_g"):
        shared[k] = np.ascontiguousarray(inputs[k][0], dtype=np.float32)
    for k in ("b_ada", "norm1_g", "rwkv_mu", "w0", "a0", "k_k", "k_a", "lnx_g", "lnx_b", "norm2_g"):
        shared[k] = np.ascontiguousarray(inputs[k][0].reshape(1, -1), dtype=np.float32)
    shared["r_k"] = np.ascontiguousarray(inputs["r_k"][0].reshape(1, -1), dtype=np.float32)
    shared.update(consts)
    maps = []
    for c in range(n_cores):
        m = dict(shared)
        m["x"] = np.ascontiguousarray(inputs["x"][c * NB:(c + 1) * NB].reshape(NT, D), dtype=np.float32)
        cc = np.asarray(inputs["c"][c * NB:(c + 1) * NB], dtype=np.float32)
        m["cT"] = np.ascontiguousarray(cc.reshape(NB, 8, 128).transpose(2, 1, 0))
        maps.append(m)
    return maps


def kernel(**inputs):
    nc, _, _ = build_core()
    maps = make_in_maps(inputs, 8)
    res = run_bass_kernel_spmd(nc, maps, core_ids=list(range(8)))
    outs = [np.asarray(r["y"]).reshape(NB, SEQ, D) for r in res.results]
    return np.concatenate(outs, axis=0).astype(np.float32)
```

```python
import contextlib
import numpy as np
import ml_dtypes
import concourse.bass as bass
import concourse.mybir as mybir
from concourse.bass_utils import run_bass_kernel_spmd

F32 = mybir.dt.float32
BF16 = mybir.dt.bfloat16
U32 = mybir.dt.uint32
I32 = mybir.dt.int32
AF = mybir.ActivationFunctionType
ALU = mybir.AluOpType
AX = mybir.AxisListType

ENGS = ("pe", "act", "dve", "pool", "sp")
NDMA = {"sp": 8, "pool": 16, "act": 4}

D = 1024
SEQ = 2048
NB = 2
NT = NB * SEQ
NTILE = NT // 128
ZC = 7680
EPS = 1e-6
GN_EPS = 64e-5


class Sched:
    def __init__(self, nc, sems):
        self.nc = nc
        self.sems = sems
        self.ops = {e: [] for e in ENGS}
        self.cnt = {e: 0 for e in ENGS}
        self.dma_slot_cnt = {e: [0] * n for e, n in NDMA.items()}
        self.dma_rr = {e: 0 for e in NDMA}
        self.state = {}
        self.waited = {e: {} for e in ENGS}

    def _deps(self, eng, reads, writes):
        need = {}

        def add(tok):
            if tok is None:
                return
            s, v, e = tok
            if e == "pe" and eng == "pe" and s == "c_pe":
                return
            if need.get(s, 0) < v:
                need[s] = v
        for k in reads:
            st = self.state.get(k)
            if st:
                add(st[0])
        for k in writes:
            st = self.state.get(k)
            if st:
                add(st[0])
                for t in st[1].values():
                    add(t)
        out = []
        w = self.waited[eng]
        for s, v in need.items():
            if w.get(s, 0) < v:
                w[s] = v
                out.append((s, v))
        return out

    def _commit(self, tok, reads, writes):
        for k in reads:
            st = self.state.setdefault(k, [None, {}])
            st[1][tok[0]] = tok
        for k in writes:
            self.state[k] = [tok, {}]

    def op(self, eng, fn, reads=(), writes=()):
        waits = self._deps(eng, reads, writes)
        self.cnt[eng] += 1
        tok = ("c_" + eng, self.cnt[eng], eng)
        self.ops[eng].append(("op", fn, waits, tok))
        self._commit(tok, reads, writes)
        return tok

    def dma(self, eng, fn, reads=(), writes=()):
        slot = self.dma_rr[eng]
        self.dma_rr[eng] = (slot + 1) % NDMA[eng]
        sname = "d_%s_%d" % (eng, slot)
        waits = self._deps(eng, reads, writes)
        prev = self.dma_slot_cnt[eng][slot]
        w = self.waited[eng]
        if prev > 0 and w.get(sname, 0) < prev:
            w[sname] = prev
            waits.append((sname, prev))
        self.dma_slot_cnt[eng][slot] = prev + 16
        tok = (sname, prev + 16, eng)
        self.ops[eng].append(("dma", fn, waits, tok))
        self._commit(tok, reads, writes)
        return tok

    def barrier(self):
        toks = []
        for e in ENGS:
            if e != "sp" and self.cnt[e] > 0:
                toks.append(("c_" + e, self.cnt[e]))
        for e, n in NDMA.items():
            for i in range(n):
                if self.dma_slot_cnt[e][i] > 0:
                    toks.append(("d_%s_%d" % (e, i), self.dma_slot_cnt[e][i]))
        for e in ENGS:
            w = self.waited[e]
            ws = []
            for s, v in toks:
                if w.get(s, 0) < v:
                    w[s] = v
                    ws.append((s, v))
            if ws:
                self.ops[e].append(("wait", None, ws, None))
        self.state = {}

    def emit(self):
        nc = self.nc
        sems = self.sems
        ops = self.ops
        self.ops = {e: [] for e in ENGS}
        with nc.Block() as block:
            def run(engname):
                def body(engine):
                    for kind, fn, waits, tok in ops[engname]:
                        for s, v in waits:
                            engine.wait_ge(sems[s], v)
                        if kind == "wait":
                            continue
                        ins = fn(engine)
                        ins.then_inc(sems[tok[0]], 16 if kind == "dma" else 1)
                return body
            block.tensor(run("pe"))
            block.scalar(run("act"))
            block.vector(run("dve"))
            block.gpsimd(run("pool"))
            block.sync(run("sp"))


def sem_names():
    names = ["c_" + e for e in ENGS if e != "sp"]
    for e, n in NDMA.items():
        names += ["d_%s_%d" % (e, i) for i in range(n)]
    return names


class H:
    def __init__(self, S):
        self.S = S

    def dma(self, out, in_, r, w, eng="sp"):
        return self.S.dma(eng, lambda e: e.dma_start(out=out, in_=in_), reads=r, writes=w)

    def tt(self, eng, out, in0, in1, op, r, w):
        return self.S.op(eng, lambda e: e.tensor_tensor(out=out, in0=in0, in1=in1, op=op), reads=r, writes=w)

    def ts(self, eng, out, in0, s1, s2, op0, op1, r, w, accum=None):
        if op1 is None:
            return self.S.op(eng, lambda e: e.tensor_scalar(out=out, in0=in0, scalar1=s1, scalar2=None, op0=op0), reads=r, writes=w)
        if accum is None:
            return self.S.op(eng, lambda e: e.tensor_scalar(out=out, in0=in0, scalar1=s1, scalar2=s2, op0=op0, op1=op1), reads=r, writes=w)
        return self.S.op(eng, lambda e: e.tensor_scalar(out=out, in0=in0, scalar1=s1, scalar2=s2, op0=op0, op1=op1, accum_out=accum), reads=r, writes=w)

    def stt(self, out, in0, scalar, in1, op0, op1, r, w, accum=None):
        if accum is None:
            return self.S.op("dve", lambda e: e.scalar_tensor_tensor(out=out, in0=in0, scalar=scalar, in1=in1, op0=op0, op1=op1), reads=r, writes=w)
        return self.S.op("dve", lambda e: e.scalar_tensor_tensor(out=out, in0=in0, scalar=scalar, in1=in1, op0=op0, op1=op1, accum_out=accum), reads=r, writes=w)

    def copy(self, eng, out, in_, r, w):
        if eng == "act":
            return self.S.op("act", lambda e: e.copy(out=out, in_=in_), reads=r, writes=w)
        return self.S.op(eng, lambda e: e.tensor_copy(out=out, in_=in_), reads=r, writes=w)

    def act(self, out, in_, func, r, w, scale=1.0, bias=None, accum=None):
        def fn(e):
            kw = dict(out=out, in_=in_, func=func, scale=scale)
            if bias is not None:
                kw["bias"] = bias
            if accum is not None:
                kw["accum_out"] = accum
            return e.activation(**kw)
        return self.S.op("act", fn, reads=r, writes=w)

    def red(self, out, in_, op, r, w):
        return self.S.op("dve", lambda e: e.tensor_reduce(out=out, in_=in_, axis=AX.X, op=op), reads=r, writes=w)

    def recip(self, out, in_, r, w):
        return self.S.op("dve", lambda e: e.reciprocal(out=out, in_=in_), reads=r, writes=w)

    def memset(self, eng, ap, val, w):
        return self.S.op(eng, lambda e: e.memset(ap, val), reads=(), writes=w)

    def mm(self, out, lhsT, rhs, start, stop, r, w):
        return self.S.op("pe", lambda e: e.matmul(out, lhsT, rhs, start=start, stop=stop), reads=r, writes=w)

    def tr(self, out, in_, ident, r, w):
        return self.S.op("pe", lambda e: e.transpose(out=out, in_=in_, identity=ident), reads=r, writes=w)


def bc3(ap, n_mid):
    p, f = ap.shape
    return ap.unsqueeze(1).to_broadcast([p, n_mid, f])


def bcl(ap, n_last):
    p, f = ap.shape
    return ap.unsqueeze(2).to_broadcast([p, f, n_last])


def build_core(stop_after=None, dbg=()):
    nc = bass.Bass("TRN2", target_bir_lowering=False)
    din = lambda name, shape, dt=F32: nc.dram_tensor(name, shape, dt, kind="ExternalInput").ap()
    x = din("x", [NT, D])
    cT = din("cT", [128, 8, NB])
    w_ada = din("w_ada", [D, 6 * D]); b_ada = din("b_ada", [1, 6 * D])
    norm1_g = din("norm1_g", [1, D]); w_in = din("w_in", [D, ZC])
    rwkv_mu = din("rwkv_mu", [1, 3328]); w0 = din("w0", [1, D]); w2 = din("w2", [64, D])
    a0 = din("a0", [1, D]); a2 = din("a2", [64, D]); g2 = din("g2", [128, D])
    k_k = din("k_k", [1, D]); k_a = din("k_a", [1, D]); r_k = din("r_k", [1, D])
    lnx_g = din("lnx_g", [1, D]); lnx_b = din("lnx_b", [1, D])
    q_norm_g = din("q_norm_g", [3, 64]); k_norm_g = din("k_norm_g", [3, 64])
    w_br_rwkv = din("w_br_rwkv", [D, D]); w_br_attn = din("w_br_attn", [256, D]); w_out = din("w_out", [D, D])
    norm2_g = din("norm2_g", [1, D]); peer_wq = din("peer_wq", [D, 2048])
    peer_k1 = din("peer_k1", [128, 128]); peer_k2 = din("peer_k2", [128, 128])
    peer_u = din("peer_u", [16384, D]); peer_v = din("peer_v", [16384, D])
    ident_d = din("ident", [128, 128]); cs_d = din("cs", [128, 16, 16])
    mcur_d = din("mcur", [128, 128]); mprev_d = din("mprev", [128, 128])
    ltri_d = din("ltri", [128, 128]); lblk_d = din("lblk", [128, 128]); m256_d = din("m256", [128, 256]); mL_d = din("mL", [128, 128])
    y_out = nc.dram_tensor("y", [NT, D], F32, kind="ExternalOutput").ap()

    dscr = lambda name, shape, dt=F32: nc.dram_tensor(name, shape, dt).ap()
    modd = dscr("modd", [NB, 6 * D])
    zs = dscr("zs", [NT, ZC])
    sc = {k: dscr("sc_" + k, [NT, D]) for k in ("r", "w", "k", "v", "a", "b", "g", "y")}
    yrs = dscr("yrs", [NT, D], BF16)
    hs = dscr("hs", [NT, D])
    n2s = dscr("n2s", [NT, D], BF16)
    uvb = dscr("uvb", [16384, 2 * D], BF16)
    dbg_out = {}
    for name, shape in dbg:
        dbg_out[name] = nc.dram_tensor("dbg_" + name, shape, F32, kind="ExternalOutput").ap()

    with contextlib.ExitStack() as top:
        sems = {n: top.enter_context(nc.semaphore(n)) for n in sem_names()}
        S = Sched(nc, sems)
        h = H(S)
        uid = [0]

        def sbt(es, name, shape, dt=F32):
            uid[0] += 1
            return es.enter_context(nc.sbuf_tensor("s%d_%s" % (uid[0], name), shape, dt))

        def pst(es, name, shape, dt=F32):
            uid[0] += 1
            return es.enter_context(nc.psum_tensor("p%d_%s" % (uid[0], name), shape, dt))

        idf = sbt(top, "idf", [128, 128]); idb = sbt(top, "idb", [128, 128], BF16)
        yaT = sbt(top, "yaT", [64, 4, NT], BF16)
        h.dma(idf[:], ident_d, [], ["idf"])
        h.copy("dve", idb[:], idf[:], ["idf"], ["idb"])
        last_tok = [None]

        def end_phase():
            S.barrier()
            S.emit()

        with contextlib.ExitStack() as es:
            ct = sbt(es, "ct", [128, 8, NB]); sil = sbt(es, "sil", [128, 8, NB])
            wad = [sbt(es, "wad%d" % i, [128, 8, 512]) for i in range(2)]
            bad = sbt(es, "bad", [1, 6 * D]); modrow = sbt(es, "modrow", [1, NB, 6 * D])
            psm = [pst(es, "psm%d" % i, [128, 512]) for i in range(NB)]
            h.dma(ct[:], cT, [], ["ct"])
            h.dma(bad[:], b_ada, [], ["bad"])
            h.act(sil[:], ct[:], AF.Silu, ["ct"], ["sil"])
            for nb in range(12):
                wb = wad[nb % 2]
                h.dma(wb[:], w_ada[:, nb * 512:(nb + 1) * 512].rearrange("(kc p) n -> p kc n", p=128), [], ["wad%d" % (nb % 2)])
                for b in range(NB):
                    for kc in range(8):
                        h.mm(psm[b][0:1, :], sil[:, kc, b:b + 1], wb[:, kc, :], kc == 0, kc == 7,
                             ["sil", "wad%d" % (nb % 2)], ["psm%d" % b])
                    h.tt("dve", modrow[0:1, b, nb * 512:(nb + 1) * 512], psm[b][0:1, :], bad[0:1, nb * 512:(nb + 1) * 512],
                         ALU.add, ["psm%d" % b, "bad"], ["modrow"])
            for b in range(NB):
                for o in (1024, 4096):
                    h.ts("dve", modrow[0:1, b, o:o + 1024], modrow[0:1, b, o:o + 1024], 1.0, None, ALU.add, None, ["modrow"], ["modrow"])
                h.dma(modd[b:b + 1, :], modrow[0:1, b, :], ["modrow"], ["modd"])
            end_phase()
        if stop_after == "0":
            return nc, dbg_out, S

        def bcast_load(dst, src_row, w, r=()):
            return h.dma(dst, src_row.partition_broadcast(128), list(r), w)

        with contextlib.ExitStack() as es:
            n1T = sbt(es, "n1T", [128, 8, NT], BF16)
            with contextlib.ExitStack() as es1:
                G1 = [sbt(es1, "G1_%d" % b, [128, D]) for b in range(NB)]
                SH1 = [sbt(es1, "SH1_%d" % b, [128, D]) for b in range(NB)]
                tmpg = sbt(es1, "tmpg", [128, D])
                xt = [sbt(es1, "xt%d" % i, [128, D]) for i in range(2)]
                junk = sbt(es1, "junk", [128, D]); n1f = sbt(es1, "n1f", [128, D])
                n1b = [sbt(es1, "n1b%d" % i, [128, D], BF16) for i in range(2)]
                ss = sbt(es1, "ss", [128, NTILE]); rs = sbt(es1, "rs", [128, NTILE])
                ptr = [pst(es1, "ptr%d" % i, [128, 8, 128], BF16) for i in range(2)]
                bcast_load(tmpg[:], norm1_g[0:1, :], ["tmpg"])
                for b in range(NB):
                    bcast_load(G1[b][:], modd[b:b + 1, 1024:2048], ["G1_%d" % b], ["modd"])
                    bcast_load(SH1[b][:], modd[b:b + 1, 0:1024], ["SH1_%d" % b], ["modd"])
                    h.tt("dve", G1[b][:], G1[b][:], tmpg[:], ALU.mult, ["G1_%d" % b, "tmpg"], ["G1_%d" % b])
                for i in range(NTILE):
                    b = i // 16; j = i % 2
                    h.dma(xt[j][:], x[i * 128:(i + 1) * 128, :], [], ["xt%d" % j])
                    h.act(junk[:], xt[j][:], AF.Square, ["xt%d" % j], ["junk", "ss%d" % i], accum=ss[:, i:i + 1])
                    h.ts("dve", rs[:, i:i + 1], ss[:, i:i + 1], 1.0 / D, EPS, ALU.mult, ALU.add, ["ss%d" % i], ["rs%d" % i])
                    h.act(rs[:, i:i + 1], rs[:, i:i + 1], AF.Sqrt, ["rs%d" % i], ["rs%d" % i])
                    h.recip(rs[:, i:i + 1], rs[:, i:i + 1], ["rs%d" % i], ["rs%d" % i])
                    h.stt(n1f[:], xt[j][:], rs[:, i:i + 1], G1[b][:], ALU.mult, ALU.mult, ["xt%d" % j, "rs%d" % i, "G1_%d" % b], ["n1f"])
                    h.tt("pool", n1b[j][:], n1f[:], SH1[b][:], ALU.add, ["n1f", "SH1_%d" % b], ["n1b%d" % j])
                    for kc in range(8):
                        h.tr(ptr[j][:, kc, :], n1b[j][:, kc * 128:(kc + 1) * 128], idb[:], ["n1b%d" % j, "idb"], ["ptr%d" % j])
                    h.copy("act", n1T[:, :, i * 128:(i + 1) * 128], ptr[j][:], ["ptr%d" % j], ["n1T%d" % i])
                end_phase()
            with contextlib.ExitStack() as es2:
                wf = [sbt(es2, "wf%d" % i, [128, 8, 512]) for i in range(2)]
                wbf = [sbt(es2, "wbf%d" % i, [128, 8, 512], BF16) for i in range(2)]
                zo = [sbt(es2, "zo%d" % i, [128, 512]) for i in range(4)]
                pz = [pst(es2, "pz%d" % i, [128, 512]) for i in range(4)]
                cnt = 0
                for nb in range(15):
                    j = nb % 2
                    h.dma(wf[j][:], w_in[:, nb * 512:(nb + 1) * 512].rearrange("(kc p) n -> p kc n", p=128), [], ["wf%d" % j])
                    h.copy("pool", wbf[j][:], wf[j][:], ["wf%d" % j], ["wbf%d" % j])
                    for i in range(NTILE):
                        q = cnt % 4; cnt += 1
                        for kc in range(8):
                            h.mm(pz[q][:], n1T[:, kc, i * 128:(i + 1) * 128], wbf[j][:, kc, :], kc == 0, kc == 7,
                                 ["wbf%d" % j], ["pz%d" % q])
                        h.copy("act" if q % 2 == 0 else "dve", zo[q][:], pz[q][:], ["pz%d" % q], ["zo%d" % q])
                        h.dma(zs[i * 128:(i + 1) * 128, nb * 512:(nb + 1) * 512], zo[q][:], ["zo%d" % q], ["zs"])
                end_phase()
        if stop_after == "A":
            return nc, dbg_out, S

        for b in range(NB):
            tb = b * SEQ
            with contextlib.ExitStack() as es:
                qT = sbt(es, "qT", [128, 6, SEQ], BF16); kT = sbt(es, "kT", [128, 6, SEQ], BF16)
                VA = [sbt(es, "VA%d" % g, [128, 16, 4, 65], BF16) for g in range(3)]
                ACC = sbt(es, "ACC", [65, 4, SEQ]); RD = sbt(es, "RD", [65, SEQ])
                onesf = sbt(es, "onesf", [65, 64])
                mcur = sbt(es, "mcur", [128, 128], BF16); mprev = sbt(es, "mprev", [128, 128], BF16)
                mtmp = sbt(es, "mtmp", [128, 128])
                QG = sbt(es, "QG", [128, 12, 64]); KG = sbt(es, "KG", [128, 12, 64])
                cs = sbt(es, "cs", [128, 16, 16])
                h.copy("dve", onesf[:, :], idf[0:65, 64:65].to_broadcast([65, 64]), ["idf"], ["onesf"])
                h.dma(mtmp[:], mcur_d, [], ["mtmp"]); h.copy("dve", mcur[:], mtmp[:], ["mtmp"], ["mcur"])
                h.dma(mtmp[:], mprev_d, ["mtmp"], ["mtmp"]); h.copy("dve", mprev[:], mtmp[:], ["mtmp"], ["mprev"])
                h.dma(cs[:], cs_d, [], ["cs"])
                for gh in range(12):
                    bcast_load(QG[:, gh, :], q_norm_g[gh // 4:gh // 4 + 1, :], ["QG"])
                    bcast_load(KG[:, gh, :], k_norm_g[gh // 4:gh // 4 + 1, :], ["KG"])
                for g in range(3):
                    h.memset("pool", VA[g][:, :, :, 64:65], 1.0, ["VA%d" % g])
                with contextlib.ExitStack() as es1:
                    zq = [sbt(es1, "zq%d" % i, [128, 768]) for i in range(2)]
                    zk = [sbt(es1, "zk%d" % i, [128, 768]) for i in range(2)]
                    sq = sbt(es1, "sq", [128, 768]); st = sbt(es1, "st", [128, 12]); qn = sbt(es1, "qn", [128, 768])
                    r1 = sbt(es1, "r1", [128, 12, 8]); r2 = sbt(es1, "r2", [128, 12, 8])
                    qb = [sbt(es1, "qb%d" % i, [128, 768], BF16) for i in range(2)]
                    ptq = [pst(es1, "ptq%d" % i, [128, 6, 128], BF16) for i in range(2)]
                    vtmp = [sbt(es1, "vtmp%d" % i, [128, 256]) for i in range(2)]
                    cntp = 0
                    for i in range(16):
                        cosb = bc3(cs[:, i, 0:8], 12); sinb = bc3(cs[:, i, 8:16], 12)
                        for which, zt, dstT, GG, c0, scl in (("q", zq, qT, QG, 3328, 0.125), ("k", zk, kT, KG, 3328 + 768, 1.0)):
                            j = cntp % 2; cntp += 1
                            zn = "z%s%d" % (which, i % 2)
                            zz = zt[i % 2]
                            h.dma(zz[:], zs[tb + i * 128:tb + (i + 1) * 128, c0:c0 + 768], ["zs"], [zn])
                            h.tt("pool", sq[:], zz[:], zz[:], ALU.mult, [zn], ["sq"])
                            h.red(st[:], sq[:].rearrange("p (a b) -> p a b", a=12), ALU.add, ["sq"], ["st"])
                            h.ts("dve", st[:], st[:], 1.0 / 64, EPS, ALU.mult, ALU.add, ["st"], ["st"])
                            h.act(st[:], st[:], AF.Sqrt, ["st"], ["st"])
                            h.recip(st[:], st[:], ["st"], ["st"])
                            if scl != 1.0:
                                h.ts("dve", st[:], st[:], scl, None, ALU.mult, None, ["st"], ["st"])
                            z3 = zz[:].rearrange("p (a b) -> p a b", a=12)
                            q3 = qn[:].rearrange("p (a b) -> p a b", a=12)
                            h.tt("dve", q3, z3, bcl(st[:], 64), ALU.mult, [zn, "st"], ["qn"])
                            h.tt("pool", q3, q3, GG[:], ALU.mult, ["qn", "QG", "KG"], ["qn"])
                            qb3 = qb[j][:].rearrange("p (a b) -> p a b", a=12)
                            h.tt("dve", r1[:], q3[:, :, 0:8], cosb, ALU.mult, ["qn", "cs"], ["r1"])
                            h.tt("dve", r2[:], q3[:, :, 8:16], sinb, ALU.mult, ["qn", "cs"], ["r2"])
                            h.tt("dve", qb3[:, :, 0:8], r1[:], r2[:], ALU.subtract, ["r1", "r2"], ["qb%d" % j])
                            h.tt("dve", r1[:], q3[:, :, 8:16], cosb, ALU.mult, ["qn", "cs", "qb%d" % j], ["r1"])
                            h.tt("dve", r2[:], q3[:, :, 0:8], sinb, ALU.mult, ["qn", "cs", "qb%d" % j], ["r2"])
                            h.tt("dve", qb3[:, :, 8:16], r1[:], r2[:], ALU.add, ["r1", "r2"], ["qb%d" % j])
                            h.copy("pool", qb3[:, :, 16:64], q3[:, :, 16:64], ["qn"], ["qb%d" % j])
                            for c6 in range(6):
                                h.tr(ptq[j][:, c6, :], qb[j][:, c6 * 128:(c6 + 1) * 128], idb[:], ["qb%d" % j, "idb"], ["ptq%d" % j])
                            h.copy("act", dstT[:, :, i * 128:(i + 1) * 128], ptq[j][:], ["ptq%d" % j], [which + "T"])
                    cv = 0
                    for g, dil in enumerate((1, 4, 16)):
                        nmt = SEQ // dil // 128
                        for r_ in range(dil):
                            for mt in range(nmt):
                                kt = r_ * nmt + mt
                                j = cv % 2; cv += 1
                                st0 = tb + r_ + dil * 128 * mt
                                src = zs[st0:st0 + dil * 127 + 1:dil, 3328 + 1536 + g * 256:3328 + 1536 + (g + 1) * 256]
                                h.dma(vtmp[j][:], src, ["zs"], ["vtmp%d" % j])
                                h.copy("pool", VA[g][:, kt, :, 0:64], vtmp[j][:].rearrange("p (a b) -> p a b", a=4),
                                       ["vtmp%d" % j], ["VA%d" % g])
                    S.barrier()
                import os
                EP = os.environ.get("EPART", "")
                if EP == "prep":
                    end_phase()
                    continue
                with contextlib.ExitStack() as es1:
                    pss = [pst(es1, "pss%d" % i, [128, 512]) for i in range(2)]
                    pso = [pst(es1, "pso%d" % i, [128, 512]) for i in range(2)]
                    psb = pst(es1, "psb", [128, 512])
                    ee = [sbt(es1, "ee%d" % i, [128, 128], BF16) for i in range(3)]
                    pp = [sbt(es1, "pp%d" % i, [128, 128], BF16) for i in range(3)]
                    ce = 0; co = 0
                    for g, dil in enumerate((1, 4, 16)):
                        nmt = SEQ // dil // 128
                        if EP == "g0" and g > 0:
                            continue
                        for hh in range(4):
                            gh = g * 4 + hh; ch = gh // 2; base = (gh % 2) * 64
                            for r_ in range(dil):
                                for mt in range(nmt):
                                    cq = slice(r_ + dil * 128 * mt, r_ + dil * 128 * mt + dil * 127 + 1, dil)
                                    jo = co % 2; co += 1
                                    kts = ([(mt - 1, mprev)] if mt > 0 else []) + [(mt, mcur)]
                                    for n_, (kmt, msk) in enumerate(kts):
                                        ck = slice(r_ + dil * 128 * kmt, r_ + dil * 128 * kmt + dil * 127 + 1, dil)
                                        js = ce % 2; je = ce % 3; ce += 1
                                        h.mm(pss[js][:, 0:128], kT[base:base + 64, ch, ck], qT[base:base + 64, ch, cq], True, True,
                                             ["qT", "kT"], ["pss%d" % js])
                                        h.act(ee[je][:], pss[js][:, 0:128], AF.Exp, ["pss%d" % js], ["ee%d" % je])
                                        h.tt("dve" if ce % 2 else "pool", pp[je][:], ee[je][:], msk[:], ALU.mult, ["ee%d" % je, "mcur", "mprev"], ["pp%d" % je])
                                        h.mm(pso[jo][0:65, 0:128], VA[g][:, r_ * nmt + kmt, hh, :], pp[je][:], n_ == 0, n_ == len(kts) - 1,
                                             ["pp%d" % je, "VA%d" % g], ["pso%d" % jo])
                                    if g == 0:
                                        h.copy("dve", ACC[:, hh, cq], pso[jo][0:65, 0:128], ["pso%d" % jo], ["ACC%d" % hh])
                                    else:
                                        h.tt("dve", ACC[:, hh, cq], ACC[:, hh, cq], pso[jo][0:65, 0:128], ALU.add, ["pso%d" % jo, "ACC%d" % hh], ["ACC%d" % hh])
                    for hh in range(4):
                        if EP in ("g0", "nofinal"):
                            continue
                        h.ts("dve", RD[:, :], ACC[:, hh, :], 1e-30, None, ALU.max, None, ["ACC%d" % hh], ["RD"])
                        h.recip(RD[:, :], RD[:, :], ["RD"], ["RD"])
                        for cb in range(4):
                            cs_ = slice(cb * 512, (cb + 1) * 512)
                            h.mm(psb[0:64, :], onesf[:, :], RD[:, cs_], True, True, ["RD", "onesf"], ["psb"])
                            h.tt("dve", yaT[:, hh, tb + cb * 512:tb + (cb + 1) * 512], ACC[0:64, hh, cs_], psb[0:64, :], ALU.mult,
                                 ["psb", "ACC%d" % hh], ["yaT"])
                    end_phase()
        if "yaT" in dbg_out:
            with contextlib.ExitStack() as es:
                tmpd = [sbt(es, "tmpd%d" % i, [64, 2048]) for i in range(2)]
                ci = 0
                for hh in range(4):
                    for cb in range(NT // 2048):
                        j = ci % 2; ci += 1
                        h.copy("dve", tmpd[j][:], yaT[:, hh, cb * 2048:(cb + 1) * 2048], ["yaT"], ["tmpd%d" % j])
                        h.dma(dbg_out["yaT"][:, hh, cb * 2048:(cb + 1) * 2048], tmpd[j][:], ["tmpd%d" % j], ["dbg"])
                end_phase()
        if stop_after == "E":
            return nc, dbg_out, S

        with contextlib.ExitStack() as es:
            MU = sbt(es, "MU", [128, 3328])
            W0b = sbt(es, "W0b", [128, D]); A0b = sbt(es, "A0b", [128, D]); KKb = sbt(es, "KKb", [128, D]); KAb = sbt(es, "KAb", [128, D])
            W2A2 = sbt(es, "W2A2", [128, D]); G2t = sbt(es, "G2t", [128, D])
            zc = sbt(es, "zc", [128, 3328]); zp = sbt(es, "zp", [128, 3328])
            L = sbt(es, "L", [128, 256]); LT = sbt(es, "LT", [128, 2, 128])
            wt = sbt(es, "wt", [128, D]); at = sbt(es, "at", [128, D]); kk = sbt(es, "kk", [128, D]); sq = sbt(es, "sqb", [128, D])
            av = sbt(es, "av", [128, D]); bv = sbt(es, "bv", [128, D]); k2 = sbt(es, "k2", [128, D]); gt = sbt(es, "gt", [128, D])
            s16 = sbt(es, "s16", [128, 16])
            pl = pst(es, "pl", [128, 2, 128]); plw = pst(es, "plw", [128, D]); pla = pst(es, "pla", [128, D]); plg = pst(es, "plg", [128, D])
            bcast_load(MU[:], rwkv_mu[0:1, :], ["MU"])
            for t_, src in ((W0b, w0), (A0b, a0), (KKb, k_k), (KAb, k_a)):
                bcast_load(t_[:], src[0:1, :], ["cB"])
            h.dma(W2A2[0:64, :], w2, [], ["cB"]); h.dma(W2A2[64:128, :], a2, [], ["cB"]); h.dma(G2t[:], g2, [], ["cB"])
            for i in range(NTILE):
                T0 = i * 128
                h.dma(zc[:], zs[T0:T0 + 128, 0:3328], [], ["zc"])
                if i % 16 == 0:
                    h.memset("pool", zp[0:1, :], 0.0, ["zp"])
                    h.dma(zp[1:128, :], zs[T0:T0 + 127, 0:3328], ["zp"], ["zp"])
                else:
                    h.dma(zp[:], zs[T0 - 1:T0 + 127, 0:3328], [], ["zp"])
                h.tt("dve", zp[:], zp[:], zc[:], ALU.subtract, ["zp", "zc"], ["zp"])
                h.tt("pool", zp[:], zp[:], MU[:], ALU.mult, ["zp", "MU"], ["zp"])
                h.tt("dve", zp[:], zp[:], zc[:], ALU.add, ["zp", "zc"], ["zp"])
                h.act(L[:, 0:64], zp[:, 3072:3136], AF.Tanh, ["zp"], ["L"])
                h.copy("pool", L[:, 64:128], zp[:, 3136:3200], ["zp"], ["L"])
                h.act(L[:, 128:256], zp[:, 3200:3328], AF.Sigmoid, ["zp"], ["L"])
                h.tr(pl[:, 0, :], L[:, 0:128], idf[:], ["L", "idf"], ["pl"])
                h.tr(pl[:, 1, :], L[:, 128:256], idf[:], ["L", "idf"], ["pl"])
                h.copy("act", LT[:], pl[:], ["pl"], ["LT"])
                for hb in range(2):
                    cs_ = slice(hb * 512, (hb + 1) * 512)
                    h.mm(plw[:, cs_], LT[0:64, 0, :], W2A2[0:64, cs_], True, True, ["LT", "cB"], ["plw"])
                    h.mm(pla[:, cs_], LT[64:128, 0, :], W2A2[64:128, cs_], True, True, ["LT", "cB"], ["pla"])
                    h.mm(plg[:, cs_], LT[:, 1, :], G2t[:, cs_], True, True, ["LT", "cB"], ["plg"])
                h.tt("dve", wt[:], plw[:], W0b[:], ALU.add, ["plw", "cB"], ["wt"])
                h.act(wt[:], wt[:], AF.Sigmoid, ["wt"], ["wt"])
                h.ts("pool", wt[:], wt[:], -float(np.exp(-0.5)), None, ALU.mult, None, ["wt"], ["wt"])
                h.tt("dve", at[:], pla[:], A0b[:], ALU.add, ["pla", "cB"], ["at"])
                h.act(at[:], at[:], AF.Sigmoid, ["at"], ["at"])
                h.copy("act", gt[:], plg[:], ["plg"], ["gt"])
                h.tt("pool", kk[:], zp[:, 1024:2048], KKb[:], ALU.mult, ["zp", "cB"], ["kk"])
                h.tt("pool", sq[:], kk[:], kk[:], ALU.mult, ["kk"], ["sqb"])
                h.red(s16[:], sq[:].rearrange("p (a b) -> p a b", a=16), ALU.add, ["sqb"], ["s16"])
                h.act(s16[:], s16[:], AF.Sqrt, ["s16"], ["s16"])
                h.ts("dve", s16[:], s16[:], 1e-12, None, ALU.max, None, ["s16"], ["s16"])
                h.recip(s16[:], s16[:], ["s16"], ["s16"])
                kk3 = kk[:].rearrange("p (a b) -> p a b", a=16)
                h.tt("dve", kk3, kk3, bcl(s16[:], 64), ALU.mult, ["kk", "s16"], ["kk"])
                h.ts("pool", av[:], kk[:], -1.0, None, ALU.mult, None, ["kk"], ["av"])
                h.tt("pool", bv[:], kk[:], at[:], ALU.mult, ["kk", "at"], ["bv"])
                h.stt(k2[:], at[:], -1.0, KAb[:], ALU.add, ALU.mult, ["at", "cB"], ["k2"])
                h.stt(k2[:], k2[:], 1.0, zp[:, 1024:2048], ALU.add, ALU.mult, ["k2", "zp"], ["k2"])
                rows = slice(T0, T0 + 128)
                h.dma(sc["r"][rows, :], zp[:, 0:1024], ["zp"], ["sc_r"])
                h.dma(sc["v"][rows, :], zp[:, 2048:3072], ["zp"], ["sc_v"])
                h.dma(sc["w"][rows, :], wt[:], ["wt"], ["sc_w"])
                h.dma(sc["k"][rows, :], k2[:], ["k2"], ["sc_k"])
                h.dma(sc["a"][rows, :], av[:], ["av"], ["sc_a"])
                h.dma(sc["b"][rows, :], bv[:], ["bv"], ["sc_b"])
                h.dma(sc["g"][rows, :], gt[:], ["gt"], ["sc_g"])
            end_phase()

        with contextlib.ExitStack() as es:
            ltri = sbt(es, "ltri", [128, 128]); lblk = sbt(es, "lblk", [128, 128])
            m256 = sbt(es, "m256", [128, 256], BF16); mL = sbt(es, "mL", [128, 128], BF16); mtmp2 = sbt(es, "mtmp2", [128, 256])
            h.dma(ltri[:], ltri_d, [], ["cC"]); h.dma(lblk[:], lblk_d, [], ["cC"])
            h.dma(mtmp2[:], m256_d, [], ["mtmp2"]); h.copy("dve", m256[:], mtmp2[:], ["mtmp2"], ["cC"])
            h.dma(mtmp2[:, 0:128], mL_d, ["mtmp2"], ["mtmp2"]); h.copy("dve", mL[:], mtmp2[:, 0:128], ["mtmp2"], ["cC"])
            ld = {n: [sbt(es, "l%s%d" % (n, i), [128, D]) for i in range(2)] for n in ("r", "w", "k", "v", "a", "b")}
            ep = sbt(es, "ep", [128, D]); em = sbt(es, "em", [128, D]); eA = sbt(es, "eA", [128, D]); eE = sbt(es, "eE", [128, D])
            cumS = sbt(es, "cumS", [128, D]); gE = sbt(es, "gE", [128, D])
            Rb, Ab, Kb, Bb, Kh, Bh, Vb = (sbt(es, n, [128, D], BF16) for n in ("Rb", "Ab", "Kb", "Bb", "Kh", "Bh", "Vb"))
            ART = sbt(es, "ART", [128, 8, 2, 128], BF16); KT = sbt(es, "KTc", [128, 8, 128], BF16); BT = sbt(es, "BTc", [128, 8, 128], BF16)
            Gf = sbt(es, "Gf", [128, 8, 2])
            STf = sbt(es, "STf", [128, 8, 64]); STb = sbt(es, "STb", [128, 8, 64], BF16)
            ytile = [sbt(es, "ytile%d" % i, [128, D]) for i in range(2)]
            NG = 4
            hb_ = lambda n, shape: [sbt(es, "%s_%d" % (n, q), shape, BF16) for q in range(NG)]
            X1 = hb_("X1", [128, 256]); X2 = hb_("X2", [128, 256]); Q0 = hb_("Q0", [128, 128]); Q1 = hb_("Q1", [128, 128])
            P0 = hb_("P0", [128, 128]); P1 = hb_("P1", [128, 128]); QI = hb_("QI", [128, 128]); TT = hb_("TT", [128, 128])
            AVs = hb_("AVs", [128, 64]); Wt = hb_("Wt", [128, 64]); U0b = hb_("U0b", [128, 64]); R2T = hb_("R2T", [128, 128])
            MTc = hb_("MTc", [128, 2, 64])
            Y0s = [sbt(es, "Y0s_%d" % q, [128, 64]) for q in range(NG)]; Z0c = [sbt(es, "Z0c_%d" % q, [128, 2, 64]) for q in range(NG)]
            PB = [pst(es, "PB%d" % q, [128, 512]) for q in range(7)]
            PT = pst(es, "PTc", [128, 8, 128], BF16)

            def head_gen(hd, hq, yt):
                hb = hd // 2; base = (hd % 2) * 64; hs = slice(base, base + 64); cols = slice(hd * 64, (hd + 1) * 64)
                B = PB[hq]; bk = "PB%d" % hq; q_ = "_%d" % hq
                cp = "act" if hq % 2 == 0 else "dve"
                art2 = ART[hs, hb, :, :].rearrange("p a b -> p (a b)")
                h.mm(B[:, 0:256], BT[hs, hb, :], art2, True, True, ["ART", "BTc"], [bk])
                h.tt("dve", X1[hq][:], B[:, 0:256], m256[:], ALU.mult, ["cC"], [bk, "X1" + q_])
                yield
                h.mm(B[:, 0:256], KT[hs, hb, :], art2, True, True, ["ART", "KTc"], [bk])
                h.tt("dve", X2[hq][:], B[:, 0:256], m256[:], ALU.mult, ["cC"], [bk, "X2" + q_])
                yield
                h.mm(B[:, 0:128], ART[hs, hb, 0, :], BT[hs, hb, :], True, True, ["ART", "BTc"], [bk])
                h.tt("dve", Q0[hq][:], B[:, 0:128], mL[:], ALU.mult, ["cC"], [bk, "Q0" + q_])
                h.tt("pool", TT[hq][:], X1[hq][:, 0:128], idb[:], ALU.add, ["X1" + q_, "idb"], ["TT" + q_])
                yield
                P_, Pk = X1[hq][:, 0:128], "X1" + q_
                Q_, Qk = Q0[hq][:], "Q0" + q_
                Qbuf = [Q0[hq], Q1[hq]]; Pbuf = [P0[hq], P1[hq]]
                for k in range(1, 6):
                    Qn, Qnk = Qbuf[k % 2][:], "Q%d%s" % (k % 2, q_)
                    h.mm(B[:, 0:128], P_, Q_, True, True, [Pk, Qk], [bk])
                    h.copy(cp, Qn, B[:, 0:128], [], [bk, Qnk])
                    if k < 5:
                        Pn, Pnk = Pbuf[k % 2][:], "P%d%s" % (k % 2, q_)
                        h.mm(B[:, 128:256], Q_, P_, True, True, [Pk, Qk], [bk])
                        h.copy(cp, Pn, B[:, 128:256], [], [bk, Pnk])
                    yield
                    h.tt("pool", QI[hq][:], Qn, idb[:], ALU.add, [Qnk, "idb"], ["QI" + q_])
                    h.mm(B[:, 256:384], QI[hq][:], TT[hq][:], True, True, ["QI" + q_, "TT" + q_], [bk])
                    h.copy(cp, TT[hq][:], B[:, 256:384], [], [bk, "TT" + q_])
                    if k < 5:
                        P_, Pk = Pn, Pnk
                    Q_, Qk = Qn, Qnk
                    yield
                h.mm(B[:, 0:64], X2[hq][:, 0:128], Vb[:, cols], True, True, ["X2" + q_, "Vb"], [bk])
                h.copy(cp, AVs[hq][:], B[:, 0:64], [], [bk, "AVs" + q_])
                h.mm(B[:, 64:128], TT[hq][:], Ab[:, cols], True, True, ["TT" + q_, "Ab"], [bk])
                h.copy(cp, Wt[hq][:], B[:, 64:128], [], [bk, "Wt" + q_])
                yield
                h.mm(B[:, 0:64], TT[hq][:], AVs[hq][:], True, True, ["TT" + q_, "AVs" + q_], [bk])
                h.copy(cp, U0b[hq][:], B[:, 0:64], [], [bk, "U0b" + q_])
                h.mm(B[hs, 128:256], Wt[hq][:], X1[hq][:, 128:256], True, True, ["Wt" + q_, "X1" + q_], [bk])
                h.tt("dve", R2T[hq][hs, :], B[hs, 128:256], ART[hs, hb, 1, :], ALU.add, ["ART"], [bk, "R2T" + q_])
                yield
                h.mm(B[:, 0:64], X1[hq][:, 128:256], U0b[hq][:], True, False, ["X1" + q_, "U0b" + q_], [bk])
                h.mm(B[:, 0:64], X2[hq][:, 128:256], Vb[:, cols], False, True, ["X2" + q_, "Vb"], [bk])
                h.copy(cp, Y0s[hq][:], B[:, 0:64], [], [bk, "Y0s" + q_])
                for cidx in range(2):
                    c0_ = cidx * 64; cs = slice(c0_, c0_ + 64)
                    h.mm(B[hs, 64 + cidx * 64:128 + cidx * 64], Wt[hq][cs, :], Bh[cs, cols], True, True, ["Wt" + q_, "Bh"], [bk])
                    h.copy(cp, MTc[hq][hs, cidx, :], B[hs, 64 + cidx * 64:128 + cidx * 64], [], [bk, "MTc" + q_])
                yield
                for cidx in range(2):
                    c0_ = cidx * 64; cs = slice(c0_, c0_ + 64)
                    h.mm(B[hs, 256 + cidx * 64:320 + cidx * 64], Bh[cs, cols], U0b[hq][cs, :], True, False, ["Bh", "U0b" + q_], [bk])
                    h.mm(B[hs, 256 + cidx * 64:320 + cidx * 64], Kh[cs, cols], Vb[cs, cols], False, True, ["Kh", "Vb"], [bk])
                    h.copy(cp, Z0c[hq][hs, cidx, :], B[hs, 256 + cidx * 64:320 + cidx * 64], [], [bk, "Z0c" + q_])
                yield
                sk = "ST%d" % hd
                for cidx in range(2):
                    c0_ = cidx * 64; cs = slice(c0_, c0_ + 64)
                    h.mm(B[cs, 0:64], R2T[hq][hs, cs], STb[hs, hb, :], True, True, ["R2T" + q_, sk + "b"], [bk])
                    h.tt("dve", yt[cs, cols], B[cs, 0:64], Y0s[hq][cs, :], ALU.add, ["Y0s" + q_], [bk, "ytile"])
                    h.mm(B[hs, 64:128], MTc[hq][hs, cidx, :], STb[hs, hb, :], True, True, ["MTc" + q_, sk + "b"], [bk])
                    h.stt(STf[hs, hb, :], STf[hs, hb, :], Gf[hs, hb, cidx:cidx + 1], B[hs, 64:128], ALU.mult, ALU.add, [sk + "f", "Gf"], [bk, sk + "f"])
                    h.tt("pool", STf[hs, hb, :], STf[hs, hb, :], Z0c[hq][hs, cidx, :], ALU.add, [sk + "f", "Z0c" + q_], [sk + "f"])
                    h.copy("pool", STb[hs, hb, :], STf[hs, hb, :], [sk + "f"], [sk + "b"])
                    yield

            ntile_c = NTILE if stop_after != "Cshort" else 2
            for i in range(ntile_c):
                j = i % 2; rows = slice(i * 128, (i + 1) * 128)
                if i % 16 == 0:
                    h.memset("dve", STf[:], 0.0, ["ST%df" % q for q in range(16)])
                    h.memset("pool", STb[:], 0.0, ["ST%db" % q for q in range(16)])
                for n in ("r", "w", "k", "v", "a", "b"):
                    h.dma(ld[n][j][:], sc[n][rows, :], [], ["l%s%d" % (n, j)])
                lr, lw_, lk, lv, la, lb = (ld[n][j] for n in ("r", "w", "k", "v", "a", "b"))
                for hf in range(2):
                    cs_ = slice(hf * 512, (hf + 1) * 512)
                    h.mm(PB[hf][:, :], ltri[:], lw_[:, cs_], True, True, ["cC", "lw%d" % j], ["PB%d" % hf])
                    h.mm(PB[2 + hf][:, :], lblk[:], lw_[:, cs_], True, True, ["cC", "lw%d" % j], ["PB%d" % (2 + hf)])
                for hf in range(2):
                    cs_ = slice(hf * 512, (hf + 1) * 512); bk = "PB%d" % hf; bk2 = "PB%d" % (2 + hf)
                    h.act(ep[:, cs_], PB[hf][:, :], AF.Exp, [], [bk, "ep"])
                    h.act(em[:, cs_], PB[hf][:, :], AF.Exp, [], [bk, "em"], scale=-1.0)
                    h.copy("act", cumS[:, cs_], PB[hf][:, :], [], [bk, "cumS"])
                    h.act(gE[:, cs_], PB[2 + hf][:, :], AF.Exp, [], [bk2, "gE"])
                    h.tt("dve", eE[:, cs_], PB[2 + hf][:, :], cumS[:, cs_], ALU.subtract, ["cumS"], [bk2, "eE"])
                h.tt("dve", eA[:], cumS[:], lw_[:], ALU.subtract, ["cumS", "lw%d" % j], ["eA"])
                h.act(eA[:], eA[:], AF.Exp, ["eA"], ["eA"])
                h.act(eE[:], eE[:], AF.Exp, ["eE"], ["eE"])
                h.tt("dve", Rb[:], lr[:], ep[:], ALU.mult, ["lr%d" % j, "ep"], ["Rb"])
                h.tt("pool", Ab[:], la[:], eA[:], ALU.mult, ["la%d" % j, "eA"], ["Ab"])
                h.tt("dve", Kb[:], lk[:], em[:], ALU.mult, ["lk%d" % j, "em"], ["Kb"])
                h.tt("pool", Bb[:], lb[:], em[:], ALU.mult, ["lb%d" % j, "em"], ["Bb"])
                h.tt("dve", Kh[:], lk[:], eE[:], ALU.mult, ["lk%d" % j, "eE"], ["Kh"])
                h.tt("pool", Bh[:], lb[:], eE[:], ALU.mult, ["lb%d" % j, "eE"], ["Bh"])
                h.copy("pool", Vb[:], lv[:], ["lv%d" % j], ["Vb"])
                for hb in range(8):
                    h.mm(PB[4][:, hb * 2:hb * 2 + 2], gE[:, hb * 128:(hb + 1) * 128], idf[:, 0:128:64], True, True, ["gE", "idf"], ["PB4"])
                h.copy("dve", Gf[:].rearrange("p a b -> p (a b)"), PB[4][:, 0:16], [], ["PB4", "Gf"])
                for src, sk_, dst, dk in ((Ab, "Ab", ART[:, :, 0, :], "ART"), (Rb, "Rb", ART[:, :, 1, :], "ART"), (Kb, "Kb", KT[:], "KTc"), (Bb, "Bb", BT[:], "BTc")):
                    for kc in range(8):
                        h.tr(PT[:, kc, :], src[:, kc * 128:(kc + 1) * 128], idb[:], [sk_, "idb"], ["PTc"])
                    h.copy("act", dst, PT[:], [], ["PTc", dk])
                for g0 in range(0, 16, NG):
                    gens = [head_gen(g0 + q, q, ytile[j]) for q in range(NG)]
                    alive = list(gens)
                    while alive:
                        nxt = []
                        for g_ in alive:
                            try:
                                next(g_)
                                nxt.append(g_)
                            except StopIteration:
                                pass
                        alive = nxt
                h.dma(sc["y"][rows, :], ytile[j][:], ["ytile"], ["sc_y"])
            end_phase()

        with contextlib.ExitStack() as es:
            LNG = sbt(es, "LNG", [128, D]); LNB = sbt(es, "LNB", [128, D]); RKb = sbt(es, "RKb", [128, D])
            ld = {n: [sbt(es, "d%s%d" % (n, i), [128, D]) for i in range(2)] for n in ("y", "r", "k", "v", "g")}
            yc = sbt(es, "yc", [128, D]); sq2 = sbt(es, "sq2", [128, D]); prod = sbt(es, "prod", [128, D])
            m16 = sbt(es, "m16", [128, 16]); v16 = sbt(es, "v16", [128, 16]); b16 = sbt(es, "b16", [128, 16])
            yb = [sbt(es, "yb%d" % i, [128, D], BF16) for i in range(2)]
            yf = sbt(es, "yf", [128, D])
            bcast_load(LNG[:], lnx_g[0:1, :], ["cD"]); bcast_load(LNB[:], lnx_b[0:1, :], ["cD"]); bcast_load(RKb[:], r_k[0:1, :], ["cD"])
            v3 = lambda t_: t_[:].rearrange("p (a b) -> p a b", a=16)
            for i in range(NTILE):
                j = i % 2; rows = slice(i * 128, (i + 1) * 128)
                for n in ("y", "r", "k", "v", "g"):
                    h.dma(ld[n][j][:], sc[n][rows, :], [], ["d%s%d" % (n, j)])
                yt, rt, kt, vt, gt_ = (ld[n][j] for n in ("y", "r", "k", "v", "g"))
                h.red(m16[:], v3(yt), ALU.add, ["dy%d" % j], ["m16"])
                h.ts("dve", m16[:], m16[:], 1.0 / 64, None, ALU.mult, None, ["m16"], ["m16"])
                h.tt("dve", v3(yc), v3(yt), bcl(m16[:], 64), ALU.subtract, ["dy%d" % j, "m16"], ["yc"])
                h.tt("pool", sq2[:], yc[:], yc[:], ALU.mult, ["yc"], ["sq2"])
                h.red(v16[:], v3(sq2), ALU.add, ["sq2"], ["v16"])
                h.ts("dve", v16[:], v16[:], 1.0 / 64, GN_EPS, ALU.mult, ALU.add, ["v16"], ["v16"])
                h.act(v16[:], v16[:], AF.Sqrt, ["v16"], ["v16"])
                h.recip(v16[:], v16[:], ["v16"], ["v16"])
                h.tt("dve", v3(yc), v3(yc), bcl(v16[:], 64), ALU.mult, ["yc", "v16"], ["yc"])
                h.tt("pool", yc[:], yc[:], LNG[:], ALU.mult, ["yc", "cD"], ["yc"])
                h.tt("pool", yc[:], yc[:], LNB[:], ALU.add, ["yc", "cD"], ["yc"])
                h.tt("pool", prod[:], rt[:], kt[:], ALU.mult, ["dr%d" % j, "dk%d" % j], ["prod"])
                h.tt("pool", prod[:], prod[:], RKb[:], ALU.mult, ["prod", "cD"], ["prod"])
                h.red(b16[:], v3(prod), ALU.add, ["prod"], ["b16"])
                h.tt("dve", v3(prod), v3(vt), bcl(b16[:], 64), ALU.mult, ["dv%d" % j, "b16", "prod"], ["prod"])
                h.tt("dve", yc[:], yc[:], prod[:], ALU.add, ["yc", "prod"], ["yc"])
                h.tt("dve", yb[j][:], yc[:], gt_[:], ALU.mult, ["yc", "dg%d" % j], ["yb%d" % j])
                h.dma(yrs[rows, :], yb[j][:], ["yb%d" % j], ["yrs"])
                if "yr" in dbg_out:
                    h.tt("pool", yf[:], yc[:], gt_[:], ALU.mult, ["yc", "dg%d" % j], ["yf"])
                    h.dma(dbg_out["yr"][rows, :], yf[:], ["yf"], ["dbg"])
            end_phase()
        if stop_after in ("D", "Cshort"):
            return nc, dbg_out, S

        def load_w_bf16(es, dst, src, kcs, ncols, prows=128):
            stg = [sbt(es, "stg%d" % i, [128, 8, 512]) for i in range(2)]
            ci = 0
            for nb in range(ncols // 512):
                j = ci % 2; ci += 1
                h.dma(stg[j][0:prows, 0:kcs, :], src[:, nb * 512:(nb + 1) * 512].rearrange("(kc p) n -> p kc n", p=prows), [], ["stg%d" % j])
                h.copy("pool", dst[:, :, nb * 512:(nb + 1) * 512], stg[j][0:prows, 0:kcs, :], ["stg%d" % j], ["wconst"])

        with contextlib.ExitStack() as es:
            WBR = sbt(es, "WBR", [128, 8, D], BF16); WBA = sbt(es, "WBA", [64, 4, D], BF16); WO = sbt(es, "WO", [128, 8, D], BF16)
            with contextlib.ExitStack() as es1:
                load_w_bf16(es1, WBR, w_br_rwkv, 8, D)
                load_w_bf16(es1, WO, w_out, 8, D)
                load_w_bf16(es1, WBA, w_br_attn, 4, D, prows=64)
                S.barrier()
            GT1 = [sbt(es, "GT1_%d" % b, [128, D]) for b in range(NB)]
            G2n = [sbt(es, "G2n_%d" % b, [128, D]) for b in range(NB)]
            SH2 = [sbt(es, "SH2_%d" % b, [128, D]) for b in range(NB)]
            tmpg = sbt(es, "tmpg2", [128, D])
            bcast_load(tmpg[:], norm2_g[0:1, :], ["tmpg"])
            for b in range(NB):
                bcast_load(GT1[b][:], modd[b:b + 1, 2048:3072], ["cF"])
                bcast_load(SH2[b][:], modd[b:b + 1, 3072:4096], ["cF"])
                bcast_load(G2n[b][:], modd[b:b + 1, 4096:5120], ["G2n%d" % b])
                h.tt("dve", G2n[b][:], G2n[b][:], tmpg[:], ALU.mult, ["G2n%d" % b, "tmpg"], ["G2n%d" % b])
            cvf = [sbt(es, "cvf%d" % i, [128, 2, D]) for i in range(2)]
            cvb = [sbt(es, "cvb%d" % i, [128, 2, D], BF16) for i in range(2)]
            cv_cnt = [0]

            def emit_conv(n):
                for _ in range(n):
                    ci = cv_cnt[0]
                    if ci >= 128:
                        return
                    cv_cnt[0] += 1
                    src, off_ = (peer_u, 0) if ci < 64 else (peer_v, D)
                    ch = ci % 64; jj = ci % 2
                    s3 = src.rearrange("(p r) d -> p r d", p=128)
                    d3 = uvb.rearrange("(p r) d -> p r d", p=128)[:, :, off_:off_ + D]
                    h.dma(cvf[jj][:], s3[:, ch * 2:(ch + 1) * 2, :], [], ["cvf%d" % jj], eng="pool")
                    h.copy("act", cvb[jj][:], cvf[jj][:], ["cvf%d" % jj], ["cvb%d" % jj])
                    h.dma(d3[:, ch * 2:(ch + 1) * 2, :], cvb[jj][:], ["cvb%d" % jj], ["uvb"], eng="pool")
            ybl = [sbt(es, "ybl%d" % i, [128, D], BF16) for i in range(2)]
            yrT = sbt(es, "yrT", [128, 8, 128], BF16); mgT = sbt(es, "mgT", [128, 8, 128], BF16)
            zg = [sbt(es, "zg%d" % i, [128, 2048]) for i in range(2)]
            t1 = sbt(es, "t1", [128, D]); t2 = sbt(es, "t2", [128, D]); mgb = sbt(es, "mgb", [128, D], BF16)
            xt = [sbt(es, "xtF%d" % i, [128, D]) for i in range(2)]
            ht = [sbt(es, "htF%d" % i, [128, D]) for i in range(2)]
            junk = sbt(es, "junkF", [128, D]); n2b = [sbt(es, "n2bF%d" % i, [128, D], BF16) for i in range(2)]
            ss = sbt(es, "ssF", [128, NTILE]); rs = sbt(es, "rsF", [128, NTILE])
            ptr = pst(es, "ptrF", [128, 8, 128], BF16)
            pm1 = pst(es, "pm1", [128, D]); pm2 = pst(es, "pm2", [128, D]); po = pst(es, "poF", [128, D])
            for i in range(NTILE):
                b = i // 16; j = i % 2; rows = slice(i * 128, (i + 1) * 128)
                h.dma(ybl[j][:], yrs[rows, :], [], ["ybl%d" % j])
                h.dma(zg[j][:], zs[rows, 5632:7680], [], ["zg%d" % j])
                h.dma(xt[j][:], x[rows, :], [], ["xtF%d" % j])
                for kc in range(8):
                    h.tr(ptr[:, kc, :], ybl[j][:, kc * 128:(kc + 1) * 128], idb[:], ["ybl%d" % j, "idb"], ["ptrF"])
                h.copy("act", yrT[:], ptr[:], ["ptrF"], ["yrT"])
                for nb in range(2):
                    cs_ = slice(nb * 512, (nb + 1) * 512)
                    for kc in range(8):
                        h.mm(pm1[:, cs_], yrT[:, kc, :], WBR[:, kc, cs_], kc == 0, kc == 7, ["yrT"], ["pm1"])
                    for hh in range(4):
                        h.mm(pm2[:, cs_], yaT[:, hh, rows], WBA[:, hh, cs_], hh == 0, hh == 3, [], ["pm2"])
                h.act(zg[j][:], zg[j][:], AF.Sigmoid, ["zg%d" % j], ["zg%d" % j])
                h.tt("dve", t1[:], pm1[:], zg[j][:, 0:1024], ALU.mult, ["pm1", "zg%d" % j], ["t1"])
                h.tt("dve", t2[:], pm2[:], zg[j][:, 1024:2048], ALU.mult, ["pm2", "zg%d" % j], ["t2"])
                h.tt("pool", mgb[:], t1[:], t2[:], ALU.add, ["t1", "t2"], ["mgb"])
                for kc in range(8):
                    h.tr(ptr[:, kc, :], mgb[:, kc * 128:(kc + 1) * 128], idb[:], ["mgb", "idb"], ["ptrF"])
                h.copy("act", mgT[:], ptr[:], ["ptrF"], ["mgT"])
                for nb in range(2):
                    cs_ = slice(nb * 512, (nb + 1) * 512)
                    for kc in range(8):
                        h.mm(po[:, cs_], mgT[:, kc, :], WO[:, kc, cs_], kc == 0, kc == 7, ["mgT"], ["poF"])
                h.tt("dve", t1[:], po[:], GT1[b][:], ALU.mult, ["poF", "cF"], ["t1"])
                h.tt("pool", ht[j][:], t1[:], xt[j][:], ALU.add, ["t1", "xtF%d" % j], ["htF%d" % j])
                h.dma(hs[rows, :], ht[j][:], ["htF%d" % j], ["hs"])
                if "h" in dbg_out:
                    h.dma(dbg_out["h"][rows, :], ht[j][:], ["htF%d" % j], ["dbg"])
                h.act(junk[:], ht[j][:], AF.Square, ["htF%d" % j], ["junkF", "ssF%d" % i], accum=ss[:, i:i + 1])
                h.ts("dve", rs[:, i:i + 1], ss[:, i:i + 1], 1.0 / D, EPS, ALU.mult, ALU.add, ["ssF%d" % i], ["rsF%d" % i])
                h.act(rs[:, i:i + 1], rs[:, i:i + 1], AF.Sqrt, ["rsF%d" % i], ["rsF%d" % i])
                h.recip(rs[:, i:i + 1], rs[:, i:i + 1], ["rsF%d" % i], ["rsF%d" % i])
                h.stt(t2[:], ht[j][:], rs[:, i:i + 1], G2n[b][:], ALU.mult, ALU.mult, ["htF%d" % j, "rsF%d" % i, "G2n%d" % b], ["t2"])
                h.tt("pool", n2b[j][:], t2[:], SH2[b][:], ALU.add, ["t2", "cF"], ["n2bF%d" % j])
                h.dma(n2s[rows, :], n2b[j][:], ["n2bF%d" % j], ["n2s"])
                emit_conv(4)
            emit_conv(128)
            end_phase()
        if stop_after == "F":
            return nc, dbg_out, S

        with contextlib.ExitStack() as es:
            WQ = sbt(es, "WQ", [128, 8, 2048], BF16)
            with contextlib.ExitStack() as es1:
                load_w_bf16(es1, WQ, peer_wq, 8, 2048)
                S.barrier()
            K1T = sbt(es, "K1T", [128, 128]); K2T = sbt(es, "K2T", [128, 128]); ktmp = sbt(es, "ktmp", [128, 128])
            GT2 = [sbt(es, "GT2_%d" % b, [128, D]) for b in range(NB)]
            n2b = sbt(es, "n2bG", [128, D], BF16); n2Tt = sbt(es, "n2Tt", [128, 8, 128], BF16)
            qTt = sbt(es, "qTt", [128, 16, 128])
            S1 = sbt(es, "S1", [128, 8, 128]); S2 = sbt(es, "S2", [128, 8, 128]); wk = sbt(es, "wk", [128, 128])
            v1 = sbt(es, "v1", [128, 8, 16]); v2 = sbt(es, "v2", [128, 8, 16])
            i1u = sbt(es, "i1u", [128, 8, 16], U32); i2u = sbt(es, "i2u", [128, 8, 16], U32)
            i1f = sbt(es, "i1f", [128, 8, 16]); i2f = sbt(es, "i2f", [128, 8, 16])
            cand = sbt(es, "cand", [128, 16, 16]); cidx = sbt(es, "cidx", [128, 16, 16]); wk2 = sbt(es, "wk2", [128, 256]); junk2 = sbt(es, "junk2", [128, 256])
            t16 = sbt(es, "t16", [128, 8, 16]); idxf = sbt(es, "idxf", [128, 128]); e16 = sbt(es, "e16", [128, 8, 16]); gate = sbt(es, "gate", [128, 128])
            nmx = sbt(es, "nmx", [128, 8]); Zs = sbt(es, "Zs", [128, 8])
            IDXT = sbt(es, "IDXT", [128, 128], I32); GTt = sbt(es, "GTt", [128, 128]); dots = sbt(es, "dots", [128, 128]); coef = sbt(es, "coef", [128, 128])
            NUB = 10
            UV = [sbt(es, "UV%d" % i, [128, 2 * D], BF16) for i in range(NUB)]
            glu = sbt(es, "glu", [128, 128])
            junkU = sbt(es, "junkU", [128, D], BF16)
            IX1 = [sbt(es, "IX1_%d" % i, [128, 1], I32) for i in range(NUB)]
            coefb = sbt(es, "coefb", [128, 128], BF16)
            WB = [sbt(es, "WB%d" % i, [128, 256], BF16) for i in range(4)]
            htG = sbt(es, "htG", [128, D]); t1 = sbt(es, "t1G", [128, D]); yo = sbt(es, "yo", [128, D])
            ptr = pst(es, "ptrG", [128, 8, 128], BF16)
            pB = pst(es, "pB", [128, 512])
            pbc = [pst(es, "pbc%d" % i, [128, D]) for i in range(2)]
            po = pst(es, "poG", [128, D])
            for b in range(NB):
                bcast_load(GT2[b][:], modd[b:b + 1, 5120:6144], ["cG"])
            for kt_, src in ((K1T, peer_k1), (K2T, peer_k2)):
                h.dma(ktmp[:], src, [], ["ktmp"])
                h.tr(pB[:, 0:128], ktmp[:], idf[:], ["ktmp", "idf"], ["pB"])
                h.copy("dve", kt_[:], pB[:, 0:128], ["pB"], ["cG"])
            for w_ in WB:
                h.memset("pool", w_[:], 0.0, ["cG"])
            ntile_g = NTILE if stop_after != "Gshort" else 1
            for i in range(ntile_g):
                b = i // 16; rows = slice(i * 128, (i + 1) * 128)
                h.dma(n2b[:], n2s[rows, :], [], ["n2bG"])
                h.dma(htG[:], hs[rows, :], [], ["htG"])
                for kc in range(8):
                    h.tr(ptr[:, kc, :], n2b[:, kc * 128:(kc + 1) * 128], idb[:], ["n2bG", "idb"], ["ptrG"])
                h.copy("act", n2Tt[:], ptr[:], ["ptrG"], ["n2Tt"])
                for c4 in range(4):
                    for cq in range(4):
                        cc = c4 * 4 + cq
                        for kc in range(8):
                            h.mm(pB[:, cq * 128:(cq + 1) * 128], WQ[:, kc, cc * 128:(cc + 1) * 128], n2Tt[:, kc, :], kc == 0, kc == 7, ["n2Tt"], ["pB"])
                    h.copy("act" if c4 % 2 == 0 else "dve", qTt[:, c4 * 4:(c4 + 1) * 4, :], pB[:].rearrange("p (a b) -> p a b", a=4), ["pB"], ["qTt"])
                for which, Sx, KT in ((0, S1, K1T), (1, S2, K2T)):
                    for h4 in range(2):
                        for hq in range(4):
                            hd = h4 * 4 + hq
                            h.mm(pB[:, hq * 128:(hq + 1) * 128], qTt[:, 2 * hd + which, :], KT[:], True, True, ["qTt", "cG"], ["pB"])
                        h.copy("act" if h4 == 0 else "dve", Sx[:, h4 * 4:(h4 + 1) * 4, :], pB[:].rearrange("p (a b) -> p a b", a=4), ["pB"], ["S%d" % which])
                import os
                GCUT = float(os.environ.get("GCUT", "9"))
                if GCUT <= 1:
                    continue
                for which, Sx, vx, ix in ((0, S1, v1, i1u), (1, S2, v2, i2u)):
                    sk = "S%d" % which; vk_ = "v%d" % which
                    for hd in range(8):
                        S.op("dve", (lambda o, i_: (lambda e: e.max(out=o, in_=i_)))(vx[:, hd, 0:8], Sx[:, hd, :]), reads=[sk], writes=[vk_])
                        S.op("dve", (lambda o, r_, i_: (lambda e: e.match_replace(out=o, in_to_replace=r_, in_values=i_, imm_value=-1e30)))(wk[:], vx[:, hd, 0:8], Sx[:, hd, :]),
                             reads=[sk, vk_], writes=["wk"])
                        S.op("dve", (lambda o, i_: (lambda e: e.max(out=o, in_=i_)))(vx[:, hd, 8:16], wk[:]), reads=["wk"], writes=[vk_])
                        S.op("dve", (lambda o, m_, i_: (lambda e: e.max_index(out=o, in_max=m_, in_values=i_)))(ix[:, hd, 0:8], vx[:, hd, 0:8], Sx[:, hd, :]),
                             reads=[sk, vk_], writes=["ix%d" % which])
                        S.op("dve", (lambda o, m_, i_: (lambda e: e.max_index(out=o, in_max=m_, in_values=i_)))(ix[:, hd, 8:16], vx[:, hd, 8:16], Sx[:, hd, :]),
                             reads=[sk, vk_], writes=["ix%d" % which])
                if GCUT <= 2:
                    continue
                h.copy("dve", i1f[:], i1u[:], ["ix0"], ["i1f"])
                h.copy("dve", i2f[:], i2u[:], ["ix1"], ["i2f"])
                h.ts("dve", i1f[:], i1f[:], 128.0, None, ALU.mult, None, ["i1f"], ["i1f"])
                cflat = cand[:].rearrange("p a b -> p (a b)"); xflat = cidx[:].rearrange("p a b -> p (a b)")
                for hd in range(8):
                    h.tt("pool", cand[:], bcl(v1[:, hd, :], 16), bc3(v2[:, hd, :], 16), ALU.add, ["v0", "v1"], ["cand"])
                    h.tt("pool", cidx[:], bcl(i1f[:, hd, :], 16), bc3(i2f[:, hd, :], 16), ALU.add, ["i1f", "i2f"], ["cidx"])
                    S.op("dve", (lambda o, i_: (lambda e: e.max(out=o, in_=i_)))(t16[:, hd, 0:8], cflat), reads=["cand"], writes=["t16"])
                    S.op("dve", (lambda o, r_, i_: (lambda e: e.match_replace(out=o, in_to_replace=r_, in_values=i_, imm_value=-1e30)))(wk2[:], t16[:, hd, 0:8], cflat),
                         reads=["cand", "t16"], writes=["wk2"])
                    S.op("dve", (lambda o, i_: (lambda e: e.max(out=o, in_=i_)))(t16[:, hd, 8:16], wk2[:]), reads=["wk2"], writes=["t16"])
                    for k in range(16):
                        if GCUT <= 2.2:
                            continue
                        h.stt(junk2[:], cflat, t16[:, hd, k:k + 1], xflat, ALU.is_equal, ALU.mult, ["cand", "cidx", "t16"], ["junk2", "idxf"],
                              accum=idxf[:, hd * 16 + k:hd * 16 + k + 1])
                h.ts("dve", idxf[:], idxf[:], 16383.0, 0.0, ALU.min, ALU.max, ["idxf"], ["idxf"])
                if GCUT <= 2.4:
                    continue
                h.ts("dve", nmx[:], t16[:, :, 0], -1.0, None, ALU.mult, None, ["t16"], ["nmx"])
                for hd in range(8):
                    h.act(e16[:, hd, :], t16[:, hd, :], AF.Exp, ["t16", "nmx"], ["e16", "Zs"], bias=nmx[:, hd:hd + 1], accum=Zs[:, hd:hd + 1])
                h.recip(Zs[:], Zs[:], ["Zs"], ["Zs"])
                h.tt("dve", gate[:].rearrange("p (a b) -> p a b", a=8), e16[:], bcl(Zs[:], 16), ALU.mult, ["e16", "Zs"], ["gate"])
                if GCUT <= 2.6:
                    continue
                h.tr(pB[:, 0:128], idxf[:], idf[:], ["idxf", "idf"], ["pB"])
                h.tr(pB[:, 128:256], gate[:], idf[:], ["gate", "idf"], ["pB"])
                if GCUT <= 2.7:
                    continue
                h.ts("dve", dots[:], pB[:, 0:128], 8388608.0, None, ALU.add, None, ["pB"], ["dots"])
                S.op("dve", (lambda o, i_: (lambda e: e.tensor_scalar(out=o, in0=i_, scalar1=0x7FFFFF, scalar2=None, op0=ALU.bitwise_and)))(IDXT[:], dots[:].bitcast(I32)), reads=["dots"], writes=["IDXT"])
                if GCUT <= 2.8:
                    continue
                h.copy("dve", GTt[:], pB[:, 128:256], ["pB"], ["GTt"])
                import os
                GCUT = float(os.environ.get("GCUT", "9"))
                if "idxf" in dbg_out and i == 0:
                    h.dma(dbg_out["idxf"], idxf[:], ["idxf"], ["dbg"])
                    h.dma(dbg_out["gate"], gate[:], ["gate"], ["dbg"])
                    h.dma(dbg_out["GTt"], GTt[:], ["GTt"], ["dbg"])
                    h.copy("dve", dots[:], IDXT[:], ["IDXT"], ["dots"])
                    h.dma(dbg_out["IDXTf"], dots[:], ["dots"], ["dbg"])
                if GCUT <= 3:
                    continue
                LA = NUB - 2

                def issue_gather(c2):
                    j2 = c2 % NUB
                    h.copy("dve", IX1[j2][:], IDXT[:, c2:c2 + 1], ["IDXT"], ["IX%d" % j2])
                    S.dma("pool", (lambda o, ix_: (lambda e: e.indirect_dma_start(out=o, out_offset=None, in_=uvb,
                                                                                  in_offset=bass.IndirectOffsetOnAxis(ap=ix_, axis=0))))(UV[j2][:], IX1[j2][:, 0:1]),
                          reads=["IX%d" % j2], writes=["UV%d" % j2])
                def emit_out(c3):
                    j3 = c3 % NUB; jw3 = c3 % 4
                    h.ts("dve", WB[jw3][:, 127:128], glu[:, c3:c3 + 1], GTt[:, c3:c3 + 1], None, ALU.mult, None, ["glu%d" % (c3 % 8), "GTt"], ["WB%d" % jw3])
                    for hb3 in range(2):
                        h.mm(po[:, hb3 * 512:(hb3 + 1) * 512], WB[jw3][:, 127 - c3:255 - c3], UV[j3][:, D + hb3 * 512:D + (hb3 + 1) * 512], c3 == 0, c3 == 127,
                             ["WB%d" % jw3, "UV%d" % j3], ["poG"])
                def emit_bcast(c4):
                    jb4 = c4 % 2
                    for hb4 in range(2):
                        cs4 = slice(hb4 * 512, (hb4 + 1) * 512)
                        h.mm(pbc[jb4][:, cs4], idb[:, c4:c4 + 1].to_broadcast([128, 128]), n2b[:, cs4], True, True, ["n2bG", "idb"], ["pbc%d" % jb4])
                for c2 in range(LA):
                    issue_gather(c2)
                emit_bcast(0)
                for c in range(128):
                    ju = c % NUB; jb = c % 2; jw = c % 4
                    if c + LA < 128:
                        issue_gather(c + LA)
                    if c + 1 < 128:
                        emit_bcast(c + 1)
                    h.stt(junkU[:], UV[ju][:, 0:D], 1.0, pbc[jb][:], ALU.mult, ALU.mult, ["UV%d" % ju, "pbc%d" % jb], ["junkU", "dots%d" % (c % 8)], accum=dots[:, c:c + 1])
                    h.act(glu[:, c:c + 1], dots[:, c:c + 1], AF.Gelu, ["dots%d" % (c % 8)], ["glu%d" % (c % 8)])
                    if c >= 1:
                        emit_out(c - 1)
                emit_out(127)
                h.tt("dve", t1[:], po[:], GT2[b][:], ALU.mult, ["poG", "cG"], ["t1G"])
                h.tt("pool", yo[:], t1[:], htG[:], ALU.add, ["t1G", "htG"], ["yo"])
                last_tok[0] = h.dma(y_out[rows, :], yo[:], ["yo"], ["y"])
            end_phase()
        return nc, dbg_out, S


def host_consts():
    ident = np.eye(128, dtype=np.float32)
    half = 8
    inv = (500000.0 ** (-np.arange(half, dtype=np.float32) / half)).astype(np.float32)
    pos = np.arange(SEQ, dtype=np.float32)
    ang = (pos[:, None] * inv[None, :]).astype(np.float32)
    cs = np.concatenate([np.cos(ang), np.sin(ang)], axis=1).astype(np.float32)
    cs = cs.reshape(16, 128, 16).transpose(1, 0, 2).copy()
    k = np.arange(128)[:, None]; q = np.arange(128)[None, :]
    mcur = (k <= q).astype(np.float32); mprev = (k >= q).astype(np.float32)
    same = (k // 64) == (q // 64)
    ltri = (same & (k <= q)).astype(np.float32)
    lblk = same.astype(np.float32)
    m256 = np.concatenate([(same & (k < q)), (same & (k <= q))], axis=1).astype(np.float32)
    mL = (same & (k > q)).astype(np.float32)
    return dict(ident=ident, cs=cs, mcur=mcur, mprev=mprev, ltri=ltri, lblk=lblk, m256=m256, mL=mL)


def make_in_maps(inputs, n_cores=8):
    consts = host_consts()
    shared = {}
    for k in ("w_ada", "w_in", "w2", "a2", "g2", "w_br_rwkv", "w_br_attn", "w_out", "peer_wq", "peer_k1", "peer_k2",
              "peer_u", "peer_v", "q_norm_g", "k_norm_g"):
        shared[k] = np.ascontiguousarray(inputs[k][0], dtype=np.float32)
    for k in ("b_ada", "norm1_g", "rwkv_mu", "w0", "a0", "k_k", "k_a", "lnx_g", "lnx_b", "norm2_g"):
        shared[k] = np.ascontiguousarray(inputs[k][0].reshape(1, -1), dtype=np.float32)
    shared["r_k"] = np.ascontiguousarray(inputs["r_k"][0].reshape(1, -1), dtype=np.float32)
    shared.update(consts)
    maps = []
    for c in range(n_cores):
        m = dict(shared)
        m["x"] = np.ascontiguousarray(inputs["x"][c * NB:(c + 1) * NB].reshape(NT, D), dtype=np.float32)
        cc = np.asarray(inputs["c"][c * NB:(c + 1) * NB], dtype=np.float32)
        m["cT"] = np.ascontiguousarray(cc.reshape(NB, 8, 128).transpose(2, 1, 0))
        maps.append(m)
    return maps


def kernel(**inputs):
    nc, _, _ = build_core()
    maps = make_in_maps(inputs, 8)
    res = run_bass_kernel_spmd(nc, maps, core_ids=list(range(8)))
    outs = [np.asarray(r["y"]).reshape(NB, SEQ, D) for r in res.results]
    return np.concatenate(outs, axis=0).astype(np.float32)
```

```python
import contextlib
import numpy as np
import ml_dtypes
import concourse.bass as bass
import concourse.mybir as mybir
from concourse.bass_utils import run_bass_kernel_spmd

F32 = mybir.dt.float32
BF16 = mybir.dt.bfloat16
U32 = mybir.dt.uint32
I32 = mybir.dt.int32
AF = mybir.ActivationFunctionType
ALU = mybir.AluOpType
AX = mybir.AxisListType

ENGS = ("pe", "act", "dve", "pool", "sp")
NDMA = {"sp": 8, "pool": 16, "act": 4}

D = 1024
SEQ = 2048
NB = 2
NT = NB * SEQ
NTILE = NT // 128
ZC = 7680
EPS = 1e-6
GN_EPS = 64e-5


class Sched:
    def __init__(self, nc, sems):
        self.nc = nc
        self.sems = sems
        self.ops = {e: [] for e in ENGS}
        self.cnt = {e: 0 for e in ENGS}
        self.dma_slot_cnt = {e: [0] * n for e, n in NDMA.items()}
        self.dma_rr = {e: 0 for e in NDMA}
        self.state = {}
        self.waited = {e: {} for e in ENGS}

    def _deps(self, eng, reads, writes):
        need = {}

        def add(tok):
            if tok is None:
                return
            s, v, e = tok
            if e == "pe" and eng == "pe" and s == "c_pe":
                return
            if need.get(s, 0) < v:
                need[s] = v
        for k in reads:
            st = self.state.get(k)
            if st:
                add(st[0])
        for k in writes:
            st = self.state.get(k)
            if st:
                add(st[0])
                for t in st[1].values():
                    add(t)
        out = []
        w = self.waited[eng]
        for s, v in need.items():
            if w.get(s, 0) < v:
                w[s] = v
                out.append((s, v))
        return out

    def _commit(self, tok, reads, writes):
        for k in reads:
            st = self.state.setdefault(k, [None, {}])
            st[1][tok[0]] = tok
        for k in writes:
            self.state[k] = [tok, {}]

    def op(self, eng, fn, reads=(), writes=()):
        waits = self._deps(eng, reads, writes)
        self.cnt[eng] += 1
        tok = ("c_" + eng, self.cnt[eng], eng)
        self.ops[eng].append(("op", fn, waits, tok))
        self._commit(tok, reads, writes)
        return tok

    def dma(self, eng, fn, reads=(), writes=()):
        slot = self.dma_rr[eng]
        self.dma_rr[eng] = (slot + 1) % NDMA[eng]
        sname = "d_%s_%d" % (eng, slot)
        waits = self._deps(eng, reads, writes)
        prev = self.dma_slot_cnt[eng][slot]
        w = self.waited[eng]
        if prev > 0 and w.get(sname, 0) < prev:
            w[sname] = prev
            waits.append((sname, prev))
        self.dma_slot_cnt[eng][slot] = prev + 16
        tok = (sname, prev + 16, eng)
        self.ops[eng].append(("dma", fn, waits, tok))
        self._commit(tok, reads, writes)
        return tok

    def barrier(self):
        toks = []
        for e in ENGS:
            if e != "sp" and self.cnt[e] > 0:
                toks.append(("c_" + e, self.cnt[e]))
        for e, n in NDMA.items():
            for i in range(n):
                if self.dma_slot_cnt[e][i] > 0:
                    toks.append(("d_%s_%d" % (e, i), self.dma_slot_cnt[e][i]))
        for e in ENGS:
            w = self.waited[e]
            ws = []
            for s, v in toks:
                if w.get(s, 0) < v:
                    w[s] = v
                    ws.append((s, v))
            if ws:
                self.ops[e].append(("wait", None, ws, None))
        self.state = {}

    def emit(self):
        nc = self.nc
        sems = self.sems
        ops = self.ops
        self.ops = {e: [] for e in ENGS}
        with nc.Block() as block:
            def run(engname):
                def body(engine):
                    for kind, fn, waits, tok in ops[engname]:
                        for s, v in waits:
                            engine.wait_ge(sems[s], v)
                        if kind == "wait":
                            continue
                        ins = fn(engine)
                        ins.then_inc(sems[tok[0]], 16 if kind == "dma" else 1)
                return body
            block.tensor(run("pe"))
            block.scalar(run("act"))
            block.vector(run("dve"))
            block.gpsimd(run("pool"))
            block.sync(run("sp"))


def sem_names():
    names = ["c_" + e for e in ENGS if e != "sp"]
    for e, n in NDMA.items():
        names += ["d_%s_%d" % (e, i) for i in range(n)]
    return names


class H:
    def __init__(self, S):
        self.S = S

    def dma(self, out, in_, r, w, eng="sp"):
        return self.S.dma(eng, lambda e: e.dma_start(out=out, in_=in_), reads=r, writes=w)

    def tt(self, eng, out, in0, in1, op, r, w):
        return self.S.op(eng, lambda e: e.tensor_tensor(out=out, in0=in0, in1=in1, op=op), reads=r, writes=w)

    def ts(self, eng, out, in0, s1, s2, op0, op1, r, w, accum=None):
        if op1 is None:
            return self.S.op(eng, lambda e: e.tensor_scalar(out=out, in0=in0, scalar1=s1, scalar2=None, op0=op0), reads=r, writes=w)
        if accum is None:
            return self.S.op(eng, lambda e: e.tensor_scalar(out=out, in0=in0, scalar1=s1, scalar2=s2, op0=op0, op1=op1), reads=r, writes=w)
        return self.S.op(eng, lambda e: e.tensor_scalar(out=out, in0=in0, scalar1=s1, scalar2=s2, op0=op0, op1=op1, accum_out=accum), reads=r, writes=w)

    def stt(self, out, in0, scalar, in1, op0, op1, r, w, accum=None):
        if accum is None:
            return self.S.op("dve", lambda e: e.scalar_tensor_tensor(out=out, in0=in0, scalar=scalar, in1=in1, op0=op0, op1=op1), reads=r, writes=w)
        return self.S.op("dve", lambda e: e.scalar_tensor_tensor(out=out, in0=in0, scalar=scalar, in1=in1, op0=op0, op1=op1, accum_out=accum), reads=r, writes=w)

    def copy(self, eng, out, in_, r, w):
        if eng == "act":
            return self.S.op("act", lambda e: e.copy(out=out, in_=in_), reads=r, writes=w)
        return self.S.op(eng, lambda e: e.tensor_copy(out=out, in_=in_), reads=r, writes=w)

    def act(self, out, in_, func, r, w, scale=1.0, bias=None, accum=None):
        def fn(e):
            kw = dict(out=out, in_=in_, func=func, scale=scale)
            if bias is not None:
                kw["bias"] = bias
            if accum is not None:
                kw["accum_out"] = accum
            return e.activation(**kw)
        return self.S.op("act", fn, reads=r, writes=w)

    def red(self, out, in_, op, r, w):
        return self.S.op("dve", lambda e: e.tensor_reduce(out=out, in_=in_, axis=AX.X, op=op), reads=r, writes=w)

    def recip(self, out, in_, r, w):
        return self.S.op("dve", lambda e: e.reciprocal(out=out, in_=in_), reads=r, writes=w)

    def memset(self, eng, ap, val, w):
        return self.S.op(eng, lambda e: e.memset(ap, val), reads=(), writes=w)

    def mm(self, out, lhsT, rhs, start, stop, r, w):
        return self.S.op("pe", lambda e: e.matmul(out, lhsT, rhs, start=start, stop=stop), reads=r, writes=w)

    def tr(self, out, in_, ident, r, w):
        return self.S.op("pe", lambda e: e.transpose(out=out, in_=in_, identity=ident), reads=r, writes=w)


def bc3(ap, n_mid):
    p, f = ap.shape
    return ap.unsqueeze(1).to_broadcast([p, n_mid, f])


def bcl(ap, n_last):
    p, f = ap.shape
    return ap.unsqueeze(2).to_broadcast([p, f, n_last])


def build_core(stop_after=None, dbg=()):
    nc = bass.Bass("TRN2", target_bir_lowering=False)
    din = lambda name, shape, dt=F32: nc.dram_tensor(name, shape, dt, kind="ExternalInput").ap()
    x = din("x", [NT, D])
    cT = din("cT", [128, 8, NB])
    w_ada = din("w_ada", [D, 6 * D]); b_ada = din("b_ada", [1, 6 * D])
    norm1_g = din("norm1_g", [1, D]); w_in = din("w_in", [D, ZC])
    rwkv_mu = din("rwkv_mu", [1, 3328]); w0 = din("w0", [1, D]); w2 = din("w2", [64, D])
    a0 = din("a0", [1, D]); a2 = din("a2", [64, D]); g2 = din("g2", [128, D])
    k_k = din("k_k", [1, D]); k_a = din("k_a", [1, D]); r_k = din("r_k", [1, D])
    lnx_g = din("lnx_g", [1, D]); lnx_b = din("lnx_b", [1, D])
    q_norm_g = din("q_norm_g", [3, 64]); k_norm_g = din("k_norm_g", [3, 64])
    w_br_rwkv = din("w_br_rwkv", [D, D]); w_br_attn = din("w_br_attn", [256, D]); w_out = din("w_out", [D, D])
    norm2_g = din("norm2_g", [1, D]); peer_wq = din("peer_wq", [D, 2048])
    peer_k1 = din("peer_k1", [128, 128]); peer_k2 = din("peer_k2", [128, 128])
    peer_u = din("peer_u", [16384, D]); peer_v = din("peer_v", [16384, D])
    ident_d = din("ident", [128, 128]); cs_d = din("cs", [128, 16, 16])
    mcur_d = din("mcur", [128, 128]); mprev_d = din("mprev", [128, 128])
    iota16_d = din("iota16", [128, 16])
    ltri_d = din("ltri", [128, 128]); lblk_d = din("lblk", [128, 128]); m256_d = din("m256", [128, 256]); mL_d = din("mL", [128, 128])
    y_out = nc.dram_tensor("y", [NT, D], F32, kind="ExternalOutput").ap()

    dscr = lambda name, shape, dt=F32: nc.dram_tensor(name, shape, dt).ap()
    modd = dscr("modd", [NB, 6 * D])
    zs = dscr("zs", [NT, ZC])
    sc = {k: dscr("sc_" + k, [NT, D]) for k in ("r", "w", "k", "v", "a", "b", "g", "y")}
    yrs = dscr("yrs", [NT, D], BF16)
    hs = dscr("hs", [NT, D])
    n2s = dscr("n2s", [NT, D], BF16)
    uvb = dscr("uvb", [16384, 2 * D], BF16)
    dbg_out = {}
    for name, shape in dbg:
        dbg_out[name] = nc.dram_tensor("dbg_" + name, shape, F32, kind="ExternalOutput").ap()

    with contextlib.ExitStack() as top:
        sems = {n: top.enter_context(nc.semaphore(n)) for n in sem_names()}
        S = Sched(nc, sems)
        h = H(S)
        uid = [0]

        def sbt(es, name, shape, dt=F32):
            uid[0] += 1
            return es.enter_context(nc.sbuf_tensor("s%d_%s" % (uid[0], name), shape, dt))

        def pst(es, name, shape, dt=F32):
            uid[0] += 1
            return es.enter_context(nc.psum_tensor("p%d_%s" % (uid[0], name), shape, dt))

        idf = sbt(top, "idf", [128, 128]); idb = sbt(top, "idb", [128, 128], BF16)
        yaT = sbt(top, "yaT", [64, 4, NT], BF16)
        h.dma(idf[:], ident_d, [], ["idf"])
        h.copy("dve", idb[:], idf[:], ["idf"], ["idb"])
        last_tok = [None]

        def end_phase():
            S.barrier()
            S.emit()

        with contextlib.ExitStack() as es:
            ct = sbt(es, "ct", [128, 8, NB]); sil = sbt(es, "sil", [128, 8, NB])
            wad = [sbt(es, "wad%d" % i, [128, 8, 512]) for i in range(2)]
            bad = sbt(es, "bad", [1, 6 * D]); modrow = sbt(es, "modrow", [1, NB, 6 * D])
            psm = [pst(es, "psm%d" % i, [128, 512]) for i in range(NB)]
            h.dma(ct[:], cT, [], ["ct"])
            h.dma(bad[:], b_ada, [], ["bad"])
            h.act(sil[:], ct[:], AF.Silu, ["ct"], ["sil"])
            for nb in range(12):
                wb = wad[nb % 2]
                h.dma(wb[:], w_ada[:, nb * 512:(nb + 1) * 512].rearrange("(kc p) n -> p kc n", p=128), [], ["wad%d" % (nb % 2)])
                for b in range(NB):
                    for kc in range(8):
                        h.mm(psm[b][0:1, :], sil[:, kc, b:b + 1], wb[:, kc, :], kc == 0, kc == 7,
                             ["sil", "wad%d" % (nb % 2)], ["psm%d" % b])
                    h.tt("dve", modrow[0:1, b, nb * 512:(nb + 1) * 512], psm[b][0:1, :], bad[0:1, nb * 512:(nb + 1) * 512],
                         ALU.add, ["psm%d" % b, "bad"], ["modrow"])
            for b in range(NB):
                for o in (1024, 4096):
                    h.ts("dve", modrow[0:1, b, o:o + 1024], modrow[0:1, b, o:o + 1024], 1.0, None, ALU.add, None, ["modrow"], ["modrow"])
                h.dma(modd[b:b + 1, :], modrow[0:1, b, :], ["modrow"], ["modd"])
            end_phase()
        if stop_after == "0":
            return nc, dbg_out, S

        def bcast_load(dst, src_row, w, r=()):
            return h.dma(dst, src_row.partition_broadcast(128), list(r), w)

        with contextlib.ExitStack() as es:
            n1T = sbt(es, "n1T", [128, 8, NT], BF16)
            with contextlib.ExitStack() as es1:
                G1 = [sbt(es1, "G1_%d" % b, [128, D]) for b in range(NB)]
                SH1 = [sbt(es1, "SH1_%d" % b, [128, D]) for b in range(NB)]
                tmpg = sbt(es1, "tmpg", [128, D])
                xt = [sbt(es1, "xt%d" % i, [128, D]) for i in range(2)]
                junk = sbt(es1, "junk", [128, D]); n1f = sbt(es1, "n1f", [128, D])
                n1b = [sbt(es1, "n1b%d" % i, [128, D], BF16) for i in range(2)]
                ss = sbt(es1, "ss", [128, NTILE]); rs = sbt(es1, "rs", [128, NTILE])
                ptr = [pst(es1, "ptr%d" % i, [128, 8, 128], BF16) for i in range(2)]
                bcast_load(tmpg[:], norm1_g[0:1, :], ["tmpg"])
                for b in range(NB):
                    bcast_load(G1[b][:], modd[b:b + 1, 1024:2048], ["G1_%d" % b], ["modd"])
                    bcast_load(SH1[b][:], modd[b:b + 1, 0:1024], ["SH1_%d" % b], ["modd"])
                    h.tt("dve", G1[b][:], G1[b][:], tmpg[:], ALU.mult, ["G1_%d" % b, "tmpg"], ["G1_%d" % b])
                for i in range(NTILE):
                    b = i // 16; j = i % 2
                    h.dma(xt[j][:], x[i * 128:(i + 1) * 128, :], [], ["xt%d" % j])
                    h.act(junk[:], xt[j][:], AF.Square, ["xt%d" % j], ["junk", "ss%d" % i], accum=ss[:, i:i + 1])
                    h.ts("dve", rs[:, i:i + 1], ss[:, i:i + 1], 1.0 / D, EPS, ALU.mult, ALU.add, ["ss%d" % i], ["rs%d" % i])
                    h.act(rs[:, i:i + 1], rs[:, i:i + 1], AF.Sqrt, ["rs%d" % i], ["rs%d" % i])
                    h.recip(rs[:, i:i + 1], rs[:, i:i + 1], ["rs%d" % i], ["rs%d" % i])
                    h.stt(n1f[:], xt[j][:], rs[:, i:i + 1], G1[b][:], ALU.mult, ALU.mult, ["xt%d" % j, "rs%d" % i, "G1_%d" % b], ["n1f"])
                    h.tt("pool", n1b[j][:], n1f[:], SH1[b][:], ALU.add, ["n1f", "SH1_%d" % b], ["n1b%d" % j])
                    for kc in range(8):
                        h.tr(ptr[j][:, kc, :], n1b[j][:, kc * 128:(kc + 1) * 128], idb[:], ["n1b%d" % j, "idb"], ["ptr%d" % j])
                    h.copy("act", n1T[:, :, i * 128:(i + 1) * 128], ptr[j][:], ["ptr%d" % j], ["n1T%d" % i])
                end_phase()
            with contextlib.ExitStack() as es2:
                wf = [sbt(es2, "wf%d" % i, [128, 8, 512]) for i in range(2)]
                wbf = [sbt(es2, "wbf%d" % i, [128, 8, 512], BF16) for i in range(2)]
                zo = [sbt(es2, "zo%d" % i, [128, 512]) for i in range(4)]
                pz = [pst(es2, "pz%d" % i, [128, 512]) for i in range(4)]
                cnt = 0
                for nb in range(15):
                    j = nb % 2
                    h.dma(wf[j][:], w_in[:, nb * 512:(nb + 1) * 512].rearrange("(kc p) n -> p kc n", p=128), [], ["wf%d" % j])
                    h.copy("pool", wbf[j][:], wf[j][:], ["wf%d" % j], ["wbf%d" % j])
                    for i in range(NTILE):
                        q = cnt % 4; cnt += 1
                        for kc in range(8):
                            h.mm(pz[q][:], n1T[:, kc, i * 128:(i + 1) * 128], wbf[j][:, kc, :], kc == 0, kc == 7,
                                 ["wbf%d" % j], ["pz%d" % q])
                        h.copy("act" if q % 2 == 0 else "dve", zo[q][:], pz[q][:], ["pz%d" % q], ["zo%d" % q])
                        h.dma(zs[i * 128:(i + 1) * 128, nb * 512:(nb + 1) * 512], zo[q][:], ["zo%d" % q], ["zs"])
                end_phase()
        if stop_after == "A":
            return nc, dbg_out, S

        for b in range(NB):
            tb = b * SEQ
            with contextlib.ExitStack() as es:
                qT = sbt(es, "qT", [128, 6, SEQ], BF16); kT = sbt(es, "kT", [128, 6, SEQ], BF16)
                VA = [sbt(es, "VA%d" % g, [128, 16, 4, 65], BF16) for g in range(3)]
                ACC = sbt(es, "ACC", [65, 4, SEQ]); RD = sbt(es, "RD", [65, SEQ])
                onesf = sbt(es, "onesf", [65, 64])
                mcur = sbt(es, "mcur", [128, 128], BF16); mprev = sbt(es, "mprev", [128, 128], BF16)
                mtmp = sbt(es, "mtmp", [128, 128])
                QG = sbt(es, "QG", [128, 12, 64]); KG = sbt(es, "KG", [128, 12, 64])
                cs = sbt(es, "cs", [128, 16, 16])
                h.copy("dve", onesf[:, :], idf[0:65, 64:65].to_broadcast([65, 64]), ["idf"], ["onesf"])
                h.dma(mtmp[:], mcur_d, [], ["mtmp"]); h.copy("dve", mcur[:], mtmp[:], ["mtmp"], ["mcur"])
                h.dma(mtmp[:], mprev_d, ["mtmp"], ["mtmp"]); h.copy("dve", mprev[:], mtmp[:], ["mtmp"], ["mprev"])
                h.dma(cs[:], cs_d, [], ["cs"])
                for gh in range(12):
                    bcast_load(QG[:, gh, :], q_norm_g[gh // 4:gh // 4 + 1, :], ["QG"])
                    bcast_load(KG[:, gh, :], k_norm_g[gh // 4:gh // 4 + 1, :], ["KG"])
                for g in range(3):
                    h.memset("pool", VA[g][:, :, :, 64:65], 1.0, ["VA%d" % g])
                with contextlib.ExitStack() as es1:
                    zq = [sbt(es1, "zq%d" % i, [128, 768]) for i in range(2)]
                    zk = [sbt(es1, "zk%d" % i, [128, 768]) for i in range(2)]
                    sq = sbt(es1, "sq", [128, 768]); st = sbt(es1, "st", [128, 12]); qn = sbt(es1, "qn", [128, 768])
                    r1 = sbt(es1, "r1", [128, 12, 8]); r2 = sbt(es1, "r2", [128, 12, 8])
                    qb = [sbt(es1, "qb%d" % i, [128, 768], BF16) for i in range(2)]
                    ptq = [pst(es1, "ptq%d" % i, [128, 6, 128], BF16) for i in range(2)]
                    vtmp = [sbt(es1, "vtmp%d" % i, [128, 256]) for i in range(2)]
                    cntp = 0
                    for i in range(16):
                        cosb = bc3(cs[:, i, 0:8], 12); sinb = bc3(cs[:, i, 8:16], 12)
                        for which, zt, dstT, GG, c0, scl in (("q", zq, qT, QG, 3328, 0.125), ("k", zk, kT, KG, 3328 + 768, 1.0)):
                            j = cntp % 2; cntp += 1
                            zn = "z%s%d" % (which, i % 2)
                            zz = zt[i % 2]
                            h.dma(zz[:], zs[tb + i * 128:tb + (i + 1) * 128, c0:c0 + 768], ["zs"], [zn])
                            h.tt("pool", sq[:], zz[:], zz[:], ALU.mult, [zn], ["sq"])
                            h.red(st[:], sq[:].rearrange("p (a b) -> p a b", a=12), ALU.add, ["sq"], ["st"])
                            h.ts("dve", st[:], st[:], 1.0 / 64, EPS, ALU.mult, ALU.add, ["st"], ["st"])
                            h.act(st[:], st[:], AF.Sqrt, ["st"], ["st"])
                            h.recip(st[:], st[:], ["st"], ["st"])
                            if scl != 1.0:
                                h.ts("dve", st[:], st[:], scl, None, ALU.mult, None, ["st"], ["st"])
                            z3 = zz[:].rearrange("p (a b) -> p a b", a=12)
                            q3 = qn[:].rearrange("p (a b) -> p a b", a=12)
                            h.tt("dve", q3, z3, bcl(st[:], 64), ALU.mult, [zn, "st"], ["qn"])
                            h.tt("pool", q3, q3, GG[:], ALU.mult, ["qn", "QG", "KG"], ["qn"])
                            qb3 = qb[j][:].rearrange("p (a b) -> p a b", a=12)
                            h.tt("dve", r1[:], q3[:, :, 0:8], cosb, ALU.mult, ["qn", "cs"], ["r1"])
                            h.tt("dve", r2[:], q3[:, :, 8:16], sinb, ALU.mult, ["qn", "cs"], ["r2"])
                            h.tt("dve", qb3[:, :, 0:8], r1[:], r2[:], ALU.subtract, ["r1", "r2"], ["qb%d" % j])
                            h.tt("dve", r1[:], q3[:, :, 8:16], cosb, ALU.mult, ["qn", "cs", "qb%d" % j], ["r1"])
                            h.tt("dve", r2[:], q3[:, :, 0:8], sinb, ALU.mult, ["qn", "cs", "qb%d" % j], ["r2"])
                            h.tt("dve", qb3[:, :, 8:16], r1[:], r2[:], ALU.add, ["r1", "r2"], ["qb%d" % j])
                            h.copy("pool", qb3[:, :, 16:64], q3[:, :, 16:64], ["qn"], ["qb%d" % j])
                            for c6 in range(6):
                                h.tr(ptq[j][:, c6, :], qb[j][:, c6 * 128:(c6 + 1) * 128], idb[:], ["qb%d" % j, "idb"], ["ptq%d" % j])
                            h.copy("act", dstT[:, :, i * 128:(i + 1) * 128], ptq[j][:], ["ptq%d" % j], [which + "T"])
                    cv = 0
                    for g, dil in enumerate((1, 4, 16)):
                        nmt = SEQ // dil // 128
                        for r_ in range(dil):
                            for mt in range(nmt):
                                kt = r_ * nmt + mt
                                j = cv % 2; cv += 1
                                st0 = tb + r_ + dil * 128 * mt
                                src = zs[st0:st0 + dil * 127 + 1:dil, 3328 + 1536 + g * 256:3328 + 1536 + (g + 1) * 256]
                                h.dma(vtmp[j][:], src, ["zs"], ["vtmp%d" % j])
                                h.copy("pool", VA[g][:, kt, :, 0:64], vtmp[j][:].rearrange("p (a b) -> p a b", a=4),
                                       ["vtmp%d" % j], ["VA%d" % g])
                    S.barrier()
                import os
                EP = os.environ.get("EPART", "")
                if EP == "prep":
                    end_phase()
                    continue
                with contextlib.ExitStack() as es1:
                    pss = [pst(es1, "pss%d" % i, [128, 512]) for i in range(2)]
                    pso = [pst(es1, "pso%d" % i, [128, 512]) for i in range(2)]
                    psb = pst(es1, "psb", [128, 512])
                    ee = [sbt(es1, "ee%d" % i, [128, 128], BF16) for i in range(3)]
                    pp = [sbt(es1, "pp%d" % i, [128, 128], BF16) for i in range(3)]
                    ce = 0; co = 0
                    for g, dil in enumerate((1, 4, 16)):
                        nmt = SEQ // dil // 128
                        if EP == "g0" and g > 0:
                            continue
                        for hh in range(4):
                            gh = g * 4 + hh; ch = gh // 2; base = (gh % 2) * 64
                            for r_ in range(dil):
                                for mt in range(nmt):
                                    cq = slice(r_ + dil * 128 * mt, r_ + dil * 128 * mt + dil * 127 + 1, dil)
                                    jo = co % 2; co += 1
                                    kts = ([(mt - 1, mprev)] if mt > 0 else []) + [(mt, mcur)]
                                    for n_, (kmt, msk) in enumerate(kts):
                                        ck = slice(r_ + dil * 128 * kmt, r_ + dil * 128 * kmt + dil * 127 + 1, dil)
                                        js = ce % 2; je = ce % 3; ce += 1
                                        h.mm(pss[js][:, 0:128], kT[base:base + 64, ch, ck], qT[base:base + 64, ch, cq], True, True,
                                             ["qT", "kT"], ["pss%d" % js])
                                        h.act(ee[je][:], pss[js][:, 0:128], AF.Exp, ["pss%d" % js], ["ee%d" % je])
                                        h.tt("dve" if ce % 2 else "pool", pp[je][:], ee[je][:], msk[:], ALU.mult, ["ee%d" % je, "mcur", "mprev"], ["pp%d" % je])
                                        h.mm(pso[jo][0:65, 0:128], VA[g][:, r_ * nmt + kmt, hh, :], pp[je][:], n_ == 0, n_ == len(kts) - 1,
                                             ["pp%d" % je, "VA%d" % g], ["pso%d" % jo])
                                    if g == 0:
                                        h.copy("dve", ACC[:, hh, cq], pso[jo][0:65, 0:128], ["pso%d" % jo], ["ACC%d" % hh])
                                    else:
                                        h.tt("dve", ACC[:, hh, cq], ACC[:, hh, cq], pso[jo][0:65, 0:128], ALU.add, ["pso%d" % jo, "ACC%d" % hh], ["ACC%d" % hh])
                    for hh in range(4):
                        if EP in ("g0", "nofinal"):
                            continue
                        h.ts("dve", RD[:, :], ACC[:, hh, :], 1e-30, None, ALU.max, None, ["ACC%d" % hh], ["RD"])
                        h.recip(RD[:, :], RD[:, :], ["RD"], ["RD"])
                        for cb in range(4):
                            cs_ = slice(cb * 512, (cb + 1) * 512)
                            h.mm(psb[0:64, :], onesf[:, :], RD[:, cs_], True, True, ["RD", "onesf"], ["psb"])
                            h.tt("dve", yaT[:, hh, tb + cb * 512:tb + (cb + 1) * 512], ACC[0:64, hh, cs_], psb[0:64, :], ALU.mult,
                                 ["psb", "ACC%d" % hh], ["yaT"])
                    end_phase()
        if "yaT" in dbg_out:
            with contextlib.ExitStack() as es:
                tmpd = [sbt(es, "tmpd%d" % i, [64, 2048]) for i in range(2)]
                ci = 0
                for hh in range(4):
                    for cb in range(NT // 2048):
                        j = ci % 2; ci += 1
                        h.copy("dve", tmpd[j][:], yaT[:, hh, cb * 2048:(cb + 1) * 2048], ["yaT"], ["tmpd%d" % j])
                        h.dma(dbg_out["yaT"][:, hh, cb * 2048:(cb + 1) * 2048], tmpd[j][:], ["tmpd%d" % j], ["dbg"])
                end_phase()
        if stop_after == "E":
            return nc, dbg_out, S

        with contextlib.ExitStack() as es:
            MU = sbt(es, "MU", [128, 3328])
            W0b = sbt(es, "W0b", [128, D]); A0b = sbt(es, "A0b", [128, D]); KKb = sbt(es, "KKb", [128, D]); KAb = sbt(es, "KAb", [128, D])
            W2A2 = sbt(es, "W2A2", [128, D]); G2t = sbt(es, "G2t", [128, D])
            zc = sbt(es, "zc", [128, 3328]); zp = sbt(es, "zp", [128, 3328])
            L = sbt(es, "L", [128, 256]); LT = sbt(es, "LT", [128, 2, 128])
            wt = sbt(es, "wt", [128, D]); at = sbt(es, "at", [128, D]); kk = sbt(es, "kk", [128, D]); sq = sbt(es, "sqb", [128, D])
            av = sbt(es, "av", [128, D]); bv = sbt(es, "bv", [128, D]); k2 = sbt(es, "k2", [128, D]); gt = sbt(es, "gt", [128, D])
            s16 = sbt(es, "s16", [128, 16])
            pl = pst(es, "pl", [128, 2, 128]); plw = pst(es, "plw", [128, D]); pla = pst(es, "pla", [128, D]); plg = pst(es, "plg", [128, D])
            bcast_load(MU[:], rwkv_mu[0:1, :], ["MU"])
            for t_, src in ((W0b, w0), (A0b, a0), (KKb, k_k), (KAb, k_a)):
                bcast_load(t_[:], src[0:1, :], ["cB"])
            h.dma(W2A2[0:64, :], w2, [], ["cB"]); h.dma(W2A2[64:128, :], a2, [], ["cB"]); h.dma(G2t[:], g2, [], ["cB"])
            for i in range(NTILE):
                T0 = i * 128
                h.dma(zc[:], zs[T0:T0 + 128, 0:3328], [], ["zc"])
                if i % 16 == 0:
                    h.memset("pool", zp[0:1, :], 0.0, ["zp"])
                    h.dma(zp[1:128, :], zs[T0:T0 + 127, 0:3328], ["zp"], ["zp"])
                else:
                    h.dma(zp[:], zs[T0 - 1:T0 + 127, 0:3328], [], ["zp"])
                h.tt("dve", zp[:], zp[:], zc[:], ALU.subtract, ["zp", "zc"], ["zp"])
                h.tt("pool", zp[:], zp[:], MU[:], ALU.mult, ["zp", "MU"], ["zp"])
                h.tt("dve", zp[:], zp[:], zc[:], ALU.add, ["zp", "zc"], ["zp"])
                h.act(L[:, 0:64], zp[:, 3072:3136], AF.Tanh, ["zp"], ["L"])
                h.copy("pool", L[:, 64:128], zp[:, 3136:3200], ["zp"], ["L"])
                h.act(L[:, 128:256], zp[:, 3200:3328], AF.Sigmoid, ["zp"], ["L"])
                h.tr(pl[:, 0, :], L[:, 0:128], idf[:], ["L", "idf"], ["pl"])
                h.tr(pl[:, 1, :], L[:, 128:256], idf[:], ["L", "idf"], ["pl"])
                h.copy("act", LT[:], pl[:], ["pl"], ["LT"])
                for hb in range(2):
                    cs_ = slice(hb * 512, (hb + 1) * 512)
                    h.mm(plw[:, cs_], LT[0:64, 0, :], W2A2[0:64, cs_], True, True, ["LT", "cB"], ["plw"])
                    h.mm(pla[:, cs_], LT[64:128, 0, :], W2A2[64:128, cs_], True, True, ["LT", "cB"], ["pla"])
                    h.mm(plg[:, cs_], LT[:, 1, :], G2t[:, cs_], True, True, ["LT", "cB"], ["plg"])
                h.tt("dve", wt[:], plw[:], W0b[:], ALU.add, ["plw", "cB"], ["wt"])
                h.act(wt[:], wt[:], AF.Sigmoid, ["wt"], ["wt"])
                h.ts("pool", wt[:], wt[:], -float(np.exp(-0.5)), None, ALU.mult, None, ["wt"], ["wt"])
                h.tt("dve", at[:], pla[:], A0b[:], ALU.add, ["pla", "cB"], ["at"])
                h.act(at[:], at[:], AF.Sigmoid, ["at"], ["at"])
                h.copy("act", gt[:], plg[:], ["plg"], ["gt"])
                h.tt("pool", kk[:], zp[:, 1024:2048], KKb[:], ALU.mult, ["zp", "cB"], ["kk"])
                h.tt("pool", sq[:], kk[:], kk[:], ALU.mult, ["kk"], ["sqb"])
                h.red(s16[:], sq[:].rearrange("p (a b) -> p a b", a=16), ALU.add, ["sqb"], ["s16"])
                h.act(s16[:], s16[:], AF.Sqrt, ["s16"], ["s16"])
                h.ts("dve", s16[:], s16[:], 1e-12, None, ALU.max, None, ["s16"], ["s16"])
                h.recip(s16[:], s16[:], ["s16"], ["s16"])
                kk3 = kk[:].rearrange("p (a b) -> p a b", a=16)
                h.tt("dve", kk3, kk3, bcl(s16[:], 64), ALU.mult, ["kk", "s16"], ["kk"])
                h.ts("pool", av[:], kk[:], -1.0, None, ALU.mult, None, ["kk"], ["av"])
                h.tt("pool", bv[:], kk[:], at[:], ALU.mult, ["kk", "at"], ["bv"])
                h.stt(k2[:], at[:], -1.0, KAb[:], ALU.add, ALU.mult, ["at", "cB"], ["k2"])
                h.stt(k2[:], k2[:], 1.0, zp[:, 1024:2048], ALU.add, ALU.mult, ["k2", "zp"], ["k2"])
                rows = slice(T0, T0 + 128)
                h.dma(sc["r"][rows, :], zp[:, 0:1024], ["zp"], ["sc_r"])
                h.dma(sc["v"][rows, :], zp[:, 2048:3072], ["zp"], ["sc_v"])
                h.dma(sc["w"][rows, :], wt[:], ["wt"], ["sc_w"])
                h.dma(sc["k"][rows, :], k2[:], ["k2"], ["sc_k"])
                h.dma(sc["a"][rows, :], av[:], ["av"], ["sc_a"])
                h.dma(sc["b"][rows, :], bv[:], ["bv"], ["sc_b"])
                h.dma(sc["g"][rows, :], gt[:], ["gt"], ["sc_g"])
            end_phase()

        with contextlib.ExitStack() as es:
            ltri = sbt(es, "ltri", [128, 128]); lblk = sbt(es, "lblk", [128, 128])
            m256 = sbt(es, "m256", [128, 256], BF16); mL = sbt(es, "mL", [128, 128], BF16); mtmp2 = sbt(es, "mtmp2", [128, 256])
            h.dma(ltri[:], ltri_d, [], ["cC"]); h.dma(lblk[:], lblk_d, [], ["cC"])
            h.dma(mtmp2[:], m256_d, [], ["mtmp2"]); h.copy("dve", m256[:], mtmp2[:], ["mtmp2"], ["cC"])
            h.dma(mtmp2[:, 0:128], mL_d, ["mtmp2"], ["mtmp2"]); h.copy("dve", mL[:], mtmp2[:, 0:128], ["mtmp2"], ["cC"])
            ld = {n: [sbt(es, "l%s%d" % (n, i), [128, D]) for i in range(2)] for n in ("r", "w", "k", "v", "a", "b")}
            ep = sbt(es, "ep", [128, D]); em = sbt(es, "em", [128, D]); eA = sbt(es, "eA", [128, D]); eE = sbt(es, "eE", [128, D])
            cumS = sbt(es, "cumS", [128, D]); gE = sbt(es, "gE", [128, D])
            Rb, Ab, Kb, Bb, Kh, Bh, Vb = (sbt(es, n, [128, D], BF16) for n in ("Rb", "Ab", "Kb", "Bb", "Kh", "Bh", "Vb"))
            ART = sbt(es, "ART", [128, 8, 2, 128], BF16); KT = sbt(es, "KTc", [128, 8, 128], BF16); BT = sbt(es, "BTc", [128, 8, 128], BF16)
            Gf = sbt(es, "Gf", [128, 8, 2])
            STf = sbt(es, "STf", [128, 8, 64]); STb = sbt(es, "STb", [128, 8, 64], BF16)
            ytile = [sbt(es, "ytile%d" % i, [128, D]) for i in range(2)]
            NG = 4
            hb_ = lambda n, shape: [sbt(es, "%s_%d" % (n, q), shape, BF16) for q in range(NG)]
            X1 = hb_("X1", [128, 256]); X2 = hb_("X2", [128, 256]); Q0 = hb_("Q0", [128, 128]); Q1 = hb_("Q1", [128, 128])
            P0 = hb_("P0", [128, 128]); P1 = hb_("P1", [128, 128]); QI = hb_("QI", [128, 128]); TT = hb_("TT", [128, 128])
            AVs = hb_("AVs", [128, 64]); Wt = hb_("Wt", [128, 64]); U0b = hb_("U0b", [128, 64]); R2T = hb_("R2T", [128, 128])
            MTc = hb_("MTc", [128, 2, 64])
            Y0s = [sbt(es, "Y0s_%d" % q, [128, 64]) for q in range(NG)]; Z0c = [sbt(es, "Z0c_%d" % q, [128, 2, 64]) for q in range(NG)]
            PB = [pst(es, "PB%d" % q, [128, 512]) for q in range(7)]
            PT = pst(es, "PTc", [128, 8, 128], BF16)

            def head_gen(hd, hq, yt):
                hb = hd // 2; base = (hd % 2) * 64; hs = slice(base, base + 64); cols = slice(hd * 64, (hd + 1) * 64)
                B = PB[hq]; bk = "PB%d" % hq; q_ = "_%d" % hq
                cp = "act" if hq % 2 == 0 else "dve"
                art2 = ART[hs, hb, :, :].rearrange("p a b -> p (a b)")
                h.mm(B[:, 0:256], BT[hs, hb, :], art2, True, True, ["ART", "BTc"], [bk])
                h.tt("dve", X1[hq][:], B[:, 0:256], m256[:], ALU.mult, ["cC"], [bk, "X1" + q_])
                yield
                h.mm(B[:, 0:256], KT[hs, hb, :], art2, True, True, ["ART", "KTc"], [bk])
                h.tt("dve", X2[hq][:], B[:, 0:256], m256[:], ALU.mult, ["cC"], [bk, "X2" + q_])
                yield
                h.mm(B[:, 0:128], ART[hs, hb, 0, :], BT[hs, hb, :], True, True, ["ART", "BTc"], [bk])
                h.tt("dve", Q0[hq][:], B[:, 0:128], mL[:], ALU.mult, ["cC"], [bk, "Q0" + q_])
                h.tt("pool", TT[hq][:], X1[hq][:, 0:128], idb[:], ALU.add, ["X1" + q_, "idb"], ["TT" + q_])
                yield
                P_, Pk = X1[hq][:, 0:128], "X1" + q_
                Q_, Qk = Q0[hq][:], "Q0" + q_
                Qbuf = [Q0[hq], Q1[hq]]; Pbuf = [P0[hq], P1[hq]]
                for k in range(1, 6):
                    Qn, Qnk = Qbuf[k % 2][:], "Q%d%s" % (k % 2, q_)
                    h.mm(B[:, 0:128], P_, Q_, True, True, [Pk, Qk], [bk])
                    h.copy(cp, Qn, B[:, 0:128], [], [bk, Qnk])
                    if k < 5:
                        Pn, Pnk = Pbuf[k % 2][:], "P%d%s" % (k % 2, q_)
                        h.mm(B[:, 128:256], Q_, P_, True, True, [Pk, Qk], [bk])
                        h.copy(cp, Pn, B[:, 128:256], [], [bk, Pnk])
                    yield
                    h.tt("pool", QI[hq][:], Qn, idb[:], ALU.add, [Qnk, "idb"], ["QI" + q_])
                    h.mm(B[:, 256:384], QI[hq][:], TT[hq][:], True, True, ["QI" + q_, "TT" + q_], [bk])
                    h.copy(cp, TT[hq][:], B[:, 256:384], [], [bk, "TT" + q_])
                    if k < 5:
                        P_, Pk = Pn, Pnk
                    Q_, Qk = Qn, Qnk
                    yield
                h.mm(B[:, 0:64], X2[hq][:, 0:128], Vb[:, cols], True, True, ["X2" + q_, "Vb"], [bk])
                h.copy(cp, AVs[hq][:], B[:, 0:64], [], [bk, "AVs" + q_])
                h.mm(B[:, 64:128], TT[hq][:], Ab[:, cols], True, True, ["TT" + q_, "Ab"], [bk])
                h.copy(cp, Wt[hq][:], B[:, 64:128], [], [bk, "Wt" + q_])
                yield
                h.mm(B[:, 0:64], TT[hq][:], AVs[hq][:], True, True, ["TT" + q_, "AVs" + q_], [bk])
                h.copy(cp, U0b[hq][:], B[:, 0:64], [], [bk, "U0b" + q_])
                h.mm(B[hs, 128:256], Wt[hq][:], X1[hq][:, 128:256], True, True, ["Wt" + q_, "X1" + q_], [bk])
                h.tt("dve", R2T[hq][hs, :], B[hs, 128:256], ART[hs, hb, 1, :], ALU.add, ["ART"], [bk, "R2T" + q_])
                yield
                h.mm(B[:, 0:64], X1[hq][:, 128:256], U0b[hq][:], True, False, ["X1" + q_, "U0b" + q_], [bk])
                h.mm(B[:, 0:64], X2[hq][:, 128:256], Vb[:, cols], False, True, ["X2" + q_, "Vb"], [bk])
                h.copy(cp, Y0s[hq][:], B[:, 0:64], [], [bk, "Y0s" + q_])
                for cidx in range(2):
                    c0_ = cidx * 64; cs = slice(c0_, c0_ + 64)
                    h.mm(B[hs, 64 + cidx * 64:128 + cidx * 64], Wt[hq][cs, :], Bh[cs, cols], True, True, ["Wt" + q_, "Bh"], [bk])
                    h.copy(cp, MTc[hq][hs, cidx, :], B[hs, 64 + cidx * 64:128 + cidx * 64], [], [bk, "MTc" + q_])
                yield
                for cidx in range(2):
                    c0_ = cidx * 64; cs = slice(c0_, c0_ + 64)
                    h.mm(B[hs, 256 + cidx * 64:320 + cidx * 64], Bh[cs, cols], U0b[hq][cs, :], True, False, ["Bh", "U0b" + q_], [bk])
                    h.mm(B[hs, 256 + cidx * 64:320 + cidx * 64], Kh[cs, cols], Vb[cs, cols], False, True, ["Kh", "Vb"], [bk])
                    h.copy(cp, Z0c[hq][hs, cidx, :], B[hs, 256 + cidx * 64:320 + cidx * 64], [], [bk, "Z0c" + q_])
                yield
                sk = "ST%d" % hd
                for cidx in range(2):
                    c0_ = cidx * 64; cs = slice(c0_, c0_ + 64)
                    h.mm(B[cs, 0:64], R2T[hq][hs, cs], STb[hs, hb, :], True, True, ["R2T" + q_, sk + "b"], [bk])
                    h.tt("dve", yt[cs, cols], B[cs, 0:64], Y0s[hq][cs, :], ALU.add, ["Y0s" + q_], [bk, "ytile"])
                    h.mm(B[hs, 64:128], MTc[hq][hs, cidx, :], STb[hs, hb, :], True, True, ["MTc" + q_, sk + "b"], [bk])
                    h.stt(STf[hs, hb, :], STf[hs, hb, :], Gf[hs, hb, cidx:cidx + 1], B[hs, 64:128], ALU.mult, ALU.add, [sk + "f", "Gf"], [bk, sk + "f"])
                    h.tt("pool", STf[hs, hb, :], STf[hs, hb, :], Z0c[hq][hs, cidx, :], ALU.add, [sk + "f", "Z0c" + q_], [sk + "f"])
                    h.copy("pool", STb[hs, hb, :], STf[hs, hb, :], [sk + "f"], [sk + "b"])
                    yield

            ntile_c = NTILE if stop_after != "Cshort" else 2
            for i in range(ntile_c):
                j = i % 2; rows = slice(i * 128, (i + 1) * 128)
                if i % 16 == 0:
                    h.memset("dve", STf[:], 0.0, ["ST%df" % q for q in range(16)])
                    h.memset("pool", STb[:], 0.0, ["ST%db" % q for q in range(16)])
                for n in ("r", "w", "k", "v", "a", "b"):
                    h.dma(ld[n][j][:], sc[n][rows, :], [], ["l%s%d" % (n, j)])
                lr, lw_, lk, lv, la, lb = (ld[n][j] for n in ("r", "w", "k", "v", "a", "b"))
                for hf in range(2):
                    cs_ = slice(hf * 512, (hf + 1) * 512)
                    h.mm(PB[hf][:, :], ltri[:], lw_[:, cs_], True, True, ["cC", "lw%d" % j], ["PB%d" % hf])
                    h.mm(PB[2 + hf][:, :], lblk[:], lw_[:, cs_], True, True, ["cC", "lw%d" % j], ["PB%d" % (2 + hf)])
                for hf in range(2):
                    cs_ = slice(hf * 512, (hf + 1) * 512); bk = "PB%d" % hf; bk2 = "PB%d" % (2 + hf)
                    h.act(ep[:, cs_], PB[hf][:, :], AF.Exp, [], [bk, "ep"])
                    h.act(em[:, cs_], PB[hf][:, :], AF.Exp, [], [bk, "em"], scale=-1.0)
                    h.copy("act", cumS[:, cs_], PB[hf][:, :], [], [bk, "cumS"])
                    h.act(gE[:, cs_], PB[2 + hf][:, :], AF.Exp, [], [bk2, "gE"])
                    h.tt("dve", eE[:, cs_], PB[2 + hf][:, :], cumS[:, cs_], ALU.subtract, ["cumS"], [bk2, "eE"])
                h.tt("dve", eA[:], cumS[:], lw_[:], ALU.subtract, ["cumS", "lw%d" % j], ["eA"])
                h.act(eA[:], eA[:], AF.Exp, ["eA"], ["eA"])
                h.act(eE[:], eE[:], AF.Exp, ["eE"], ["eE"])
                h.tt("dve", Rb[:], lr[:], ep[:], ALU.mult, ["lr%d" % j, "ep"], ["Rb"])
                h.tt("pool", Ab[:], la[:], eA[:], ALU.mult, ["la%d" % j, "eA"], ["Ab"])
                h.tt("dve", Kb[:], lk[:], em[:], ALU.mult, ["lk%d" % j, "em"], ["Kb"])
                h.tt("pool", Bb[:], lb[:], em[:], ALU.mult, ["lb%d" % j, "em"], ["Bb"])
                h.tt("dve", Kh[:], lk[:], eE[:], ALU.mult, ["lk%d" % j, "eE"], ["Kh"])
                h.tt("pool", Bh[:], lb[:], eE[:], ALU.mult, ["lb%d" % j, "eE"], ["Bh"])
                h.copy("pool", Vb[:], lv[:], ["lv%d" % j], ["Vb"])
                for hb in range(8):
                    h.mm(PB[4][:, hb * 2:hb * 2 + 2], gE[:, hb * 128:(hb + 1) * 128], idf[:, 0:128:64], True, True, ["gE", "idf"], ["PB4"])
                h.copy("dve", Gf[:].rearrange("p a b -> p (a b)"), PB[4][:, 0:16], [], ["PB4", "Gf"])
                for src, sk_, dst, dk in ((Ab, "Ab", ART[:, :, 0, :], "ART"), (Rb, "Rb", ART[:, :, 1, :], "ART"), (Kb, "Kb", KT[:], "KTc"), (Bb, "Bb", BT[:], "BTc")):
                    for kc in range(8):
                        h.tr(PT[:, kc, :], src[:, kc * 128:(kc + 1) * 128], idb[:], [sk_, "idb"], ["PTc"])
                    h.copy("act", dst, PT[:], [], ["PTc", dk])
                for g0 in range(0, 16, NG):
                    gens = [head_gen(g0 + q, q, ytile[j]) for q in range(NG)]
                    alive = list(gens)
                    while alive:
                        nxt = []
                        for g_ in alive:
                            try:
                                next(g_)
                                nxt.append(g_)
                            except StopIteration:
                                pass
                        alive = nxt
                h.dma(sc["y"][rows, :], ytile[j][:], ["ytile"], ["sc_y"])
            end_phase()

        with contextlib.ExitStack() as es:
            LNG = sbt(es, "LNG", [128, D]); LNB = sbt(es, "LNB", [128, D]); RKb = sbt(es, "RKb", [128, D])
            ld = {n: [sbt(es, "d%s%d" % (n, i), [128, D]) for i in range(2)] for n in ("y", "r", "k", "v", "g")}
            yc = sbt(es, "yc", [128, D]); sq2 = sbt(es, "sq2", [128, D]); prod = sbt(es, "prod", [128, D])
            m16 = sbt(es, "m16", [128, 16]); v16 = sbt(es, "v16", [128, 16]); b16 = sbt(es, "b16", [128, 16])
            yb = [sbt(es, "yb%d" % i, [128, D], BF16) for i in range(2)]
            yf = sbt(es, "yf", [128, D])
            bcast_load(LNG[:], lnx_g[0:1, :], ["cD"]); bcast_load(LNB[:], lnx_b[0:1, :], ["cD"]); bcast_load(RKb[:], r_k[0:1, :], ["cD"])
            v3 = lambda t_: t_[:].rearrange("p (a b) -> p a b", a=16)
            for i in range(NTILE):
                j = i % 2; rows = slice(i * 128, (i + 1) * 128)
                for n in ("y", "r", "k", "v", "g"):
                    h.dma(ld[n][j][:], sc[n][rows, :], [], ["d%s%d" % (n, j)])
                yt, rt, kt, vt, gt_ = (ld[n][j] for n in ("y", "r", "k", "v", "g"))
                h.red(m16[:], v3(yt), ALU.add, ["dy%d" % j], ["m16"])
                h.ts("dve", m16[:], m16[:], 1.0 / 64, None, ALU.mult, None, ["m16"], ["m16"])
                h.tt("dve", v3(yc), v3(yt), bcl(m16[:], 64), ALU.subtract, ["dy%d" % j, "m16"], ["yc"])
                h.tt("pool", sq2[:], yc[:], yc[:], ALU.mult, ["yc"], ["sq2"])
                h.red(v16[:], v3(sq2), ALU.add, ["sq2"], ["v16"])
                h.ts("dve", v16[:], v16[:], 1.0 / 64, GN_EPS, ALU.mult, ALU.add, ["v16"], ["v16"])
                h.act(v16[:], v16[:], AF.Sqrt, ["v16"], ["v16"])
                h.recip(v16[:], v16[:], ["v16"], ["v16"])
                h.tt("dve", v3(yc), v3(yc), bcl(v16[:], 64), ALU.mult, ["yc", "v16"], ["yc"])
                h.tt("pool", yc[:], yc[:], LNG[:], ALU.mult, ["yc", "cD"], ["yc"])
                h.tt("pool", yc[:], yc[:], LNB[:], ALU.add, ["yc", "cD"], ["yc"])
                h.tt("pool", prod[:], rt[:], kt[:], ALU.mult, ["dr%d" % j, "dk%d" % j], ["prod"])
                h.tt("pool", prod[:], prod[:], RKb[:], ALU.mult, ["prod", "cD"], ["prod"])
                h.red(b16[:], v3(prod), ALU.add, ["prod"], ["b16"])
                h.tt("dve", v3(prod), v3(vt), bcl(b16[:], 64), ALU.mult, ["dv%d" % j, "b16", "prod"], ["prod"])
                h.tt("dve", yc[:], yc[:], prod[:], ALU.add, ["yc", "prod"], ["yc"])
                h.tt("dve", yb[j][:], yc[:], gt_[:], ALU.mult, ["yc", "dg%d" % j], ["yb%d" % j])
                h.dma(yrs[rows, :], yb[j][:], ["yb%d" % j], ["yrs"])
                if "yr" in dbg_out:
                    h.tt("pool", yf[:], yc[:], gt_[:], ALU.mult, ["yc", "dg%d" % j], ["yf"])
                    h.dma(dbg_out["yr"][rows, :], yf[:], ["yf"], ["dbg"])
            end_phase()
        if stop_after in ("D", "Cshort"):
            return nc, dbg_out, S

        def load_w_bf16(es, dst, src, kcs, ncols, prows=128):
            stg = [sbt(es, "stg%d" % i, [128, 8, 512]) for i in range(2)]
            ci = 0
            for nb in range(ncols // 512):
                j = ci % 2; ci += 1
                h.dma(stg[j][0:prows, 0:kcs, :], src[:, nb * 512:(nb + 1) * 512].rearrange("(kc p) n -> p kc n", p=prows), [], ["stg%d" % j])
                h.copy("pool", dst[:, :, nb * 512:(nb + 1) * 512], stg[j][0:prows, 0:kcs, :], ["stg%d" % j], ["wconst"])

        with contextlib.ExitStack() as es:
            WBR = sbt(es, "WBR", [128, 8, D], BF16); WBA = sbt(es, "WBA", [64, 4, D], BF16); WO = sbt(es, "WO", [128, 8, D], BF16)
            with contextlib.ExitStack() as es1:
                load_w_bf16(es1, WBR, w_br_rwkv, 8, D)
                load_w_bf16(es1, WO, w_out, 8, D)
                load_w_bf16(es1, WBA, w_br_attn, 4, D, prows=64)
                S.barrier()
            GT1 = [sbt(es, "GT1_%d" % b, [128, D]) for b in range(NB)]
            G2n = [sbt(es, "G2n_%d" % b, [128, D]) for b in range(NB)]
            SH2 = [sbt(es, "SH2_%d" % b, [128, D]) for b in range(NB)]
            tmpg = sbt(es, "tmpg2", [128, D])
            bcast_load(tmpg[:], norm2_g[0:1, :], ["tmpg"])
            for b in range(NB):
                bcast_load(GT1[b][:], modd[b:b + 1, 2048:3072], ["cF"])
                bcast_load(SH2[b][:], modd[b:b + 1, 3072:4096], ["cF"])
                bcast_load(G2n[b][:], modd[b:b + 1, 4096:5120], ["G2n%d" % b])
                h.tt("dve", G2n[b][:], G2n[b][:], tmpg[:], ALU.mult, ["G2n%d" % b, "tmpg"], ["G2n%d" % b])
            ybl = [sbt(es, "ybl%d" % i, [128, D], BF16) for i in range(2)]
            yrT = sbt(es, "yrT", [128, 8, 128], BF16); mgT = sbt(es, "mgT", [128, 8, 128], BF16)
            zg = [sbt(es, "zg%d" % i, [128, 2048]) for i in range(2)]
            t1 = sbt(es, "t1", [128, D]); t2 = sbt(es, "t2", [128, D]); mgb = sbt(es, "mgb", [128, D], BF16)
            xt = [sbt(es, "xtF%d" % i, [128, D]) for i in range(2)]
            ht = [sbt(es, "htF%d" % i, [128, D]) for i in range(2)]
            junk = sbt(es, "junkF", [128, D]); n2b = [sbt(es, "n2bF%d" % i, [128, D], BF16) for i in range(2)]
            ss = sbt(es, "ssF", [128, NTILE]); rs = sbt(es, "rsF", [128, NTILE])
            ptr = pst(es, "ptrF", [128, 8, 128], BF16)
            pm1 = pst(es, "pm1", [128, D]); pm2 = pst(es, "pm2", [128, D]); po = pst(es, "poF", [128, D])
            for i in range(NTILE):
                b = i // 16; j = i % 2; rows = slice(i * 128, (i + 1) * 128)
                h.dma(ybl[j][:], yrs[rows, :], [], ["ybl%d" % j])
                h.dma(zg[j][:], zs[rows, 5632:7680], [], ["zg%d" % j])
                h.dma(xt[j][:], x[rows, :], [], ["xtF%d" % j])
                for kc in range(8):
                    h.tr(ptr[:, kc, :], ybl[j][:, kc * 128:(kc + 1) * 128], idb[:], ["ybl%d" % j, "idb"], ["ptrF"])
                h.copy("act", yrT[:], ptr[:], ["ptrF"], ["yrT"])
                for nb in range(2):
                    cs_ = slice(nb * 512, (nb + 1) * 512)
                    for kc in range(8):
                        h.mm(pm1[:, cs_], yrT[:, kc, :], WBR[:, kc, cs_], kc == 0, kc == 7, ["yrT"], ["pm1"])
                    for hh in range(4):
                        h.mm(pm2[:, cs_], yaT[:, hh, rows], WBA[:, hh, cs_], hh == 0, hh == 3, [], ["pm2"])
                h.act(zg[j][:], zg[j][:], AF.Sigmoid, ["zg%d" % j], ["zg%d" % j])
                h.tt("dve", t1[:], pm1[:], zg[j][:, 0:1024], ALU.mult, ["pm1", "zg%d" % j], ["t1"])
                h.tt("dve", t2[:], pm2[:], zg[j][:, 1024:2048], ALU.mult, ["pm2", "zg%d" % j], ["t2"])
                h.tt("pool", mgb[:], t1[:], t2[:], ALU.add, ["t1", "t2"], ["mgb"])
                for kc in range(8):
                    h.tr(ptr[:, kc, :], mgb[:, kc * 128:(kc + 1) * 128], idb[:], ["mgb", "idb"], ["ptrF"])
                h.copy("act", mgT[:], ptr[:], ["ptrF"], ["mgT"])
                for nb in range(2):
                    cs_ = slice(nb * 512, (nb + 1) * 512)
                    for kc in range(8):
                        h.mm(po[:, cs_], mgT[:, kc, :], WO[:, kc, cs_], kc == 0, kc == 7, ["mgT"], ["poF"])
                h.tt("dve", t1[:], po[:], GT1[b][:], ALU.mult, ["poF", "cF"], ["t1"])
                h.tt("pool", ht[j][:], t1[:], xt[j][:], ALU.add, ["t1", "xtF%d" % j], ["htF%d" % j])
                h.dma(hs[rows, :], ht[j][:], ["htF%d" % j], ["hs"])
                if "h" in dbg_out:
                    h.dma(dbg_out["h"][rows, :], ht[j][:], ["htF%d" % j], ["dbg"])
                h.act(junk[:], ht[j][:], AF.Square, ["htF%d" % j], ["junkF", "ssF%d" % i], accum=ss[:, i:i + 1])
                h.ts("dve", rs[:, i:i + 1], ss[:, i:i + 1], 1.0 / D, EPS, ALU.mult, ALU.add, ["ssF%d" % i], ["rsF%d" % i])
                h.act(rs[:, i:i + 1], rs[:, i:i + 1], AF.Sqrt, ["rsF%d" % i], ["rsF%d" % i])
                h.recip(rs[:, i:i + 1], rs[:, i:i + 1], ["rsF%d" % i], ["rsF%d" % i])
                h.stt(t2[:], ht[j][:], rs[:, i:i + 1], G2n[b][:], ALU.mult, ALU.mult, ["htF%d" % j, "rsF%d" % i, "G2n%d" % b], ["t2"])
                h.tt("pool", n2b[j][:], t2[:], SH2[b][:], ALU.add, ["t2", "cF"], ["n2bF%d" % j])
                h.dma(n2s[rows, :], n2b[j][:], ["n2bF%d" % j], ["n2s"])
            end_phase()
        if stop_after == "F":
            return nc, dbg_out, S

        with contextlib.ExitStack() as es:
            WQ = sbt(es, "WQ", [128, 8, 2048], BF16)
            with contextlib.ExitStack() as es1:
                load_w_bf16(es1, WQ, peer_wq, 8, 2048)
                S.barrier()
            with contextlib.ExitStack() as es1:
                cf = [sbt(es1, "cf%d" % i, [128, 4, D]) for i in range(3)]
                cb = [sbt(es1, "cb%d" % i, [128, 4, D], BF16) for i in range(3)]
                ci = 0
                for src, off_ in ((peer_u, 0), (peer_v, D)):
                    s3 = src.rearrange("(p r) d -> p r d", p=128); d3 = uvb.rearrange("(p r) d -> p r d", p=128)[:, :, off_:off_ + D]
                    for ch in range(32):
                        j = ci % 3; ci += 1
                        h.dma(cf[j][:], s3[:, ch * 4:(ch + 1) * 4, :], [], ["cf%d" % j])
                        h.copy(("act", "pool", "dve")[j], cb[j][:], cf[j][:], ["cf%d" % j], ["cb%d" % j])
                        h.dma(d3[:, ch * 4:(ch + 1) * 4, :], cb[j][:], ["cb%d" % j], ["ubvb"])
                S.barrier()
            K1T = sbt(es, "K1T", [128, 128]); K2T = sbt(es, "K2T", [128, 128]); ktmp = sbt(es, "ktmp", [128, 128])
            GT2 = [sbt(es, "GT2_%d" % b, [128, D]) for b in range(NB)]
            n2b = sbt(es, "n2bG", [128, D], BF16); n2Tt = sbt(es, "n2Tt", [128, 8, 128], BF16)
            qTt = sbt(es, "qTt", [128, 16, 128])
            S1 = sbt(es, "S1", [128, 8, 128]); S2 = sbt(es, "S2", [128, 8, 128]); wk = sbt(es, "wk", [128, 128])
            v1 = sbt(es, "v1", [128, 8, 16]); v2 = sbt(es, "v2", [128, 8, 16])
            i1u = sbt(es, "i1u", [128, 8, 16], U32); i2u = sbt(es, "i2u", [128, 8, 16], U32)
            i1f = sbt(es, "i1f", [128, 8, 16]); i2f = sbt(es, "i2f", [128, 8, 16])
            cand = sbt(es, "cand", [128, 16, 16]); cidx = sbt(es, "cidx", [128, 16, 16]); wk2 = sbt(es, "wk2", [128, 256]); junk2 = sbt(es, "junk2", [128, 256])
            t16 = sbt(es, "t16", [128, 8, 16]); idxf = sbt(es, "idxf", [128, 128]); e16 = sbt(es, "e16", [128, 8, 16]); gate = sbt(es, "gate", [128, 128])
            nmx = sbt(es, "nmx", [128, 8]); Zs = sbt(es, "Zs", [128, 8])
            posu = sbt(es, "posu", [128, 8, 16], U32); au = sbt(es, "au", [128, 8, 16], U32); bu = sbt(es, "bu", [128, 8, 16], U32)
            af = sbt(es, "af", [128, 8, 16]); bf_ = sbt(es, "bf_", [128, 8, 16]); ia = sbt(es, "ia", [128, 8, 16]); ib = sbt(es, "ib", [128, 8, 16])
            eqa = sbt(es, "eqa", [128, 16, 16]); eqb = sbt(es, "eqb", [128, 16, 16]); iota16 = sbt(es, "iota16", [128, 16])
            h.dma(iota16[:], iota16_d, [], ["cG"])
            IDXT = sbt(es, "IDXT", [128, 128], I32); GTt = sbt(es, "GTt", [128, 128]); dots = sbt(es, "dots", [128, 128]); coef = sbt(es, "coef", [128, 128])
            NUB = 10
            UV = [sbt(es, "UV%d" % i, [128, 2 * D], BF16) for i in range(NUB)]
            glu = sbt(es, "glu", [128, 128])
            junkU = sbt(es, "junkU", [128, D], BF16)
            IX1 = [sbt(es, "IX1_%d" % i, [128, 1], I32) for i in range(NUB)]
            coefb = sbt(es, "coefb", [128, 128], BF16)
            WB = [sbt(es, "WB%d" % i, [128, 256], BF16) for i in range(4)]
            htG = sbt(es, "htG", [128, D]); t1 = sbt(es, "t1G", [128, D]); yo = sbt(es, "yo", [128, D])
            ptr = pst(es, "ptrG", [128, 8, 128], BF16)
            pB = pst(es, "pB", [128, 512])
            pbc = [pst(es, "pbc%d" % i, [128, D]) for i in range(2)]
            po = pst(es, "poG", [128, D])
            for b in range(NB):
                bcast_load(GT2[b][:], modd[b:b + 1, 5120:6144], ["cG"])
            for kt_, src in ((K1T, peer_k1), (K2T, peer_k2)):
                h.dma(ktmp[:], src, [], ["ktmp"])
                h.tr(pB[:, 0:128], ktmp[:], idf[:], ["ktmp", "idf"], ["pB"])
                h.copy("dve", kt_[:], pB[:, 0:128], ["pB"], ["cG"])
            for w_ in WB:
                h.memset("pool", w_[:], 0.0, ["cG"])
            ntile_g = NTILE if stop_after != "Gshort" else 1
            for i in range(ntile_g):
                b = i // 16; rows = slice(i * 128, (i + 1) * 128)
                h.dma(n2b[:], n2s[rows, :], [], ["n2bG"])
                h.dma(htG[:], hs[rows, :], [], ["htG"])
                for kc in range(8):
                    h.tr(ptr[:, kc, :], n2b[:, kc * 128:(kc + 1) * 128], idb[:], ["n2bG", "idb"], ["ptrG"])
                h.copy("act", n2Tt[:], ptr[:], ["ptrG"], ["n2Tt"])
                for c4 in range(4):
                    for cq in range(4):
                        cc = c4 * 4 + cq
                        for kc in range(8):
                            h.mm(pB[:, cq * 128:(cq + 1) * 128], WQ[:, kc, cc * 128:(cc + 1) * 128], n2Tt[:, kc, :], kc == 0, kc == 7, ["n2Tt"], ["pB"])
                    h.copy("act" if c4 % 2 == 0 else "dve", qTt[:, c4 * 4:(c4 + 1) * 4, :], pB[:].rearrange("p (a b) -> p a b", a=4), ["pB"], ["qTt"])
                for which, Sx, KT in ((0, S1, K1T), (1, S2, K2T)):
                    for h4 in range(2):
                        for hq in range(4):
                            hd = h4 * 4 + hq
                            h.mm(pB[:, hq * 128:(hq + 1) * 128], qTt[:, 2 * hd + which, :], KT[:], True, True, ["qTt", "cG"], ["pB"])
                        h.copy("act" if h4 == 0 else "dve", Sx[:, h4 * 4:(h4 + 1) * 4, :], pB[:].rearrange("p (a b) -> p a b", a=4), ["pB"], ["S%d" % which])
                import os
                GCUT = float(os.environ.get("GCUT", "9"))
                if GCUT <= 1:
                    continue
                for which, Sx, vx, ix in ((0, S1, v1, i1u), (1, S2, v2, i2u)):
                    sk = "S%d" % which; vk_ = "v%d" % which
                    for hd in range(8):
                        S.op("dve", (lambda o, i_: (lambda e: e.max(out=o, in_=i_)))(vx[:, hd, 0:8], Sx[:, hd, :]), reads=[sk], writes=[vk_])
                        S.op("dve", (lambda o, r_, i_: (lambda e: e.match_replace(out=o, in_to_replace=r_, in_values=i_, imm_value=-1e30)))(wk[:], vx[:, hd, 0:8], Sx[:, hd, :]),
                             reads=[sk, vk_], writes=["wk"])
                        S.op("dve", (lambda o, i_: (lambda e: e.max(out=o, in_=i_)))(vx[:, hd, 8:16], wk[:]), reads=["wk"], writes=[vk_])
                        S.op("dve", (lambda o, m_, i_: (lambda e: e.max_index(out=o, in_max=m_, in_values=i_)))(ix[:, hd, 0:8], vx[:, hd, 0:8], Sx[:, hd, :]),
                             reads=[sk, vk_], writes=["ix%d" % which])
                        S.op("dve", (lambda o, m_, i_: (lambda e: e.max_index(out=o, in_max=m_, in_values=i_)))(ix[:, hd, 8:16], vx[:, hd, 8:16], Sx[:, hd, :]),
                             reads=[sk, vk_], writes=["ix%d" % which])
                if GCUT <= 2:
                    continue
                h.copy("dve", i1f[:], i1u[:], ["ix0"], ["i1f"])
                h.copy("dve", i2f[:], i2u[:], ["ix1"], ["i2f"])
                h.ts("dve", i1f[:], i1f[:], 128.0, None, ALU.mult, None, ["i1f"], ["i1f"])
                cflat = cand[:].rearrange("p a b -> p (a b)"); xflat = cidx[:].rearrange("p a b -> p (a b)")
                for hd in range(8):
                    h.tt("pool", cand[:], bcl(v1[:, hd, :], 16), bc3(v2[:, hd, :], 16), ALU.add, ["v0", "v1"], ["cand"])
                    S.op("dve", (lambda o, i_: (lambda e: e.max(out=o, in_=i_)))(t16[:, hd, 0:8], cflat), reads=["cand"], writes=["t16"])
                    S.op("dve", (lambda o, r_, i_: (lambda e: e.match_replace(out=o, in_to_replace=r_, in_values=i_, imm_value=-1e30)))(wk2[:], t16[:, hd, 0:8], cflat),
                         reads=["cand", "t16"], writes=["wk2"])
                    S.op("dve", (lambda o, i_: (lambda e: e.max(out=o, in_=i_)))(t16[:, hd, 8:16], wk2[:]), reads=["wk2"], writes=["t16"])
                    S.op("dve", (lambda o, m_, i_: (lambda e: e.max_index(out=o, in_max=m_, in_values=i_)))(posu[:, hd, 0:8], t16[:, hd, 0:8], cflat),
                         reads=["cand", "t16"], writes=["posu"])
                    S.op("dve", (lambda o, m_, i_: (lambda e: e.max_index(out=o, in_max=m_, in_values=i_)))(posu[:, hd, 8:16], t16[:, hd, 8:16], cflat),
                         reads=["cand", "t16"], writes=["posu"])
                S.op("dve", (lambda o, i_: (lambda e: e.tensor_scalar(out=o, in0=i_, scalar1=4, scalar2=None, op0=ALU.logical_shift_right)))(au[:], posu[:]), reads=["posu"], writes=["au"])
                S.op("dve", (lambda o, i_: (lambda e: e.tensor_scalar(out=o, in0=i_, scalar1=15, scalar2=None, op0=ALU.bitwise_and)))(bu[:], posu[:]), reads=["posu"], writes=["bu"])
                h.copy("dve", af[:], au[:], ["au"], ["af"])
                h.copy("dve", bf_[:], bu[:], ["bu"], ["bf"])
                for hd in range(8):
                    h.tt("dve", eqa[:], bcl(af[:, hd, :], 16), bc3(iota16[:], 16), ALU.is_equal, ["af", "cG"], ["eqa"])
                    h.tt("pool", eqa[:], eqa[:], bc3(i1f[:, hd, :], 16), ALU.mult, ["eqa", "i1f"], ["eqa"])
                    h.red(ia[:, hd, :], eqa[:], ALU.add, ["eqa"], ["ia"])
                    h.tt("dve", eqb[:], bcl(bf_[:, hd, :], 16), bc3(iota16[:], 16), ALU.is_equal, ["bf", "cG"], ["eqb"])
                    h.tt("pool", eqb[:], eqb[:], bc3(i2f[:, hd, :], 16), ALU.mult, ["eqb", "i2f"], ["eqb"])
                    h.red(ib[:, hd, :], eqb[:], ALU.add, ["eqb"], ["ib"])
                h.tt("dve", idxf[:].rearrange("p (a b) -> p a b", a=8), ia[:], ib[:], ALU.add, ["ia", "ib"], ["idxf"])
                h.ts("dve", idxf[:], idxf[:], 16383.0, 0.0, ALU.min, ALU.max, ["idxf"], ["idxf"])
                if GCUT <= 2.4:
                    continue
                h.ts("dve", nmx[:], t16[:, :, 0], -1.0, None, ALU.mult, None, ["t16"], ["nmx"])
                for hd in range(8):
                    h.act(e16[:, hd, :], t16[:, hd, :], AF.Exp, ["t16", "nmx"], ["e16", "Zs"], bias=nmx[:, hd:hd + 1], accum=Zs[:, hd:hd + 1])
                h.recip(Zs[:], Zs[:], ["Zs"], ["Zs"])
                h.tt("dve", gate[:].rearrange("p (a b) -> p a b", a=8), e16[:], bcl(Zs[:], 16), ALU.mult, ["e16", "Zs"], ["gate"])
                if GCUT <= 2.6:
                    continue
                h.tr(pB[:, 0:128], idxf[:], idf[:], ["idxf", "idf"], ["pB"])
                h.tr(pB[:, 128:256], gate[:], idf[:], ["gate", "idf"], ["pB"])
                if GCUT <= 2.7:
                    continue
                h.ts("dve", dots[:], pB[:, 0:128], 8388608.0, None, ALU.add, None, ["pB"], ["dots"])
                S.op("dve", (lambda o, i_: (lambda e: e.tensor_scalar(out=o, in0=i_, scalar1=0x7FFFFF, scalar2=None, op0=ALU.bitwise_and)))(IDXT[:], dots[:].bitcast(I32)), reads=["dots"], writes=["IDXT"])
                if GCUT <= 2.8:
                    continue
                h.copy("dve", GTt[:], pB[:, 128:256], ["pB"], ["GTt"])
                import os
                GCUT = float(os.environ.get("GCUT", "9"))
                if "idxf" in dbg_out and i == 0:
                    h.dma(dbg_out["idxf"], idxf[:], ["idxf"], ["dbg"])
                    h.dma(dbg_out["gate"], gate[:], ["gate"], ["dbg"])
                    h.dma(dbg_out["GTt"], GTt[:], ["GTt"], ["dbg"])
                    h.copy("dve", dots[:], IDXT[:], ["IDXT"], ["dots"])
                    h.dma(dbg_out["IDXTf"], dots[:], ["dots"], ["dbg"])
                if GCUT <= 3:
                    continue
                LA = NUB - 2

                def issue_gather(c2):
                    j2 = c2 % NUB
                    h.copy("dve", IX1[j2][:], IDXT[:, c2:c2 + 1], ["IDXT"], ["IX%d" % j2])
                    S.dma("pool", (lambda o, ix_: (lambda e: e.indirect_dma_start(out=o, out_offset=None, in_=uvb,
                                                                                  in_offset=bass.IndirectOffsetOnAxis(ap=ix_, axis=0))))(UV[j2][:], IX1[j2][:, 0:1]),
                          reads=["IX%d" % j2], writes=["UV%d" % j2])
                def emit_out(c3):
                    j3 = c3 % NUB; jw3 = c3 % 4
                    h.ts("dve", WB[jw3][:, 127:128], glu[:, c3:c3 + 1], GTt[:, c3:c3 + 1], None, ALU.mult, None, ["glu%d" % (c3 % 8), "GTt"], ["WB%d" % jw3])
                    for hb3 in range(2):
                        h.mm(po[:, hb3 * 512:(hb3 + 1) * 512], WB[jw3][:, 127 - c3:255 - c3], UV[j3][:, D + hb3 * 512:D + (hb3 + 1) * 512], c3 == 0, c3 == 127,
                             ["WB%d" % jw3, "UV%d" % j3], ["poG"])
                def emit_bcast(c4):
                    jb4 = c4 % 2
                    for hb4 in range(2):
                        cs4 = slice(hb4 * 512, (hb4 + 1) * 512)
                        h.mm(pbc[jb4][:, cs4], idb[:, c4:c4 + 1].to_broadcast([128, 128]), n2b[:, cs4], True, True, ["n2bG", "idb"], ["pbc%d" % jb4])
                for c2 in range(LA):
                    issue_gather(c2)
                emit_bcast(0)
                for c in range(128):
                    ju = c % NUB; jb = c % 2; jw = c % 4
                    if c + LA < 128:
                        issue_gather(c + LA)
                    if c + 1 < 128:
                        emit_bcast(c + 1)
                    h.stt(junkU[:], UV[ju][:, 0:D], 1.0, pbc[jb][:], ALU.mult, ALU.mult, ["UV%d" % ju, "pbc%d" % jb], ["junkU", "dots%d" % (c % 8)], accum=dots[:, c:c + 1])
                    h.act(glu[:, c:c + 1], dots[:, c:c + 1], AF.Gelu, ["dots%d" % (c % 8)], ["glu%d" % (c % 8)])
                    if c >= 1:
                        emit_out(c - 1)
                emit_out(127)
                h.tt("dve", t1[:], po[:], GT2[b][:], ALU.mult, ["poG", "cG"], ["t1G"])
                h.tt("pool", yo[:], t1[:], htG[:], ALU.add, ["t1G", "htG"], ["yo"])
                last_tok[0] = h.dma(y_out[rows, :], yo[:], ["yo"], ["y"])
            end_phase()
        return nc, dbg_out, S


def host_consts():
    ident = np.eye(128, dtype=np.float32)
    half = 8
    inv = (500000.0 ** (-np.arange(half, dtype=np.float32) / half)).astype(np.float32)
    pos = np.arange(SEQ, dtype=np.float32)
    ang = (pos[:, None] * inv[None, :]).astype(np.float32)
    cs = np.concatenate([np.cos(ang), np.sin(ang)], axis=1).astype(np.float32)
    cs = cs.reshape(16, 128, 16).transpose(1, 0, 2).copy()
    k = np.arange(128)[:, None]; q = np.arange(128)[None, :]
    mcur = (k <= q).astype(np.float32); mprev = (k >= q).astype(np.float32)
    same = (k // 64) == (q // 64)
    ltri = (same & (k <= q)).astype(np.float32)
    lblk = same.astype(np.float32)
    m256 = np.concatenate([(same & (k < q)), (same & (k <= q))], axis=1).astype(np.float32)
    mL = (same & (k > q)).astype(np.float32)
    iota16 = np.tile(np.arange(16, dtype=np.float32)[None, :], (128, 1))
    return dict(iota16=iota16, ident=ident, cs=cs, mcur=mcur, mprev=mprev, ltri=ltri, lblk=lblk, m256=m256, mL=mL)


def make_in_maps(inputs, n_cores=8):
    consts = host_consts()
    shared = {}
    for k in ("w_ada", "w_in", "w2", "a2", "g2", "w_br_rwkv", "w_br_attn", "w_out", "peer_wq", "peer_k1", "peer_k2",
              "peer_u", "peer_v", "q_norm_g", "k_norm_g"):
        shared[k] = np.ascontiguousarray(inputs[k][0], dtype=np.float32)
    for k in ("b_ada", "norm1_g", "rwkv_mu", "w0", "a0", "k_k", "k_a", "lnx_g", "lnx_b", "norm2_g"):
        shared[k] = np.ascontiguousarray(inputs[k][0].reshape(1, -1), dtype=np.float32)
    shared["r_k"] = np.ascontiguousarray(inputs["r_k"][0].reshape(1, -1), dtype=np.float32)
    shared.update(consts)
    maps = []
    for c in range(n_cores):
        m = dict(shared)
        m["x"] = np.ascontiguousarray(inputs["x"][c * NB:(c + 1) * NB].reshape(NT, D), dtype=np.float32)
        cc = np.asarray(inputs["c"][c * NB:(c + 1) * NB], dtype=np.float32)
        m["cT"] = np.ascontiguousarray(cc.reshape(NB, 8, 128).transpose(2, 1, 0))
        maps.append(m)
    return maps


def kernel(**inputs):
    nc, _, _ = build_core()
    maps = make_in_maps(inputs, 8)
    res = run_bass_kernel_spmd(nc, maps, core_ids=list(range(8)))
    outs = [np.asarray(r["y"]).reshape(NB, SEQ, D) for r in res.results]
    return np.concatenate(outs, axis=0).astype(np.float32)
```

```python
import contextlib
import numpy as np
import ml_dtypes
import concourse.bass as bass
import concourse.mybir as mybir
from concourse.bass_utils import run_bass_kernel_spmd

F32 = mybir.dt.float32
BF16 = mybir.dt.bfloat16
U32 = mybir.dt.uint32
I32 = mybir.dt.int32
AF = mybir.ActivationFunctionType
ALU = mybir.AluOpType
AX = mybir.AxisListType

ENGS = ("pe", "act", "dve", "pool", "sp")
NDMA = {"sp": 8, "pool": 16, "act": 4}

D = 1024
SEQ = 2048
NB = 2
NT = NB * SEQ
NTILE = NT // 128
ZC = 7680
EPS = 1e-6
GN_EPS = 64e-5


class Sched:
    def __init__(self, nc, sems):
        self.nc = nc
        self.sems = sems
        self.ops = {e: [] for e in ENGS}
        self.cnt = {e: 0 for e in ENGS}
        self.dma_slot_cnt = {e: [0] * n for e, n in NDMA.items()}
        self.dma_rr = {e: 0 for e in NDMA}
        self.state = {}
        self.waited = {e: {} for e in ENGS}

    def _deps(self, eng, reads, writes):
        need = {}

        def add(tok):
            if tok is None:
                return
            s, v, e = tok
            if e == "pe" and eng == "pe" and s == "c_pe":
                return
            if need.get(s, 0) < v:
                need[s] = v
        for k in reads:
            st = self.state.get(k)
            if st:
                add(st[0])
        for k in writes:
            st = self.state.get(k)
            if st:
                add(st[0])
                for t in st[1].values():
                    add(t)
        out = []
        w = self.waited[eng]
        for s, v in need.items():
            if w.get(s, 0) < v:
                w[s] = v
                out.append((s, v))
        return out

    def _commit(self, tok, reads, writes):
        for k in reads:
            st = self.state.setdefault(k, [None, {}])
            st[1][tok[0]] = tok
        for k in writes:
            self.state[k] = [tok, {}]

    def op(self, eng, fn, reads=(), writes=()):
        waits = self._deps(eng, reads, writes)
        self.cnt[eng] += 1
        tok = ("c_" + eng, self.cnt[eng], eng)
        self.ops[eng].append(("op", fn, waits, tok))
        self._commit(tok, reads, writes)
        return tok

    def dma(self, eng, fn, reads=(), writes=()):
        slot = self.dma_rr[eng]
        self.dma_rr[eng] = (slot + 1) % NDMA[eng]
        sname = "d_%s_%d" % (eng, slot)
        waits = self._deps(eng, reads, writes)
        prev = self.dma_slot_cnt[eng][slot]
        w = self.waited[eng]
        if prev > 0 and w.get(sname, 0) < prev:
            w[sname] = prev
            waits.append((sname, prev))
        self.dma_slot_cnt[eng][slot] = prev + 16
        tok = (sname, prev + 16, eng)
        self.ops[eng].append(("dma", fn, waits, tok))
        self._commit(tok, reads, writes)
        return tok

    def barrier(self):
        toks = []
        for e in ENGS:
            if e != "sp" and self.cnt[e] > 0:
                toks.append(("c_" + e, self.cnt[e]))
        for e, n in NDMA.items():
            for i in range(n):
                if self.dma_slot_cnt[e][i] > 0:
                    toks.append(("d_%s_%d" % (e, i), self.dma_slot_cnt[e][i]))
        for e in ENGS:
            w = self.waited[e]
            ws = []
            for s, v in toks:
                if w.get(s, 0) < v:
                    w[s] = v
                    ws.append((s, v))
            if ws:
                self.ops[e].append(("wait", None, ws, None))
        self.state = {}

    def emit(self):
        nc = self.nc
        sems = self.sems
        ops = self.ops
        self.ops = {e: [] for e in ENGS}
        with nc.Block() as block:
            def run(engname):
                def body(engine):
                    for kind, fn, waits, tok in ops[engname]:
                        for s, v in waits:
                            engine.wait_ge(sems[s], v)
                        if kind == "wait":
                            continue
                        ins = fn(engine)
                        ins.then_inc(sems[tok[0]], 16 if kind == "dma" else 1)
                return body
            block.tensor(run("pe"))
            block.scalar(run("act"))
            block.vector(run("dve"))
            block.gpsimd(run("pool"))
            block.sync(run("sp"))


def sem_names():
    names = ["c_" + e for e in ENGS if e != "sp"]
    for e, n in NDMA.items():
        names += ["d_%s_%d" % (e, i) for i in range(n)]
    return names


class H:
    def __init__(self, S):
        self.S = S

    def dma(self, out, in_, r, w, eng="sp"):
        return self.S.dma(eng, lambda e: e.dma_start(out=out, in_=in_), reads=r, writes=w)

    def tt(self, eng, out, in0, in1, op, r, w):
        return self.S.op(eng, lambda e: e.tensor_tensor(out=out, in0=in0, in1=in1, op=op), reads=r, writes=w)

    def ts(self, eng, out, in0, s1, s2, op0, op1, r, w, accum=None):
        if op1 is None:
            return self.S.op(eng, lambda e: e.tensor_scalar(out=out, in0=in0, scalar1=s1, scalar2=None, op0=op0), reads=r, writes=w)
        if accum is None:
            return self.S.op(eng, lambda e: e.tensor_scalar(out=out, in0=in0, scalar1=s1, scalar2=s2, op0=op0, op1=op1), reads=r, writes=w)
        return self.S.op(eng, lambda e: e.tensor_scalar(out=out, in0=in0, scalar1=s1, scalar2=s2, op0=op0, op1=op1, accum_out=accum), reads=r, writes=w)

    def stt(self, out, in0, scalar, in1, op0, op1, r, w, accum=None):
        if accum is None:
            return self.S.op("dve", lambda e: e.scalar_tensor_tensor(out=out, in0=in0, scalar=scalar, in1=in1, op0=op0, op1=op1), reads=r, writes=w)
        return self.S.op("dve", lambda e: e.scalar_tensor_tensor(out=out, in0=in0, scalar=scalar, in1=in1, op0=op0, op1=op1, accum_out=accum), reads=r, writes=w)

    def copy(self, eng, out, in_, r, w):
        if eng == "act":
            return self.S.op("act", lambda e: e.copy(out=out, in_=in_), reads=r, writes=w)
        return self.S.op(eng, lambda e: e.tensor_copy(out=out, in_=in_), reads=r, writes=w)

    def act(self, out, in_, func, r, w, scale=1.0, bias=None, accum=None):
        def fn(e):
            kw = dict(out=out, in_=in_, func=func, scale=scale)
            if bias is not None:
                kw["bias"] = bias
            if accum is not None:
                kw["accum_out"] = accum
            return e.activation(**kw)
        return self.S.op("act", fn, reads=r, writes=w)

    def red(self, out, in_, op, r, w):
        return self.S.op("dve", lambda e: e.tensor_reduce(out=out, in_=in_, axis=AX.X, op=op), reads=r, writes=w)

    def recip(self, out, in_, r, w):
        return self.S.op("dve", lambda e: e.reciprocal(out=out, in_=in_), reads=r, writes=w)

    def memset(self, eng, ap, val, w):
        return self.S.op(eng, lambda e: e.memset(ap, val), reads=(), writes=w)

    def mm(self, out, lhsT, rhs, start, stop, r, w):
        return self.S.op("pe", lambda e: e.matmul(out, lhsT, rhs, start=start, stop=stop), reads=r, writes=w)

    def tr(self, out, in_, ident, r, w):
        return self.S.op("pe", lambda e: e.transpose(out=out, in_=in_, identity=ident), reads=r, writes=w)


def bc3(ap, n_mid):
    p, f = ap.shape
    return ap.unsqueeze(1).to_broadcast([p, n_mid, f])


def bcl(ap, n_last):
    p, f = ap.shape
    return ap.unsqueeze(2).to_broadcast([p, f, n_last])


def build_core(stop_after=None, dbg=()):
    nc = bass.Bass("TRN2", target_bir_lowering=False)
    din = lambda name, shape, dt=F32: nc.dram_tensor(name, shape, dt, kind="ExternalInput").ap()
    x = din("x", [NT, D])
    cT = din("cT", [128, 8, NB])
    w_ada = din("w_ada", [D, 6 * D]); b_ada = din("b_ada", [1, 6 * D])
    norm1_g = din("norm1_g", [1, D]); w_in = din("w_in", [D, ZC])
    rwkv_mu = din("rwkv_mu", [1, 3328]); w0 = din("w0", [1, D]); w2 = din("w2", [64, D])
    a0 = din("a0", [1, D]); a2 = din("a2", [64, D]); g2 = din("g2", [128, D])
    k_k = din("k_k", [1, D]); k_a = din("k_a", [1, D]); r_k = din("r_k", [1, D])
    lnx_g = din("lnx_g", [1, D]); lnx_b = din("lnx_b", [1, D])
    q_norm_g = din("q_norm_g", [3, 64]); k_norm_g = din("k_norm_g", [3, 64])
    w_br_rwkv = din("w_br_rwkv", [D, D]); w_br_attn = din("w_br_attn", [256, D]); w_out = din("w_out", [D, D])
    norm2_g = din("norm2_g", [1, D]); peer_wq = din("peer_wq", [D, 2048])
    peer_k1 = din("peer_k1", [128, 128]); peer_k2 = din("peer_k2", [128, 128])
    peer_u = din("peer_u", [16384, D]); peer_v = din("peer_v", [16384, D])
    ident_d = din("ident", [128, 128]); cs_d = din("cs", [128, 16, 16])
    mcur_d = din("mcur", [128, 128]); mprev_d = din("mprev", [128, 128])
    iota16_d = din("iota16", [128, 16])
    ltri_d = din("ltri", [128, 128]); lblk_d = din("lblk", [128, 128]); m256_d = din("m256", [128, 256]); mL_d = din("mL", [128, 128])
    y_out = nc.dram_tensor("y", [NT, D], F32, kind="ExternalOutput").ap()

    dscr = lambda name, shape, dt=F32: nc.dram_tensor(name, shape, dt).ap()
    modd = dscr("modd", [NB, 6 * D])
    zs = dscr("zs", [NT, ZC])
    sc = {k: dscr("sc_" + k, [NT, D]) for k in ("r", "w", "k", "v", "a", "b", "g", "y")}
    yrs = dscr("yrs", [NT, D], BF16)
    hs = dscr("hs", [NT, D])
    n2s = dscr("n2s", [NT, D], BF16)
    uvb = dscr("uvb", [16384, 2 * D], BF16)
    dbg_out = {}
    for name, shape in dbg:
        dbg_out[name] = nc.dram_tensor("dbg_" + name, shape, F32, kind="ExternalOutput").ap()

    with contextlib.ExitStack() as top:
        sems = {n: top.enter_context(nc.semaphore(n)) for n in sem_names()}
        S = Sched(nc, sems)
        h = H(S)
        uid = [0]

        def sbt(es, name, shape, dt=F32):
            uid[0] += 1
            return es.enter_context(nc.sbuf_tensor("s%d_%s" % (uid[0], name), shape, dt))

        def pst(es, name, shape, dt=F32):
            uid[0] += 1
            return es.enter_context(nc.psum_tensor("p%d_%s" % (uid[0], name), shape, dt))

        idf = sbt(top, "idf", [128, 128]); idb = sbt(top, "idb", [128, 128], BF16)
        yaT = sbt(top, "yaT", [64, 4, NT], BF16)
        h.dma(idf[:], ident_d, [], ["idf"])
        h.copy("dve", idb[:], idf[:], ["idf"], ["idb"])
        last_tok = [None]

        def end_phase():
            S.barrier()
            S.emit()

        with contextlib.ExitStack() as es:
            ct = sbt(es, "ct", [128, 8, NB]); sil = sbt(es, "sil", [128, 8, NB])
            wad = [sbt(es, "wad%d" % i, [128, 8, 512]) for i in range(2)]
            bad = sbt(es, "bad", [1, 6 * D]); modrow = sbt(es, "modrow", [1, NB, 6 * D])
            psm = [pst(es, "psm%d" % i, [128, 512]) for i in range(NB)]
            h.dma(ct[:], cT, [], ["ct"])
            h.dma(bad[:], b_ada, [], ["bad"])
            h.act(sil[:], ct[:], AF.Silu, ["ct"], ["sil"])
            for nb in range(12):
                wb = wad[nb % 2]
                h.dma(wb[:], w_ada[:, nb * 512:(nb + 1) * 512].rearrange("(kc p) n -> p kc n", p=128), [], ["wad%d" % (nb % 2)])
                for b in range(NB):
                    for kc in range(8):
                        h.mm(psm[b][0:1, :], sil[:, kc, b:b + 1], wb[:, kc, :], kc == 0, kc == 7,
                             ["sil", "wad%d" % (nb % 2)], ["psm%d" % b])
                    h.tt("dve", modrow[0:1, b, nb * 512:(nb + 1) * 512], psm[b][0:1, :], bad[0:1, nb * 512:(nb + 1) * 512],
                         ALU.add, ["psm%d" % b, "bad"], ["modrow"])
            for b in range(NB):
                for o in (1024, 4096):
                    h.ts("dve", modrow[0:1, b, o:o + 1024], modrow[0:1, b, o:o + 1024], 1.0, None, ALU.add, None, ["modrow"], ["modrow"])
                h.dma(modd[b:b + 1, :], modrow[0:1, b, :], ["modrow"], ["modd"])
            end_phase()
        if stop_after == "0":
            return nc, dbg_out, S

        def bcast_load(dst, src_row, w, r=()):
            return h.dma(dst, src_row.partition_broadcast(128), list(r), w)

        with contextlib.ExitStack() as es:
            n1T = sbt(es, "n1T", [128, 8, NT], BF16)
            with contextlib.ExitStack() as es1:
                G1 = [sbt(es1, "G1_%d" % b, [128, D]) for b in range(NB)]
                SH1 = [sbt(es1, "SH1_%d" % b, [128, D]) for b in range(NB)]
                tmpg = sbt(es1, "tmpg", [128, D])
                xt = [sbt(es1, "xt%d" % i, [128, D]) for i in range(2)]
                junk = sbt(es1, "junk", [128, D]); n1f = sbt(es1, "n1f", [128, D])
                n1b = [sbt(es1, "n1b%d" % i, [128, D], BF16) for i in range(2)]
                ss = sbt(es1, "ss", [128, NTILE]); rs = sbt(es1, "rs", [128, NTILE])
                ptr = [pst(es1, "ptr%d" % i, [128, 8, 128], BF16) for i in range(2)]
                bcast_load(tmpg[:], norm1_g[0:1, :], ["tmpg"])
                for b in range(NB):
                    bcast_load(G1[b][:], modd[b:b + 1, 1024:2048], ["G1_%d" % b], ["modd"])
                    bcast_load(SH1[b][:], modd[b:b + 1, 0:1024], ["SH1_%d" % b], ["modd"])
                    h.tt("dve", G1[b][:], G1[b][:], tmpg[:], ALU.mult, ["G1_%d" % b, "tmpg"], ["G1_%d" % b])
                for i in range(NTILE):
                    b = i // 16; j = i % 2
                    h.dma(xt[j][:], x[i * 128:(i + 1) * 128, :], [], ["xt%d" % j])
                    h.act(junk[:], xt[j][:], AF.Square, ["xt%d" % j], ["junk", "ss%d" % i], accum=ss[:, i:i + 1])
                    h.ts("dve", rs[:, i:i + 1], ss[:, i:i + 1], 1.0 / D, EPS, ALU.mult, ALU.add, ["ss%d" % i], ["rs%d" % i])
                    h.act(rs[:, i:i + 1], rs[:, i:i + 1], AF.Sqrt, ["rs%d" % i], ["rs%d" % i])
                    h.recip(rs[:, i:i + 1], rs[:, i:i + 1], ["rs%d" % i], ["rs%d" % i])
                    h.stt(n1f[:], xt[j][:], rs[:, i:i + 1], G1[b][:], ALU.mult, ALU.mult, ["xt%d" % j, "rs%d" % i, "G1_%d" % b], ["n1f"])
                    h.tt("pool", n1b[j][:], n1f[:], SH1[b][:], ALU.add, ["n1f", "SH1_%d" % b], ["n1b%d" % j])
                    for kc in range(8):
                        h.tr(ptr[j][:, kc, :], n1b[j][:, kc * 128:(kc + 1) * 128], idb[:], ["n1b%d" % j, "idb"], ["ptr%d" % j])
                    h.copy("act", n1T[:, :, i * 128:(i + 1) * 128], ptr[j][:], ["ptr%d" % j], ["n1T%d" % i])
                end_phase()
            with contextlib.ExitStack() as es2:
                wf = [sbt(es2, "wf%d" % i, [128, 8, 512]) for i in range(2)]
                wbf = [sbt(es2, "wbf%d" % i, [128, 8, 512], BF16) for i in range(2)]
                zo = [sbt(es2, "zo%d" % i, [128, 512]) for i in range(4)]
                pz = [pst(es2, "pz%d" % i, [128, 512]) for i in range(4)]
                cnt = 0
                for nb in range(15):
                    j = nb % 2
                    h.dma(wf[j][:], w_in[:, nb * 512:(nb + 1) * 512].rearrange("(kc p) n -> p kc n", p=128), [], ["wf%d" % j])
                    h.copy("pool", wbf[j][:], wf[j][:], ["wf%d" % j], ["wbf%d" % j])
                    for i in range(NTILE):
                        q = cnt % 4; cnt += 1
                        for kc in range(8):
                            h.mm(pz[q][:], n1T[:, kc, i * 128:(i + 1) * 128], wbf[j][:, kc, :], kc == 0, kc == 7,
                                 ["wbf%d" % j], ["pz%d" % q])
                        h.copy("act" if q % 2 == 0 else "dve", zo[q][:], pz[q][:], ["pz%d" % q], ["zo%d" % q])
                        h.dma(zs[i * 128:(i + 1) * 128, nb * 512:(nb + 1) * 512], zo[q][:], ["zo%d" % q], ["zs"])
                end_phase()
        if stop_after == "A":
            return nc, dbg_out, S

        for b in range(NB):
            tb = b * SEQ
            with contextlib.ExitStack() as es:
                qT = sbt(es, "qT", [128, 6, SEQ], BF16); kT = sbt(es, "kT", [128, 6, SEQ], BF16)
                VA = [sbt(es, "VA%d" % g, [128, 16, 4, 65], BF16) for g in range(3)]
                ACC = sbt(es, "ACC", [65, 4, SEQ]); RD = sbt(es, "RD", [65, SEQ])
                onesf = sbt(es, "onesf", [65, 64])
                mcur = sbt(es, "mcur", [128, 128], BF16); mprev = sbt(es, "mprev", [128, 128], BF16)
                mtmp = sbt(es, "mtmp", [128, 128])
                QG = sbt(es, "QG", [128, 12, 64]); KG = sbt(es, "KG", [128, 12, 64])
                cs = sbt(es, "cs", [128, 16, 16])
                h.copy("dve", onesf[:, :], idf[0:65, 64:65].to_broadcast([65, 64]), ["idf"], ["onesf"])
                h.dma(mtmp[:], mcur_d, [], ["mtmp"]); h.copy("dve", mcur[:], mtmp[:], ["mtmp"], ["mcur"])
                h.dma(mtmp[:], mprev_d, ["mtmp"], ["mtmp"]); h.copy("dve", mprev[:], mtmp[:], ["mtmp"], ["mprev"])
                h.dma(cs[:], cs_d, [], ["cs"])
                for gh in range(12):
                    bcast_load(QG[:, gh, :], q_norm_g[gh // 4:gh // 4 + 1, :], ["QG"])
                    bcast_load(KG[:, gh, :], k_norm_g[gh // 4:gh // 4 + 1, :], ["KG"])
                for g in range(3):
                    h.memset("pool", VA[g][:, :, :, 64:65], 1.0, ["VA%d" % g])
                with contextlib.ExitStack() as es1:
                    zq = [sbt(es1, "zq%d" % i, [128, 768]) for i in range(2)]
                    zk = [sbt(es1, "zk%d" % i, [128, 768]) for i in range(2)]
                    sq = sbt(es1, "sq", [128, 768]); st = sbt(es1, "st", [128, 12]); qn = sbt(es1, "qn", [128, 768])
                    r1 = sbt(es1, "r1", [128, 12, 8]); r2 = sbt(es1, "r2", [128, 12, 8])
                    qb = [sbt(es1, "qb%d" % i, [128, 768], BF16) for i in range(2)]
                    ptq = [pst(es1, "ptq%d" % i, [128, 6, 128], BF16) for i in range(2)]
                    vtmp = [sbt(es1, "vtmp%d" % i, [128, 256]) for i in range(2)]
                    cntp = 0
                    for i in range(16):
                        cosb = bc3(cs[:, i, 0:8], 12); sinb = bc3(cs[:, i, 8:16], 12)
                        for which, zt, dstT, GG, c0, scl in (("q", zq, qT, QG, 3328, 0.125), ("k", zk, kT, KG, 3328 + 768, 1.0)):
                            j = cntp % 2; cntp += 1
                            zn = "z%s%d" % (which, i % 2)
                            zz = zt[i % 2]
                            h.dma(zz[:], zs[tb + i * 128:tb + (i + 1) * 128, c0:c0 + 768], ["zs"], [zn])
                            h.tt("pool", sq[:], zz[:], zz[:], ALU.mult, [zn], ["sq"])
                            h.red(st[:], sq[:].rearrange("p (a b) -> p a b", a=12), ALU.add, ["sq"], ["st"])
                            h.ts("dve", st[:], st[:], 1.0 / 64, EPS, ALU.mult, ALU.add, ["st"], ["st"])
                            h.act(st[:], st[:], AF.Sqrt, ["st"], ["st"])
                            h.recip(st[:], st[:], ["st"], ["st"])
                            if scl != 1.0:
                                h.ts("dve", st[:], st[:], scl, None, ALU.mult, None, ["st"], ["st"])
                            z3 = zz[:].rearrange("p (a b) -> p a b", a=12)
                            q3 = qn[:].rearrange("p (a b) -> p a b", a=12)
                            h.tt("dve", q3, z3, bcl(st[:], 64), ALU.mult, [zn, "st"], ["qn"])
                            h.tt("pool", q3, q3, GG[:], ALU.mult, ["qn", "QG", "KG"], ["qn"])
                            qb3 = qb[j][:].rearrange("p (a b) -> p a b", a=12)
                            h.tt("dve", r1[:], q3[:, :, 0:8], cosb, ALU.mult, ["qn", "cs"], ["r1"])
                            h.tt("dve", r2[:], q3[:, :, 8:16], sinb, ALU.mult, ["qn", "cs"], ["r2"])
                            h.tt("dve", qb3[:, :, 0:8], r1[:], r2[:], ALU.subtract, ["r1", "r2"], ["qb%d" % j])
                            h.tt("dve", r1[:], q3[:, :, 8:16], cosb, ALU.mult, ["qn", "cs", "qb%d" % j], ["r1"])
                            h.tt("dve", r2[:], q3[:, :, 0:8], sinb, ALU.mult, ["qn", "cs", "qb%d" % j], ["r2"])
                            h.tt("dve", qb3[:, :, 8:16], r1[:], r2[:], ALU.add, ["r1", "r2"], ["qb%d" % j])
                            h.copy("pool", qb3[:, :, 16:64], q3[:, :, 16:64], ["qn"], ["qb%d" % j])
                            for c6 in range(6):
                                h.tr(ptq[j][:, c6, :], qb[j][:, c6 * 128:(c6 + 1) * 128], idb[:], ["qb%d" % j, "idb"], ["ptq%d" % j])
                            h.copy("act", dstT[:, :, i * 128:(i + 1) * 128], ptq[j][:], ["ptq%d" % j], [which + "T"])
                    cv = 0
                    for g, dil in enumerate((1, 4, 16)):
                        nmt = SEQ // dil // 128
                        for r_ in range(dil):
                            for mt in range(nmt):
                                kt = r_ * nmt + mt
                                j = cv % 2; cv += 1
                                st0 = tb + r_ + dil * 128 * mt
                                src = zs[st0:st0 + dil * 127 + 1:dil, 3328 + 1536 + g * 256:3328 + 1536 + (g + 1) * 256]
                                h.dma(vtmp[j][:], src, ["zs"], ["vtmp%d" % j])
                                h.copy("pool", VA[g][:, kt, :, 0:64], vtmp[j][:].rearrange("p (a b) -> p a b", a=4),
                                       ["vtmp%d" % j], ["VA%d" % g])
                    S.barrier()
                import os
                EP = os.environ.get("EPART", "")
                if EP == "prep":
                    end_phase()
                    continue
                with contextlib.ExitStack() as es1:
                    pss = [pst(es1, "pss%d" % i, [128, 512]) for i in range(2)]
                    pso = [pst(es1, "pso%d" % i, [128, 512]) for i in range(2)]
                    psb = pst(es1, "psb", [128, 512])
                    ee = [sbt(es1, "ee%d" % i, [128, 128], BF16) for i in range(3)]
                    pp = [sbt(es1, "pp%d" % i, [128, 128], BF16) for i in range(3)]
                    pairs = []
                    co = 0
                    for g, dil in enumerate((1, 4, 16)):
                        nmt = SEQ // dil // 128
                        if EP == "g0" and g > 0:
                            continue
                        for hh in range(4):
                            gh = g * 4 + hh; ch = gh // 2; base = (gh % 2) * 64
                            for r_ in range(dil):
                                for mt in range(nmt):
                                    cq = slice(r_ + dil * 128 * mt, r_ + dil * 128 * mt + dil * 127 + 1, dil)
                                    jo = co % 2; co += 1
                                    kts = ([(mt - 1, mprev)] if mt > 0 else []) + [(mt, mcur)]
                                    for n_, (kmt, msk) in enumerate(kts):
                                        ck = slice(r_ + dil * 128 * kmt, r_ + dil * 128 * kmt + dil * 127 + 1, dil)
                                        pairs.append(dict(g=g, hh=hh, ch=ch, base=base, cq=cq, ck=ck, jo=jo, msk=msk, n_=n_, nk=len(kts),
                                                          kt=r_ * nmt + kmt))

                    def emit_score(n):
                        p = pairs[n]; js = n % 2; bs = p["base"]
                        h.mm(pss[js][:, 0:128], kT[bs:bs + 64, p["ch"], p["ck"]], qT[bs:bs + 64, p["ch"], p["cq"]], True, True,
                             ["qT", "kT"], ["pss%d" % js])
                    if pairs:
                        emit_score(0)
                    for n, p in enumerate(pairs):
                        js = n % 2; je = n % 3; jo = p["jo"]; g = p["g"]; hh = p["hh"]
                        if n + 1 < len(pairs):
                            emit_score(n + 1)
                        h.act(ee[je][:], pss[js][:, 0:128], AF.Exp, ["pss%d" % js], ["ee%d" % je])
                        h.tt("dve" if n % 2 else "pool", pp[je][:], ee[je][:], p["msk"][:], ALU.mult, ["ee%d" % je, "mcur", "mprev"], ["pp%d" % je])
                        h.mm(pso[jo][0:65, 0:128], VA[g][:, p["kt"], hh, :], pp[je][:], p["n_"] == 0, p["n_"] == p["nk"] - 1,
                             ["pp%d" % je, "VA%d" % g], ["pso%d" % jo])
                        if p["n_"] == p["nk"] - 1:
                            if g == 0:
                                h.copy("dve", ACC[:, hh, p["cq"]], pso[jo][0:65, 0:128], ["pso%d" % jo], ["ACC%d" % hh])
                            else:
                                h.tt("dve", ACC[:, hh, p["cq"]], ACC[:, hh, p["cq"]], pso[jo][0:65, 0:128], ALU.add, ["pso%d" % jo, "ACC%d" % hh], ["ACC%d" % hh])
                    for hh in range(4):
                        if EP in ("g0", "nofinal"):
                            continue
                        h.ts("dve", RD[:, :], ACC[:, hh, :], 1e-30, None, ALU.max, None, ["ACC%d" % hh], ["RD"])
                        h.recip(RD[:, :], RD[:, :], ["RD"], ["RD"])
                        for cb in range(4):
                            cs_ = slice(cb * 512, (cb + 1) * 512)
                            h.mm(psb[0:64, :], onesf[:, :], RD[:, cs_], True, True, ["RD", "onesf"], ["psb"])
                            h.tt("dve", yaT[:, hh, tb + cb * 512:tb + (cb + 1) * 512], ACC[0:64, hh, cs_], psb[0:64, :], ALU.mult,
                                 ["psb", "ACC%d" % hh], ["yaT"])
                    end_phase()
        if "yaT" in dbg_out:
            with contextlib.ExitStack() as es:
                tmpd = [sbt(es, "tmpd%d" % i, [64, 2048]) for i in range(2)]
                ci = 0
                for hh in range(4):
                    for cb in range(NT // 2048):
                        j = ci % 2; ci += 1
                        h.copy("dve", tmpd[j][:], yaT[:, hh, cb * 2048:(cb + 1) * 2048], ["yaT"], ["tmpd%d" % j])
                        h.dma(dbg_out["yaT"][:, hh, cb * 2048:(cb + 1) * 2048], tmpd[j][:], ["tmpd%d" % j], ["dbg"])
                end_phase()
        if stop_after == "E":
            return nc, dbg_out, S

        with contextlib.ExitStack() as es:
            MU = sbt(es, "MU", [128, 3328])
            W0b = sbt(es, "W0b", [128, D]); A0b = sbt(es, "A0b", [128, D]); KKb = sbt(es, "KKb", [128, D]); KAb = sbt(es, "KAb", [128, D])
            W2A2 = sbt(es, "W2A2", [128, D]); G2t = sbt(es, "G2t", [128, D])
            zc = sbt(es, "zc", [128, 3328]); zp = sbt(es, "zp", [128, 3328])
            L = sbt(es, "L", [128, 256]); LT = sbt(es, "LT", [128, 2, 128])
            wt = sbt(es, "wt", [128, D]); at = sbt(es, "at", [128, D]); kk = sbt(es, "kk", [128, D]); sq = sbt(es, "sqb", [128, D])
            av = sbt(es, "av", [128, D]); bv = sbt(es, "bv", [128, D]); k2 = sbt(es, "k2", [128, D]); gt = sbt(es, "gt", [128, D])
            s16 = sbt(es, "s16", [128, 16])
            pl = pst(es, "pl", [128, 2, 128]); plw = pst(es, "plw", [128, D]); pla = pst(es, "pla", [128, D]); plg = pst(es, "plg", [128, D])
            bcast_load(MU[:], rwkv_mu[0:1, :], ["MU"])
            for t_, src in ((W0b, w0), (A0b, a0), (KKb, k_k), (KAb, k_a)):
                bcast_load(t_[:], src[0:1, :], ["cB"])
            h.dma(W2A2[0:64, :], w2, [], ["cB"]); h.dma(W2A2[64:128, :], a2, [], ["cB"]); h.dma(G2t[:], g2, [], ["cB"])
            for i in range(NTILE):
                T0 = i * 128
                h.dma(zc[:], zs[T0:T0 + 128, 0:3328], [], ["zc"])
                if i % 16 == 0:
                    h.memset("pool", zp[0:1, :], 0.0, ["zp"])
                    h.dma(zp[1:128, :], zs[T0:T0 + 127, 0:3328], ["zp"], ["zp"])
                else:
                    h.dma(zp[:], zs[T0 - 1:T0 + 127, 0:3328], [], ["zp"])
                h.tt("dve", zp[:], zp[:], zc[:], ALU.subtract, ["zp", "zc"], ["zp"])
                h.tt("pool", zp[:], zp[:], MU[:], ALU.mult, ["zp", "MU"], ["zp"])
                h.tt("dve", zp[:], zp[:], zc[:], ALU.add, ["zp", "zc"], ["zp"])
                h.act(L[:, 0:64], zp[:, 3072:3136], AF.Tanh, ["zp"], ["L"])
                h.copy("pool", L[:, 64:128], zp[:, 3136:3200], ["zp"], ["L"])
                h.act(L[:, 128:256], zp[:, 3200:3328], AF.Sigmoid, ["zp"], ["L"])
                h.tr(pl[:, 0, :], L[:, 0:128], idf[:], ["L", "idf"], ["pl"])
                h.tr(pl[:, 1, :], L[:, 128:256], idf[:], ["L", "idf"], ["pl"])
                h.copy("act", LT[:], pl[:], ["pl"], ["LT"])
                for hb in range(2):
                    cs_ = slice(hb * 512, (hb + 1) * 512)
                    h.mm(plw[:, cs_], LT[0:64, 0, :], W2A2[0:64, cs_], True, True, ["LT", "cB"], ["plw"])
                    h.mm(pla[:, cs_], LT[64:128, 0, :], W2A2[64:128, cs_], True, True, ["LT", "cB"], ["pla"])
                    h.mm(plg[:, cs_], LT[:, 1, :], G2t[:, cs_], True, True, ["LT", "cB"], ["plg"])
                h.tt("dve", wt[:], plw[:], W0b[:], ALU.add, ["plw", "cB"], ["wt"])
                h.act(wt[:], wt[:], AF.Sigmoid, ["wt"], ["wt"])
                h.ts("pool", wt[:], wt[:], -float(np.exp(-0.5)), None, ALU.mult, None, ["wt"], ["wt"])
                h.tt("dve", at[:], pla[:], A0b[:], ALU.add, ["pla", "cB"], ["at"])
                h.act(at[:], at[:], AF.Sigmoid, ["at"], ["at"])
                h.copy("act", gt[:], plg[:], ["plg"], ["gt"])
                h.tt("pool", kk[:], zp[:, 1024:2048], KKb[:], ALU.mult, ["zp", "cB"], ["kk"])
                h.tt("pool", sq[:], kk[:], kk[:], ALU.mult, ["kk"], ["sqb"])
                h.red(s16[:], sq[:].rearrange("p (a b) -> p a b", a=16), ALU.add, ["sqb"], ["s16"])
                h.act(s16[:], s16[:], AF.Sqrt, ["s16"], ["s16"])
                h.ts("dve", s16[:], s16[:], 1e-12, None, ALU.max, None, ["s16"], ["s16"])
                h.recip(s16[:], s16[:], ["s16"], ["s16"])
                kk3 = kk[:].rearrange("p (a b) -> p a b", a=16)
                h.tt("dve", kk3, kk3, bcl(s16[:], 64), ALU.mult, ["kk", "s16"], ["kk"])
                h.ts("pool", av[:], kk[:], -1.0, None, ALU.mult, None, ["kk"], ["av"])
                h.tt("pool", bv[:], kk[:], at[:], ALU.mult, ["kk", "at"], ["bv"])
                h.stt(k2[:], at[:], -1.0, KAb[:], ALU.add, ALU.mult, ["at", "cB"], ["k2"])
                h.stt(k2[:], k2[:], 1.0, zp[:, 1024:2048], ALU.add, ALU.mult, ["k2", "zp"], ["k2"])
                rows = slice(T0, T0 + 128)
                h.dma(sc["r"][rows, :], zp[:, 0:1024], ["zp"], ["sc_r"])
                h.dma(sc["v"][rows, :], zp[:, 2048:3072], ["zp"], ["sc_v"])
                h.dma(sc["w"][rows, :], wt[:], ["wt"], ["sc_w"])
                h.dma(sc["k"][rows, :], k2[:], ["k2"], ["sc_k"])
                h.dma(sc["a"][rows, :], av[:], ["av"], ["sc_a"])
                h.dma(sc["b"][rows, :], bv[:], ["bv"], ["sc_b"])
                h.dma(sc["g"][rows, :], gt[:], ["gt"], ["sc_g"])
            end_phase()

        with contextlib.ExitStack() as es:
            ltri = sbt(es, "ltri", [128, 128]); lblk = sbt(es, "lblk", [128, 128])
            m256 = sbt(es, "m256", [128, 256], BF16); mL = sbt(es, "mL", [128, 128], BF16); mtmp2 = sbt(es, "mtmp2", [128, 256])
            h.dma(ltri[:], ltri_d, [], ["cC"]); h.dma(lblk[:], lblk_d, [], ["cC"])
            h.dma(mtmp2[:], m256_d, [], ["mtmp2"]); h.copy("dve", m256[:], mtmp2[:], ["mtmp2"], ["cC"])
            h.dma(mtmp2[:, 0:128], mL_d, ["mtmp2"], ["mtmp2"]); h.copy("dve", mL[:], mtmp2[:, 0:128], ["mtmp2"], ["cC"])
            ld = {n: [sbt(es, "l%s%d" % (n, i), [128, D]) for i in range(2)] for n in ("r", "w", "k", "v", "a", "b")}
            ep = sbt(es, "ep", [128, D]); em = sbt(es, "em", [128, D]); eA = sbt(es, "eA", [128, D]); eE = sbt(es, "eE", [128, D])
            cumS = sbt(es, "cumS", [128, D]); gE = sbt(es, "gE", [128, D])
            Rb, Ab, Kb, Bb, Kh, Bh, Vb = (sbt(es, n, [128, D], BF16) for n in ("Rb", "Ab", "Kb", "Bb", "Kh", "Bh", "Vb"))
            ART = sbt(es, "ART", [128, 8, 2, 128], BF16); KT = sbt(es, "KTc", [128, 8, 128], BF16); BT = sbt(es, "BTc", [128, 8, 128], BF16)
            Gf = sbt(es, "Gf", [128, 8, 2])
            STf = sbt(es, "STf", [128, 8, 64]); STb = sbt(es, "STb", [128, 8, 64], BF16)
            ytile = [sbt(es, "ytile%d" % i, [128, D]) for i in range(2)]
            NG = 4
            hb_ = lambda n, shape: [sbt(es, "%s_%d" % (n, q), shape, BF16) for q in range(NG)]
            X1 = hb_("X1", [128, 256]); X2 = hb_("X2", [128, 256]); Q0 = hb_("Q0", [128, 128]); Q1 = hb_("Q1", [128, 128])
            P0 = hb_("P0", [128, 128]); P1 = hb_("P1", [128, 128]); QI = hb_("QI", [128, 128]); TT = hb_("TT", [128, 128])
            AVs = hb_("AVs", [128, 64]); Wt = hb_("Wt", [128, 64]); U0b = hb_("U0b", [128, 64]); R2T = hb_("R2T", [128, 128])
            MTc = hb_("MTc", [128, 2, 64])
            Y0s = [sbt(es, "Y0s_%d" % q, [128, 64]) for q in range(NG)]; Z0c = [sbt(es, "Z0c_%d" % q, [128, 2, 64]) for q in range(NG)]
            PB = [pst(es, "PB%d" % q, [128, 512]) for q in range(7)]
            PT = pst(es, "PTc", [128, 8, 128], BF16)

            def head_gen(hd, hq, yt):
                hb = hd // 2; base = (hd % 2) * 64; hs = slice(base, base + 64); cols = slice(hd * 64, (hd + 1) * 64)
                B = PB[hq]; bk = "PB%d" % hq; q_ = "_%d" % hq
                cp = "act" if hq % 2 == 0 else "dve"
                art2 = ART[hs, hb, :, :].rearrange("p a b -> p (a b)")
                h.mm(B[:, 0:256], BT[hs, hb, :], art2, True, True, ["ART", "BTc"], [bk])
                h.tt("dve", X1[hq][:], B[:, 0:256], m256[:], ALU.mult, ["cC"], [bk, "X1" + q_])
                yield
                h.mm(B[:, 0:256], KT[hs, hb, :], art2, True, True, ["ART", "KTc"], [bk])
                h.tt("dve", X2[hq][:], B[:, 0:256], m256[:], ALU.mult, ["cC"], [bk, "X2" + q_])
                yield
                h.mm(B[:, 0:128], ART[hs, hb, 0, :], BT[hs, hb, :], True, True, ["ART", "BTc"], [bk])
                h.tt("dve", Q0[hq][:], B[:, 0:128], mL[:], ALU.mult, ["cC"], [bk, "Q0" + q_])
                h.tt("pool", TT[hq][:], X1[hq][:, 0:128], idb[:], ALU.add, ["X1" + q_, "idb"], ["TT" + q_])
                yield
                P_, Pk = X1[hq][:, 0:128], "X1" + q_
                Q_, Qk = Q0[hq][:], "Q0" + q_
                Qbuf = [Q0[hq], Q1[hq]]; Pbuf = [P0[hq], P1[hq]]
                for k in range(1, 6):
                    Qn, Qnk = Qbuf[k % 2][:], "Q%d%s" % (k % 2, q_)
                    h.mm(B[:, 0:128], P_, Q_, True, True, [Pk, Qk], [bk])
                    h.copy(cp, Qn, B[:, 0:128], [], [bk, Qnk])
                    if k < 5:
                        Pn, Pnk = Pbuf[k % 2][:], "P%d%s" % (k % 2, q_)
                        h.mm(B[:, 128:256], Q_, P_, True, True, [Pk, Qk], [bk])
                        h.copy(cp, Pn, B[:, 128:256], [], [bk, Pnk])
                    yield
                    h.tt("pool", QI[hq][:], Qn, idb[:], ALU.add, [Qnk, "idb"], ["QI" + q_])
                    h.mm(B[:, 256:384], QI[hq][:], TT[hq][:], True, True, ["QI" + q_, "TT" + q_], [bk])
                    h.copy(cp, TT[hq][:], B[:, 256:384], [], [bk, "TT" + q_])
                    if k < 5:
                        P_, Pk = Pn, Pnk
                    Q_, Qk = Qn, Qnk
                    yield
                h.mm(B[:, 0:64], X2[hq][:, 0:128], Vb[:, cols], True, True, ["X2" + q_, "Vb"], [bk])
                h.copy(cp, AVs[hq][:], B[:, 0:64], [], [bk, "AVs" + q_])
                h.mm(B[:, 64:128], TT[hq][:], Ab[:, cols], True, True, ["TT" + q_, "Ab"], [bk])
                h.copy(cp, Wt[hq][:], B[:, 64:128], [], [bk, "Wt" + q_])
                yield
                h.mm(B[:, 0:64], TT[hq][:], AVs[hq][:], True, True, ["TT" + q_, "AVs" + q_], [bk])
                h.copy(cp, U0b[hq][:], B[:, 0:64], [], [bk, "U0b" + q_])
                h.mm(B[hs, 128:256], Wt[hq][:], X1[hq][:, 128:256], True, True, ["Wt" + q_, "X1" + q_], [bk])
                h.tt("dve", R2T[hq][hs, :], B[hs, 128:256], ART[hs, hb, 1, :], ALU.add, ["ART"], [bk, "R2T" + q_])
                yield
                h.mm(B[:, 0:64], X1[hq][:, 128:256], U0b[hq][:], True, False, ["X1" + q_, "U0b" + q_], [bk])
                h.mm(B[:, 0:64], X2[hq][:, 128:256], Vb[:, cols], False, True, ["X2" + q_, "Vb"], [bk])
                h.copy(cp, Y0s[hq][:], B[:, 0:64], [], [bk, "Y0s" + q_])
                for cidx in range(2):
                    c0_ = cidx * 64; cs = slice(c0_, c0_ + 64)
                    h.mm(B[hs, 64 + cidx * 64:128 + cidx * 64], Wt[hq][cs, :], Bh[cs, cols], True, True, ["Wt" + q_, "Bh"], [bk])
                    h.copy(cp, MTc[hq][hs, cidx, :], B[hs, 64 + cidx * 64:128 + cidx * 64], [], [bk, "MTc" + q_])
                yield
                for cidx in range(2):
                    c0_ = cidx * 64; cs = slice(c0_, c0_ + 64)
                    h.mm(B[hs, 256 + cidx * 64:320 + cidx * 64], Bh[cs, cols], U0b[hq][cs, :], True, False, ["Bh", "U0b" + q_], [bk])
                    h.mm(B[hs, 256 + cidx * 64:320 + cidx * 64], Kh[cs, cols], Vb[cs, cols], False, True, ["Kh", "Vb"], [bk])
                    h.copy(cp, Z0c[hq][hs, cidx, :], B[hs, 256 + cidx * 64:320 + cidx * 64], [], [bk, "Z0c" + q_])
                yield
                sk = "ST%d" % hd
                for cidx in range(2):
                    c0_ = cidx * 64; cs = slice(c0_, c0_ + 64)
                    h.mm(B[cs, 0:64], R2T[hq][hs, cs], STb[hs, hb, :], True, True, ["R2T" + q_, sk + "b"], [bk])
                    h.tt("dve", yt[cs, cols], B[cs, 0:64], Y0s[hq][cs, :], ALU.add, ["Y0s" + q_], [bk, "ytile"])
                    h.mm(B[hs, 64:128], MTc[hq][hs, cidx, :], STb[hs, hb, :], True, True, ["MTc" + q_, sk + "b"], [bk])
                    h.stt(STf[hs, hb, :], STf[hs, hb, :], Gf[hs, hb, cidx:cidx + 1], B[hs, 64:128], ALU.mult, ALU.add, [sk + "f", "Gf"], [bk, sk + "f"])
                    h.tt("pool", STf[hs, hb, :], STf[hs, hb, :], Z0c[hq][hs, cidx, :], ALU.add, [sk + "f", "Z0c" + q_], [sk + "f"])
                    h.copy("pool", STb[hs, hb, :], STf[hs, hb, :], [sk + "f"], [sk + "b"])
                    yield

            ntile_c = NTILE if stop_after != "Cshort" else 2
            for i in range(ntile_c):
                j = i % 2; rows = slice(i * 128, (i + 1) * 128)
                if i % 16 == 0:
                    h.memset("dve", STf[:], 0.0, ["ST%df" % q for q in range(16)])
                    h.memset("pool", STb[:], 0.0, ["ST%db" % q for q in range(16)])
                for n in ("r", "w", "k", "v", "a", "b"):
                    h.dma(ld[n][j][:], sc[n][rows, :], [], ["l%s%d" % (n, j)])
                lr, lw_, lk, lv, la, lb = (ld[n][j] for n in ("r", "w", "k", "v", "a", "b"))
                for hf in range(2):
                    cs_ = slice(hf * 512, (hf + 1) * 512)
                    h.mm(PB[hf][:, :], ltri[:], lw_[:, cs_], True, True, ["cC", "lw%d" % j], ["PB%d" % hf])
                    h.mm(PB[2 + hf][:, :], lblk[:], lw_[:, cs_], True, True, ["cC", "lw%d" % j], ["PB%d" % (2 + hf)])
                for hf in range(2):
                    cs_ = slice(hf * 512, (hf + 1) * 512); bk = "PB%d" % hf; bk2 = "PB%d" % (2 + hf)
                    h.act(ep[:, cs_], PB[hf][:, :], AF.Exp, [], [bk, "ep"])
                    h.act(em[:, cs_], PB[hf][:, :], AF.Exp, [], [bk, "em"], scale=-1.0)
                    h.copy("act", cumS[:, cs_], PB[hf][:, :], [], [bk, "cumS"])
                    h.act(gE[:, cs_], PB[2 + hf][:, :], AF.Exp, [], [bk2, "gE"])
                    h.tt("dve", eE[:, cs_], PB[2 + hf][:, :], cumS[:, cs_], ALU.subtract, ["cumS"], [bk2, "eE"])
                h.tt("dve", eA[:], cumS[:], lw_[:], ALU.subtract, ["cumS", "lw%d" % j], ["eA"])
                h.act(eA[:], eA[:], AF.Exp, ["eA"], ["eA"])
                h.act(eE[:], eE[:], AF.Exp, ["eE"], ["eE"])
                h.tt("dve", Rb[:], lr[:], ep[:], ALU.mult, ["lr%d" % j, "ep"], ["Rb"])
                h.tt("pool", Ab[:], la[:], eA[:], ALU.mult, ["la%d" % j, "eA"], ["Ab"])
                h.tt("dve", Kb[:], lk[:], em[:], ALU.mult, ["lk%d" % j, "em"], ["Kb"])
                h.tt("pool", Bb[:], lb[:], em[:], ALU.mult, ["lb%d" % j, "em"], ["Bb"])
                h.tt("dve", Kh[:], lk[:], eE[:], ALU.mult, ["lk%d" % j, "eE"], ["Kh"])
                h.tt("pool", Bh[:], lb[:], eE[:], ALU.mult, ["lb%d" % j, "eE"], ["Bh"])
                h.copy("pool", Vb[:], lv[:], ["lv%d" % j], ["Vb"])
                for hb in range(8):
                    h.mm(PB[4][:, hb * 2:hb * 2 + 2], gE[:, hb * 128:(hb + 1) * 128], idf[:, 0:128:64], True, True, ["gE", "idf"], ["PB4"])
                h.copy("dve", Gf[:].rearrange("p a b -> p (a b)"), PB[4][:, 0:16], [], ["PB4", "Gf"])
                for src, sk_, dst, dk in ((Ab, "Ab", ART[:, :, 0, :], "ART"), (Rb, "Rb", ART[:, :, 1, :], "ART"), (Kb, "Kb", KT[:], "KTc"), (Bb, "Bb", BT[:], "BTc")):
                    for kc in range(8):
                        h.tr(PT[:, kc, :], src[:, kc * 128:(kc + 1) * 128], idb[:], [sk_, "idb"], ["PTc"])
                    h.copy("act", dst, PT[:], [], ["PTc", dk])
                for g0 in range(0, 16, NG):
                    gens = [head_gen(g0 + q, q, ytile[j]) for q in range(NG)]
                    alive = list(gens)
                    while alive:
                        nxt = []
                        for g_ in alive:
                            try:
                                next(g_)
                                nxt.append(g_)
                            except StopIteration:
                                pass
                        alive = nxt
                h.dma(sc["y"][rows, :], ytile[j][:], ["ytile"], ["sc_y"])
            end_phase()

        with contextlib.ExitStack() as es:
            LNG = sbt(es, "LNG", [128, D]); LNB = sbt(es, "LNB", [128, D]); RKb = sbt(es, "RKb", [128, D])
            ld = {n: [sbt(es, "d%s%d" % (n, i), [128, D]) for i in range(2)] for n in ("y", "r", "k", "v", "g")}
            yc = sbt(es, "yc", [128, D]); sq2 = sbt(es, "sq2", [128, D]); prod = sbt(es, "prod", [128, D])
            m16 = sbt(es, "m16", [128, 16]); v16 = sbt(es, "v16", [128, 16]); b16 = sbt(es, "b16", [128, 16])
            yb = [sbt(es, "yb%d" % i, [128, D], BF16) for i in range(2)]
            yf = sbt(es, "yf", [128, D])
            bcast_load(LNG[:], lnx_g[0:1, :], ["cD"]); bcast_load(LNB[:], lnx_b[0:1, :], ["cD"]); bcast_load(RKb[:], r_k[0:1, :], ["cD"])
            v3 = lambda t_: t_[:].rearrange("p (a b) -> p a b", a=16)
            for i in range(NTILE):
                j = i % 2; rows = slice(i * 128, (i + 1) * 128)
                for n in ("y", "r", "k", "v", "g"):
                    h.dma(ld[n][j][:], sc[n][rows, :], [], ["d%s%d" % (n, j)])
                yt, rt, kt, vt, gt_ = (ld[n][j] for n in ("y", "r", "k", "v", "g"))
                h.red(m16[:], v3(yt), ALU.add, ["dy%d" % j], ["m16"])
                h.ts("dve", m16[:], m16[:], 1.0 / 64, None, ALU.mult, None, ["m16"], ["m16"])
                h.tt("dve", v3(yc), v3(yt), bcl(m16[:], 64), ALU.subtract, ["dy%d" % j, "m16"], ["yc"])
                h.tt("pool", sq2[:], yc[:], yc[:], ALU.mult, ["yc"], ["sq2"])
                h.red(v16[:], v3(sq2), ALU.add, ["sq2"], ["v16"])
                h.ts("dve", v16[:], v16[:], 1.0 / 64, GN_EPS, ALU.mult, ALU.add, ["v16"], ["v16"])
                h.act(v16[:], v16[:], AF.Sqrt, ["v16"], ["v16"])
                h.recip(v16[:], v16[:], ["v16"], ["v16"])
                h.tt("dve", v3(yc), v3(yc), bcl(v16[:], 64), ALU.mult, ["yc", "v16"], ["yc"])
                h.tt("pool", yc[:], yc[:], LNG[:], ALU.mult, ["yc", "cD"], ["yc"])
                h.tt("pool", yc[:], yc[:], LNB[:], ALU.add, ["yc", "cD"], ["yc"])
                h.tt("pool", prod[:], rt[:], kt[:], ALU.mult, ["dr%d" % j, "dk%d" % j], ["prod"])
                h.tt("pool", prod[:], prod[:], RKb[:], ALU.mult, ["prod", "cD"], ["prod"])
                h.red(b16[:], v3(prod), ALU.add, ["prod"], ["b16"])
                h.tt("dve", v3(prod), v3(vt), bcl(b16[:], 64), ALU.mult, ["dv%d" % j, "b16", "prod"], ["prod"])
                h.tt("dve", yc[:], yc[:], prod[:], ALU.add, ["yc", "prod"], ["yc"])
                h.tt("dve", yb[j][:], yc[:], gt_[:], ALU.mult, ["yc", "dg%d" % j], ["yb%d" % j])
                h.dma(yrs[rows, :], yb[j][:], ["yb%d" % j], ["yrs"])
                if "yr" in dbg_out:
                    h.tt("pool", yf[:], yc[:], gt_[:], ALU.mult, ["yc", "dg%d" % j], ["yf"])
                    h.dma(dbg_out["yr"][rows, :], yf[:], ["yf"], ["dbg"])
            end_phase()
        if stop_after in ("D", "Cshort"):
            return nc, dbg_out, S

        def load_w_bf16(es, dst, src, kcs, ncols, prows=128):
            stg = [sbt(es, "stg%d" % i, [128, 8, 512]) for i in range(2)]
            ci = 0
            for nb in range(ncols // 512):
                j = ci % 2; ci += 1
                h.dma(stg[j][0:prows, 0:kcs, :], src[:, nb * 512:(nb + 1) * 512].rearrange("(kc p) n -> p kc n", p=prows), [], ["stg%d" % j])
                h.copy("pool", dst[:, :, nb * 512:(nb + 1) * 512], stg[j][0:prows, 0:kcs, :], ["stg%d" % j], ["wconst"])

        with contextlib.ExitStack() as es:
            WBR = sbt(es, "WBR", [128, 8, D], BF16); WBA = sbt(es, "WBA", [64, 4, D], BF16); WO = sbt(es, "WO", [128, 8, D], BF16)
            with contextlib.ExitStack() as es1:
                load_w_bf16(es1, WBR, w_br_rwkv, 8, D)
                load_w_bf16(es1, WO, w_out, 8, D)
                load_w_bf16(es1, WBA, w_br_attn, 4, D, prows=64)
                S.barrier()
            GT1 = [sbt(es, "GT1_%d" % b, [128, D]) for b in range(NB)]
            G2n = [sbt(es, "G2n_%d" % b, [128, D]) for b in range(NB)]
            SH2 = [sbt(es, "SH2_%d" % b, [128, D]) for b in range(NB)]
            tmpg = sbt(es, "tmpg2", [128, D])
            bcast_load(tmpg[:], norm2_g[0:1, :], ["tmpg"])
            for b in range(NB):
                bcast_load(GT1[b][:], modd[b:b + 1, 2048:3072], ["cF"])
                bcast_load(SH2[b][:], modd[b:b + 1, 3072:4096], ["cF"])
                bcast_load(G2n[b][:], modd[b:b + 1, 4096:5120], ["G2n%d" % b])
                h.tt("dve", G2n[b][:], G2n[b][:], tmpg[:], ALU.mult, ["G2n%d" % b, "tmpg"], ["G2n%d" % b])
            ybl = [sbt(es, "ybl%d" % i, [128, D], BF16) for i in range(2)]
            yrT = sbt(es, "yrT", [128, 8, 128], BF16); mgT = sbt(es, "mgT", [128, 8, 128], BF16)
            zg = [sbt(es, "zg%d" % i, [128, 2048]) for i in range(2)]
            t1 = sbt(es, "t1", [128, D]); t2 = sbt(es, "t2", [128, D]); mgb = sbt(es, "mgb", [128, D], BF16)
            xt = [sbt(es, "xtF%d" % i, [128, D]) for i in range(2)]
            ht = [sbt(es, "htF%d" % i, [128, D]) for i in range(2)]
            junk = sbt(es, "junkF", [128, D]); n2b = [sbt(es, "n2bF%d" % i, [128, D], BF16) for i in range(2)]
            ss = sbt(es, "ssF", [128, NTILE]); rs = sbt(es, "rsF", [128, NTILE])
            ptr = pst(es, "ptrF", [128, 8, 128], BF16)
            pm1 = pst(es, "pm1", [128, D]); pm2 = pst(es, "pm2", [128, D]); po = pst(es, "poF", [128, D])
            for i in range(NTILE):
                b = i // 16; j = i % 2; rows = slice(i * 128, (i + 1) * 128)
                h.dma(ybl[j][:], yrs[rows, :], [], ["ybl%d" % j])
                h.dma(zg[j][:], zs[rows, 5632:7680], [], ["zg%d" % j])
                h.dma(xt[j][:], x[rows, :], [], ["xtF%d" % j])
                for kc in range(8):
                    h.tr(ptr[:, kc, :], ybl[j][:, kc * 128:(kc + 1) * 128], idb[:], ["ybl%d" % j, "idb"], ["ptrF"])
                h.copy("act", yrT[:], ptr[:], ["ptrF"], ["yrT"])
                for nb in range(2):
                    cs_ = slice(nb * 512, (nb + 1) * 512)
                    for kc in range(8):
                        h.mm(pm1[:, cs_], yrT[:, kc, :], WBR[:, kc, cs_], kc == 0, kc == 7, ["yrT"], ["pm1"])
                    for hh in range(4):
                        h.mm(pm2[:, cs_], yaT[:, hh, rows], WBA[:, hh, cs_], hh == 0, hh == 3, [], ["pm2"])
                h.act(zg[j][:], zg[j][:], AF.Sigmoid, ["zg%d" % j], ["zg%d" % j])
                h.tt("dve", t1[:], pm1[:], zg[j][:, 0:1024], ALU.mult, ["pm1", "zg%d" % j], ["t1"])
                h.tt("dve", t2[:], pm2[:], zg[j][:, 1024:2048], ALU.mult, ["pm2", "zg%d" % j], ["t2"])
                h.tt("pool", mgb[:], t1[:], t2[:], ALU.add, ["t1", "t2"], ["mgb"])
                for kc in range(8):
                    h.tr(ptr[:, kc, :], mgb[:, kc * 128:(kc + 1) * 128], idb[:], ["mgb", "idb"], ["ptrF"])
                h.copy("act", mgT[:], ptr[:], ["ptrF"], ["mgT"])
                for nb in range(2):
                    cs_ = slice(nb * 512, (nb + 1) * 512)
                    for kc in range(8):
                        h.mm(po[:, cs_], mgT[:, kc, :], WO[:, kc, cs_], kc == 0, kc == 7, ["mgT"], ["poF"])
                h.tt("dve", t1[:], po[:], GT1[b][:], ALU.mult, ["poF", "cF"], ["t1"])
                h.tt("pool", ht[j][:], t1[:], xt[j][:], ALU.add, ["t1", "xtF%d" % j], ["htF%d" % j])
                h.dma(hs[rows, :], ht[j][:], ["htF%d" % j], ["hs"])
                if "h" in dbg_out:
                    h.dma(dbg_out["h"][rows, :], ht[j][:], ["htF%d" % j], ["dbg"])
                h.act(junk[:], ht[j][:], AF.Square, ["htF%d" % j], ["junkF", "ssF%d" % i], accum=ss[:, i:i + 1])
                h.ts("dve", rs[:, i:i + 1], ss[:, i:i + 1], 1.0 / D, EPS, ALU.mult, ALU.add, ["ssF%d" % i], ["rsF%d" % i])
                h.act(rs[:, i:i + 1], rs[:, i:i + 1], AF.Sqrt, ["rsF%d" % i], ["rsF%d" % i])
                h.recip(rs[:, i:i + 1], rs[:, i:i + 1], ["rsF%d" % i], ["rsF%d" % i])
                h.stt(t2[:], ht[j][:], rs[:, i:i + 1], G2n[b][:], ALU.mult, ALU.mult, ["htF%d" % j, "rsF%d" % i, "G2n%d" % b], ["t2"])
                h.tt("pool", n2b[j][:], t2[:], SH2[b][:], ALU.add, ["t2", "cF"], ["n2bF%d" % j])
                h.dma(n2s[rows, :], n2b[j][:], ["n2bF%d" % j], ["n2s"])
            end_phase()
        if stop_after == "F":
            return nc, dbg_out, S

        with contextlib.ExitStack() as es:
            WQ = sbt(es, "WQ", [128, 8, 2048], BF16)
            with contextlib.ExitStack() as es1:
                load_w_bf16(es1, WQ, peer_wq, 8, 2048)
                S.barrier()
            with contextlib.ExitStack() as es1:
                cf = [sbt(es1, "cf%d" % i, [128, 4, D]) for i in range(3)]
                cb = [sbt(es1, "cb%d" % i, [128, 4, D], BF16) for i in range(3)]
                ci = 0
                for src, off_ in ((peer_u, 0), (peer_v, D)):
                    s3 = src.rearrange("(p r) d -> p r d", p=128); d3 = uvb.rearrange("(p r) d -> p r d", p=128)[:, :, off_:off_ + D]
                    for ch in range(32):
                        j = ci % 3; ci += 1
                        h.dma(cf[j][:], s3[:, ch * 4:(ch + 1) * 4, :], [], ["cf%d" % j])
                        h.copy(("act", "pool", "dve")[j], cb[j][:], cf[j][:], ["cf%d" % j], ["cb%d" % j])
                        h.dma(d3[:, ch * 4:(ch + 1) * 4, :], cb[j][:], ["cb%d" % j], ["ubvb"])
                S.barrier()
            K1T = sbt(es, "K1T", [128, 128]); K2T = sbt(es, "K2T", [128, 128]); ktmp = sbt(es, "ktmp", [128, 128])
            GT2 = [sbt(es, "GT2_%d" % b, [128, D]) for b in range(NB)]
            n2b = sbt(es, "n2bG", [128, D], BF16); n2Tt = sbt(es, "n2Tt", [128, 8, 128], BF16)
            qTt = sbt(es, "qTt", [128, 16, 128])
            S1 = sbt(es, "S1", [128, 8, 128]); S2 = sbt(es, "S2", [128, 8, 128]); wk = sbt(es, "wk", [128, 128])
            v1 = sbt(es, "v1", [128, 8, 16]); v2 = sbt(es, "v2", [128, 8, 16])
            i1u = sbt(es, "i1u", [128, 8, 16], U32); i2u = sbt(es, "i2u", [128, 8, 16], U32)
            i1f = sbt(es, "i1f", [128, 8, 16]); i2f = sbt(es, "i2f", [128, 8, 16])
            cand = sbt(es, "cand", [128, 16, 16]); cidx = sbt(es, "cidx", [128, 16, 16]); wk2 = sbt(es, "wk2", [128, 256]); junk2 = sbt(es, "junk2", [128, 256])
            t16 = sbt(es, "t16", [128, 8, 16]); idxf = sbt(es, "idxf", [128, 128]); e16 = sbt(es, "e16", [128, 8, 16]); gate = sbt(es, "gate", [128, 128])
            nmx = sbt(es, "nmx", [128, 8]); Zs = sbt(es, "Zs", [128, 8])
            posu = sbt(es, "posu", [128, 8, 16], U32); au = sbt(es, "au", [128, 8, 16], U32); bu = sbt(es, "bu", [128, 8, 16], U32)
            af = sbt(es, "af", [128, 8, 16]); bf_ = sbt(es, "bf_", [128, 8, 16]); ia = sbt(es, "ia", [128, 8, 16]); ib = sbt(es, "ib", [128, 8, 16])
            eqa = sbt(es, "eqa", [128, 16, 16]); eqb = sbt(es, "eqb", [128, 16, 16]); iota16 = sbt(es, "iota16", [128, 16])
            h.dma(iota16[:], iota16_d, [], ["cG"])
            IDXT = sbt(es, "IDXT", [128, 128], I32); GTt = sbt(es, "GTt", [128, 128]); dots = sbt(es, "dots", [128, 128]); coef = sbt(es, "coef", [128, 128])
            NUB = 10
            UV = [sbt(es, "UV%d" % i, [128, 2 * D], BF16) for i in range(NUB)]
            glu = sbt(es, "glu", [128, 128])
            junkU = sbt(es, "junkU", [128, D], BF16)
            IX1 = [sbt(es, "IX1_%d" % i, [128, 1], I32) for i in range(NUB)]
            coefb = sbt(es, "coefb", [128, 128], BF16)
            WB = [sbt(es, "WB%d" % i, [128, 256], BF16) for i in range(4)]
            htG = sbt(es, "htG", [128, D]); t1 = sbt(es, "t1G", [128, D]); yo = sbt(es, "yo", [128, D])
            ptr = pst(es, "ptrG", [128, 8, 128], BF16)
            pB = pst(es, "pB", [128, 512])
            pbc = [pst(es, "pbc%d" % i, [128, D]) for i in range(2)]
            po = pst(es, "poG", [128, D])
            for b in range(NB):
                bcast_load(GT2[b][:], modd[b:b + 1, 5120:6144], ["cG"])
            for kt_, src in ((K1T, peer_k1), (K2T, peer_k2)):
                h.dma(ktmp[:], src, [], ["ktmp"])
                h.tr(pB[:, 0:128], ktmp[:], idf[:], ["ktmp", "idf"], ["pB"])
                h.copy("dve", kt_[:], pB[:, 0:128], ["pB"], ["cG"])
            for w_ in WB:
                h.memset("pool", w_[:], 0.0, ["cG"])
            ntile_g = NTILE if stop_after != "Gshort" else 1
            for i in range(ntile_g):
                b = i // 16; rows = slice(i * 128, (i + 1) * 128)
                h.dma(n2b[:], n2s[rows, :], [], ["n2bG"])
                h.dma(htG[:], hs[rows, :], [], ["htG"])
                for kc in range(8):
                    h.tr(ptr[:, kc, :], n2b[:, kc * 128:(kc + 1) * 128], idb[:], ["n2bG", "idb"], ["ptrG"])
                h.copy("act", n2Tt[:], ptr[:], ["ptrG"], ["n2Tt"])
                for c4 in range(4):
                    for cq in range(4):
                        cc = c4 * 4 + cq
                        for kc in range(8):
                            h.mm(pB[:, cq * 128:(cq + 1) * 128], WQ[:, kc, cc * 128:(cc + 1) * 128], n2Tt[:, kc, :], kc == 0, kc == 7, ["n2Tt"], ["pB"])
                    h.copy("act" if c4 % 2 == 0 else "dve", qTt[:, c4 * 4:(c4 + 1) * 4, :], pB[:].rearrange("p (a b) -> p a b", a=4), ["pB"], ["qTt"])
                for which, Sx, KT in ((0, S1, K1T), (1, S2, K2T)):
                    for h4 in range(2):
                        for hq in range(4):
                            hd = h4 * 4 + hq
                            h.mm(pB[:, hq * 128:(hq + 1) * 128], qTt[:, 2 * hd + which, :], KT[:], True, True, ["qTt", "cG"], ["pB"])
                        h.copy("act" if h4 == 0 else "dve", Sx[:, h4 * 4:(h4 + 1) * 4, :], pB[:].rearrange("p (a b) -> p a b", a=4), ["pB"], ["S%d" % which])
                import os
                GCUT = float(os.environ.get("GCUT", "9"))
                if GCUT <= 1:
                    continue
                for which, Sx, vx, ix in ((0, S1, v1, i1u), (1, S2, v2, i2u)):
                    sk = "S%d" % which; vk_ = "v%d" % which
                    for hd in range(8):
                        S.op("dve", (lambda o, i_: (lambda e: e.max(out=o, in_=i_)))(vx[:, hd, 0:8], Sx[:, hd, :]), reads=[sk], writes=[vk_])
                        S.op("dve", (lambda o, r_, i_: (lambda e: e.match_replace(out=o, in_to_replace=r_, in_values=i_, imm_value=-1e30)))(wk[:], vx[:, hd, 0:8], Sx[:, hd, :]),
                             reads=[sk, vk_], writes=["wk"])
                        S.op("dve", (lambda o, i_: (lambda e: e.max(out=o, in_=i_)))(vx[:, hd, 8:16], wk[:]), reads=["wk"], writes=[vk_])
                        S.op("dve", (lambda o, m_, i_: (lambda e: e.max_index(out=o, in_max=m_, in_values=i_)))(ix[:, hd, 0:8], vx[:, hd, 0:8], Sx[:, hd, :]),
                             reads=[sk, vk_], writes=["ix%d" % which])
                        S.op("dve", (lambda o, m_, i_: (lambda e: e.max_index(out=o, in_max=m_, in_values=i_)))(ix[:, hd, 8:16], vx[:, hd, 8:16], Sx[:, hd, :]),
                             reads=[sk, vk_], writes=["ix%d" % which])
                if GCUT <= 2:
                    continue
                h.copy("dve", i1f[:], i1u[:], ["ix0"], ["i1f"])
                h.copy("dve", i2f[:], i2u[:], ["ix1"], ["i2f"])
                h.ts("dve", i1f[:], i1f[:], 128.0, None, ALU.mult, None, ["i1f"], ["i1f"])
                cflat = cand[:].rearrange("p a b -> p (a b)"); xflat = cidx[:].rearrange("p a b -> p (a b)")
                for hd in range(8):
                    h.tt("pool", cand[:], bcl(v1[:, hd, :], 16), bc3(v2[:, hd, :], 16), ALU.add, ["v0", "v1"], ["cand"])
                    S.op("dve", (lambda o, i_: (lambda e: e.max(out=o, in_=i_)))(t16[:, hd, 0:8], cflat), reads=["cand"], writes=["t16"])
                    S.op("dve", (lambda o, r_, i_: (lambda e: e.match_replace(out=o, in_to_replace=r_, in_values=i_, imm_value=-1e30)))(wk2[:], t16[:, hd, 0:8], cflat),
                         reads=["cand", "t16"], writes=["wk2"])
                    S.op("dve", (lambda o, i_: (lambda e: e.max(out=o, in_=i_)))(t16[:, hd, 8:16], wk2[:]), reads=["wk2"], writes=["t16"])
                    S.op("dve", (lambda o, m_, i_: (lambda e: e.max_index(out=o, in_max=m_, in_values=i_)))(posu[:, hd, 0:8], t16[:, hd, 0:8], cflat),
                         reads=["cand", "t16"], writes=["posu"])
                    S.op("dve", (lambda o, m_, i_: (lambda e: e.max_index(out=o, in_max=m_, in_values=i_)))(posu[:, hd, 8:16], t16[:, hd, 8:16], cflat),
                         reads=["cand", "t16"], writes=["posu"])
                S.op("dve", (lambda o, i_: (lambda e: e.tensor_scalar(out=o, in0=i_, scalar1=4, scalar2=None, op0=ALU.logical_shift_right)))(au[:], posu[:]), reads=["posu"], writes=["au"])
                S.op("dve", (lambda o, i_: (lambda e: e.tensor_scalar(out=o, in0=i_, scalar1=15, scalar2=None, op0=ALU.bitwise_and)))(bu[:], posu[:]), reads=["posu"], writes=["bu"])
                h.copy("dve", af[:], au[:], ["au"], ["af"])
                h.copy("dve", bf_[:], bu[:], ["bu"], ["bf"])
                for hd in range(8):
                    h.tt("dve", eqa[:], bcl(af[:, hd, :], 16), bc3(iota16[:], 16), ALU.is_equal, ["af", "cG"], ["eqa"])
                    h.tt("pool", eqa[:], eqa[:], bc3(i1f[:, hd, :], 16), ALU.mult, ["eqa", "i1f"], ["eqa"])
                    h.red(ia[:, hd, :], eqa[:], ALU.add, ["eqa"], ["ia"])
                    h.tt("dve", eqb[:], bcl(bf_[:, hd, :], 16), bc3(iota16[:], 16), ALU.is_equal, ["bf", "cG"], ["eqb"])
                    h.tt("pool", eqb[:], eqb[:], bc3(i2f[:, hd, :], 16), ALU.mult, ["eqb", "i2f"], ["eqb"])
                    h.red(ib[:, hd, :], eqb[:], ALU.add, ["eqb"], ["ib"])
                h.tt("dve", idxf[:].rearrange("p (a b) -> p a b", a=8), ia[:], ib[:], ALU.add, ["ia", "ib"], ["idxf"])
                h.ts("dve", idxf[:], idxf[:], 16383.0, 0.0, ALU.min, ALU.max, ["idxf"], ["idxf"])
                if GCUT <= 2.4:
                    continue
                h.ts("dve", nmx[:], t16[:, :, 0], -1.0, None, ALU.mult, None, ["t16"], ["nmx"])
                for hd in range(8):
                    h.act(e16[:, hd, :], t16[:, hd, :], AF.Exp, ["t16", "nmx"], ["e16", "Zs"], bias=nmx[:, hd:hd + 1], accum=Zs[:, hd:hd + 1])
                h.recip(Zs[:], Zs[:], ["Zs"], ["Zs"])
                h.tt("dve", gate[:].rearrange("p (a b) -> p a b", a=8), e16[:], bcl(Zs[:], 16), ALU.mult, ["e16", "Zs"], ["gate"])
                if GCUT <= 2.6:
                    continue
                h.tr(pB[:, 0:128], idxf[:], idf[:], ["idxf", "idf"], ["pB"])
                h.tr(pB[:, 128:256], gate[:], idf[:], ["gate", "idf"], ["pB"])
                if GCUT <= 2.7:
                    continue
                h.ts("dve", dots[:], pB[:, 0:128], 8388608.0, None, ALU.add, None, ["pB"], ["dots"])
                S.op("dve", (lambda o, i_: (lambda e: e.tensor_scalar(out=o, in0=i_, scalar1=0x7FFFFF, scalar2=None, op0=ALU.bitwise_and)))(IDXT[:], dots[:].bitcast(I32)), reads=["dots"], writes=["IDXT"])
                if GCUT <= 2.8:
                    continue
                h.copy("dve", GTt[:], pB[:, 128:256], ["pB"], ["GTt"])
                import os
                GCUT = float(os.environ.get("GCUT", "9"))
                if "idxf" in dbg_out and i == 0:
                    h.dma(dbg_out["idxf"], idxf[:], ["idxf"], ["dbg"])
                    h.dma(dbg_out["gate"], gate[:], ["gate"], ["dbg"])
                    h.dma(dbg_out["GTt"], GTt[:], ["GTt"], ["dbg"])
                    h.copy("dve", dots[:], IDXT[:], ["IDXT"], ["dots"])
                    h.dma(dbg_out["IDXTf"], dots[:], ["dots"], ["dbg"])
                if GCUT <= 3:
                    continue
                LA = NUB - 2

                def issue_gather(c2):
                    j2 = c2 % NUB
                    h.copy("dve", IX1[j2][:], IDXT[:, c2:c2 + 1], ["IDXT"], ["IX%d" % j2])
                    S.dma("pool", (lambda o, ix_: (lambda e: e.indirect_dma_start(out=o, out_offset=None, in_=uvb,
                                                                                  in_offset=bass.IndirectOffsetOnAxis(ap=ix_, axis=0))))(UV[j2][:], IX1[j2][:, 0:1]),
                          reads=["IX%d" % j2], writes=["UV%d" % j2])
                def emit_out(c3):
                    j3 = c3 % NUB; jw3 = c3 % 4
                    h.ts("dve", WB[jw3][:, 127:128], glu[:, c3:c3 + 1], GTt[:, c3:c3 + 1], None, ALU.mult, None, ["glu%d" % (c3 % 8), "GTt"], ["WB%d" % jw3])
                    for hb3 in range(2):
                        h.mm(po[:, hb3 * 512:(hb3 + 1) * 512], WB[jw3][:, 127 - c3:255 - c3], UV[j3][:, D + hb3 * 512:D + (hb3 + 1) * 512], c3 == 0, c3 == 127,
                             ["WB%d" % jw3, "UV%d" % j3], ["poG"])
                def emit_bcast(c4):
                    jb4 = c4 % 2
                    for hb4 in range(2):
                        cs4 = slice(hb4 * 512, (hb4 + 1) * 512)
                        h.mm(pbc[jb4][:, cs4], idb[:, c4:c4 + 1].to_broadcast([128, 128]), n2b[:, cs4], True, True, ["n2bG", "idb"], ["pbc%d" % jb4])
                for c2 in range(LA):
                    issue_gather(c2)
                emit_bcast(0)
                for c in range(128):
                    ju = c % NUB; jb = c % 2; jw = c % 4
                    if c + LA < 128:
                        issue_gather(c + LA)
                    if c + 1 < 128:
                        emit_bcast(c + 1)
                    h.stt(junkU[:], UV[ju][:, 0:D], 1.0, pbc[jb][:], ALU.mult, ALU.mult, ["UV%d" % ju, "pbc%d" % jb], ["junkU", "dots%d" % (c % 8)], accum=dots[:, c:c + 1])
                    h.act(glu[:, c:c + 1], dots[:, c:c + 1], AF.Gelu, ["dots%d" % (c % 8)], ["glu%d" % (c % 8)])
                    if c >= 1:
                        emit_out(c - 1)
                emit_out(127)
                h.tt("dve", t1[:], po[:], GT2[b][:], ALU.mult, ["poG", "cG"], ["t1G"])
                h.tt("pool", yo[:], t1[:], htG[:], ALU.add, ["t1G", "htG"], ["yo"])
                last_tok[0] = h.dma(y_out[rows, :], yo[:], ["yo"], ["y"])
            end_phase()
        return nc, dbg_out, S


def host_consts():
    ident = np.eye(128, dtype=np.float32)
    half = 8
    inv = (500000.0 ** (-np.arange(half, dtype=np.float32) / half)).astype(np.float32)
    pos = np.arange(SEQ, dtype=np.float32)
    ang = (pos[:, None] * inv[None, :]).astype(np.float32)
    cs = np.concatenate([np.cos(ang), np.sin(ang)], axis=1).astype(np.float32)
    cs = cs.reshape(16, 128, 16).transpose(1, 0, 2).copy()
    k = np.arange(128)[:, None]; q = np.arange(128)[None, :]
    mcur = (k <= q).astype(np.float32); mprev = (k >= q).astype(np.float32)
    same = (k // 64) == (q // 64)
    ltri = (same & (k <= q)).astype(np.float32)
    lblk = same.astype(np.float32)
    m256 = np.concatenate([(same & (k < q)), (same & (k <= q))], axis=1).astype(np.float32)
    mL = (same & (k > q)).astype(np.float32)
    iota16 = np.tile(np.arange(16, dtype=np.float32)[None, :], (128, 1))
    return dict(iota16=iota16, ident=ident, cs=cs, mcur=mcur, mprev=mprev, ltri=ltri, lblk=lblk, m256=m256, mL=mL)


def make_in_maps(inputs, n_cores=8):
    consts = host_consts()
    shared = {}
    for k in ("w_ada", "w_in", "w2", "a2", "g2", "w_br_rwkv", "w_br_attn", "w_out", "peer_wq", "peer_k1", "peer_k2",
              "peer_u", "peer_v", "q_norm_g", "k_norm_g"):
        shared[k] = np.ascontiguousarray(inputs[k][0], dtype=np.float32)
    for k in ("b_ada", "norm1_g", "rwkv_mu", "w0", "a0", "k_k", "k_a", "lnx_g", "lnx_b", "norm2_g"):
        shared[k] = np.ascontiguousarray(inputs[k][0].reshape(1, -1), dtype=np.float32)
    shared["r_k"] = np.ascontiguousarray(inputs["r_k"][0].reshape(1, -1), dtype=np.float32)
    shared.update(consts)
    maps = []
    for c in range(n_cores):
        m = dict(shared)
        m["x"] = np.ascontiguousarray(inputs["x"][c * NB:(c + 1) * NB].reshape(NT, D), dtype=np.float32)
        cc = np.asarray(inputs["c"][c * NB:(c + 1) * NB], dtype=np.float32)
        m["cT"] = np.ascontiguousarray(cc.reshape(NB, 8, 128).transpose(2, 1, 0))
        maps.append(m)
    return maps


def kernel(**inputs):
    nc, _, _ = build_core()
    maps = make_in_maps(inputs, 8)
    res = run_bass_kernel_spmd(nc, maps, core_ids=list(range(8)))
    outs = [np.asarray(r["y"]).reshape(NB, SEQ, D) for r in res.results]
    return np.concatenate(outs, axis=0).astype(np.float32)
```
